# Optimizing a Trainium2 kernel written in Bass

```python
import math
import jax, jax.numpy as jnp
from jax import lax
import numpy as np

D_MODEL = 1024
BATCH = 8
SEQ = 4096
DEPTH = 4

N_MIXERS = 4
FFN_DIM = ((8 * D_MODEL // 3 + 127) // 128) * 128
DN_ALPHA = (2.0 * DEPTH) ** 0.25
DN_BETA = (8.0 * DEPTH) ** -0.25
ROPE_THETA = 500000.0
LN_EPS = 1e-5
RMS_EPS = 1e-6

CONF_KERNEL = 31
SCONV_KERNEL = 3

ATT_HEADS = D_MODEL // 64
ATT_HEAD_DIM = 64
ROPE_DIM = ATT_HEAD_DIM // 4
ATT_NOPE_DIM = ATT_HEAD_DIM - ROPE_DIM
Q_RANK = D_MODEL // 4
KV_RANK = D_MODEL // 8
IDX_HEADS = 8
IDX_DIM = ATT_HEAD_DIM
TOPK_MAX = 256
Q_BLOCK = 128

GDN_K_HEADS = D_MODEL // 128
GDN_V_HEADS = 2 * GDN_K_HEADS
GDN_K_DIM = 128
GDN_V_DIM = 128
GDN_CONV = 4
GDN_CHUNK = 64
GDN_QKV = 2 * GDN_K_HEADS * GDN_K_DIM + GDN_V_HEADS * GDN_V_DIM
GDN_IN = GDN_QKV + GDN_V_HEADS * GDN_V_DIM + 2 * GDN_V_HEADS

kernel_name = 'hybrid_conv_shortconv_dsa_gdn_trunk'


def _mixer_count(m):
    return (DEPTH + N_MIXERS - 1 - m) // N_MIXERS


def layer_norm(x, g, b):
    xf = x.astype(jnp.float32)
    mu = jnp.mean(xf, -1, keepdims=True)
    var = jnp.mean(jnp.square(xf - mu), -1, keepdims=True)
    return ((xf - mu) * lax.rsqrt(var + LN_EPS)).astype(x.dtype) * g + b


def rms_norm(x, g):
    xf = x.astype(jnp.float32)
    return (xf * lax.rsqrt(jnp.mean(xf * xf, -1, keepdims=True) + RMS_EPS)).astype(x.dtype) * g


def causal_dwconv(x, w):
    width, ch = w.shape
    return lax.conv_general_dilated(
        x, w[:, None, :].astype(x.dtype), window_strides=(1,), padding=[(width - 1, 0)],
        dimension_numbers=('NWC', 'WIO', 'NWC'), feature_group_count=ch)


def rope_tables(positions):
    inv = ROPE_THETA ** (-jnp.arange(0, ROPE_DIM, 2, dtype=jnp.float32) / ROPE_DIM)
    ang = positions.astype(jnp.float32)[..., None] * inv
    return jnp.cos(ang), jnp.sin(ang)


def apply_partial_rope(x, cos, sin):
    half = cos.shape[-1]
    x1, x2, rest = x[..., :half], x[..., half:2 * half], x[..., 2 * half:]
    c, s = cos.astype(x.dtype), sin.astype(x.dtype)
    return jnp.concatenate([x1 * c - x2 * s, x2 * c + x1 * s, rest], axis=-1)


def swiglu(x, w_gate, w_up, w_down):
    return (jax.nn.silu(x @ w_gate) * (x @ w_up)) @ w_down


def conformer_conv(h, w_in, w_dw, ln_g, ln_b, w_out):
    u = h @ w_in
    u = u[..., :D_MODEL] * jax.nn.sigmoid(u[..., D_MODEL:])
    u = causal_dwconv(u, w_dw)
    u = jax.nn.silu(layer_norm(u, ln_g, ln_b))
    return u @ w_out


def short_gated_conv(h, w_in, w_conv, w_out):
    bch = h @ w_in
    gate_b = bch[..., :D_MODEL]
    gate_c = bch[..., D_MODEL:2 * D_MODEL]
    val = bch[..., 2 * D_MODEL:]
    y = gate_b * causal_dwconv(gate_c * val, w_conv)
    return y @ w_out


def dsa_attention(h, cos, sin, w_dq, q_norm, w_uq, w_dkv, kv_norm, w_kr, w_uk, w_uv, w_o,
                  w_iq, w_ik, ik_g, ik_b, w_iw):
    bn, s_len, _ = h.shape
    n_blk = s_len // Q_BLOCK
    topk = min(TOPK_MAX, s_len // 4)
    cos_h, sin_h = cos[:, :, None, :], sin[:, :, None, :]
    c_q = rms_norm(h @ w_dq, q_norm)
    q = (c_q @ w_uq).reshape(bn, s_len, ATT_HEADS, ATT_HEAD_DIM)
    q_rope = apply_partial_rope(q[..., :ROPE_DIM], cos_h, sin_h)
    q_lat = jnp.einsum('bshn,hnc->bshc', q[..., ROPE_DIM:], w_uk)
    c_kv = rms_norm(h @ w_dkv, kv_norm)
    k_rope = apply_partial_rope(h @ w_kr, cos, sin)
    qi = apply_partial_rope((c_q @ w_iq).reshape(bn, s_len, IDX_HEADS, IDX_DIM), cos_h, sin_h)
    ki = apply_partial_rope(layer_norm(h @ w_ik, ik_g, ik_b), cos, sin)
    qi, ki = qi.astype(jnp.float32), ki.astype(jnp.float32)
    wi = (h @ w_iw).astype(jnp.float32) * (IDX_HEADS * IDX_DIM) ** -0.5

    def blocks(a):
        return jnp.swapaxes(a.reshape(bn, n_blk, Q_BLOCK, *a.shape[2:]), 0, 1)

    t_idx = jnp.arange(s_len, dtype=jnp.int32).reshape(n_blk, Q_BLOCK)
    key_idx = jnp.arange(s_len, dtype=jnp.int32)
    b_idx = jnp.arange(bn)[:, None, None]
    scale = ATT_HEAD_DIM ** -0.5

    def attend_block(args):
        ql, qr, qib, wib, t = args
        rel = jax.nn.relu(jnp.einsum('bqhd,bsd->bqhs', qib, ki))
        score = jnp.einsum('bqhs,bqh->bqs', rel, wib)
        score = jnp.where(key_idx[None, None, :] <= t[None, :, None], score, -jnp.inf)
        _, sel = lax.top_k(score, topk)
        valid = sel <= t[None, :, None]
        kv_sel = c_kv[b_idx, sel]
        kr_sel = k_rope[b_idx, sel]
        logits = (jnp.einsum('bqhc,bqkc->bqhk', ql, kv_sel)
                  + jnp.einsum('bqhr,bqkr->bqhk', qr, kr_sel))
        logits = jnp.where(valid[:, :, None, :], logits.astype(jnp.float32) * scale, -jnp.inf)
        p = jax.nn.softmax(logits, axis=-1).astype(kv_sel.dtype)
        return jnp.einsum('bqhk,bqkc->bqhc', p, kv_sel)

    o_lat = lax.map(attend_block, (blocks(q_lat), blocks(q_rope), blocks(qi), blocks(wi), t_idx))
    o_lat = jnp.swapaxes(o_lat, 0, 1).reshape(bn, s_len, ATT_HEADS, KV_RANK)
    o = jnp.einsum('bshc,hcd->bshd', o_lat, w_uv).reshape(bn, s_len, ATT_HEADS * ATT_HEAD_DIM)
    return o @ w_o


def gated_delta_rule(q, k, v, g, beta):
    bn, s_len, nh, dk = q.shape
    dv = v.shape[-1]
    n_ch, c = s_len // GDN_CHUNK, GDN_CHUNK
    q = q * lax.rsqrt(jnp.sum(q * q, -1, keepdims=True) + RMS_EPS) * dk ** -0.5
    k = k * lax.rsqrt(jnp.sum(k * k, -1, keepdims=True) + RMS_EPS)

    def chunks(a):
        a = a.reshape(bn, n_ch, c, nh, *a.shape[3:])
        return jnp.moveaxis(a, (1, 3), (0, 2))

    q, k, v, g, beta = chunks(q), chunks(k), chunks(v), chunks(g), chunks(beta)
    g = jnp.cumsum(g, axis=-1)
    ii = jnp.arange(c)
    tril = ii[:, None] >= ii[None, :]
    strict = ii[:, None] > ii[None, :]
    decay = jnp.exp(jnp.where(tril, g[..., :, None] - g[..., None, :], -jnp.inf))
    kb = k * beta[..., None]
    a_mat = jnp.where(strict, jnp.einsum('nbhid,nbhjd->nbhij', kb, k) * decay, 0.0)
    eye = jnp.eye(c, dtype=jnp.float32)
    t_mat = lax.linalg.triangular_solve(a_mat + eye, jnp.broadcast_to(eye, a_mat.shape),
                                        left_side=True, lower=True, unit_diagonal=True)
    u = jnp.einsum('nbhij,nbhjd->nbhid', t_mat, v * beta[..., None])
    w = jnp.einsum('nbhij,nbhjd->nbhid', t_mat, kb * jnp.exp(g)[..., None])
    intra = jnp.where(tril, jnp.einsum('nbhid,nbhjd->nbhij', q, k) * decay, 0.0)
    g_last = g[..., -1]
    k_tail = k * jnp.exp(g_last[..., None] - g)[..., None]
    q_head = q * jnp.exp(g)[..., None]

    def step(state, xs):
        q_n, kt_n, u_n, w_n, intra_n, gl_n = xs
        v_new = u_n - jnp.einsum('bhck,bhkv->bhcv', w_n, state)
        out = (jnp.einsum('bhck,bhkv->bhcv', q_n, state)
               + jnp.einsum('bhij,bhjv->bhiv', intra_n, v_new))
        state = state * jnp.exp(gl_n)[..., None, None] + jnp.einsum('bhck,bhcv->bhkv', kt_n, v_new)
        return state, out

    state0 = jnp.zeros((bn, nh, dk, dv), jnp.float32)
    _, out = lax.scan(step, state0, (q_head, k_tail, u, w, intra, g_last))
    return jnp.moveaxis(out, (0, 2), (1, 3)).reshape(bn, s_len, nh, dv)


def gated_deltanet(h, w_in, w_conv, a_log, dt_bias, norm_g, w_out):
    bn, s_len, _ = h.shape
    nk = GDN_K_HEADS * GDN_K_DIM
    nv = GDN_V_HEADS * GDN_V_DIM
    proj = h @ w_in
    qkv = jax.nn.silu(causal_dwconv(proj[..., :GDN_QKV], w_conv))
    rep = GDN_V_HEADS // GDN_K_HEADS
    q = jnp.repeat(qkv[..., :nk].reshape(bn, s_len, GDN_K_HEADS, GDN_K_DIM), rep, axis=2)
    k = jnp.repeat(qkv[..., nk:2 * nk].reshape(bn, s_len, GDN_K_HEADS, GDN_K_DIM), rep, axis=2)
    v = qkv[..., 2 * nk:].reshape(bn, s_len, GDN_V_HEADS, GDN_V_DIM)
    z = proj[..., GDN_QKV:GDN_QKV + nv].reshape(bn, s_len, GDN_V_HEADS, GDN_V_DIM)
    b_logit = proj[..., GDN_QKV + nv:GDN_QKV + nv + GDN_V_HEADS].astype(jnp.float32)
    a_in = proj[..., GDN_QKV + nv + GDN_V_HEADS:].astype(jnp.float32)
    beta = jax.nn.sigmoid(b_logit)
    g = -jnp.exp(a_log.astype(jnp.float32)) * jax.nn.softplus(a_in + dt_bias.astype(jnp.float32))
    o = gated_delta_rule(q.astype(jnp.float32), k.astype(jnp.float32), v.astype(jnp.float32),
                         g, beta).astype(h.dtype)
    o = rms_norm(o, norm_g) * jax.nn.silu(z)
    return o.reshape(bn, s_len, nv) @ w_out


def setup_inputs(seed: int = 0) -> dict:
    key = jax.random.key(seed)
    ks = iter(jax.random.split(key, 48))
    f32 = jnp.float32

    def w(shape, fan_in, scale=1.0):
        return jax.random.normal(next(ks), shape, f32) * (scale * fan_in ** -0.5)

    def gain(shape):
        return 1.0 + 0.02 * jax.random.normal(next(ks), shape, f32)

    def bias(shape):
        return 0.02 * jax.random.normal(next(ks), shape, f32)

    n_a, n_b, n_c, n_d = (_mixer_count(m) for m in range(N_MIXERS))
    d = D_MODEL
    x = jax.random.normal(next(ks), (BATCH, SEQ, d), f32)
    positions = (jax.random.randint(next(ks), (BATCH, 1), 0, 1024, jnp.int32)
                 + jnp.arange(SEQ, dtype=jnp.int32)[None, :])
    ln_g = gain((DEPTH, 3, d))
    ln_b = bias((DEPTH, 3, d))
    ffn_w_gate = w((DEPTH, 2, d, FFN_DIM), d)
    ffn_w_up = w((DEPTH, 2, d, FFN_DIM), d)
    ffn_w_down = w((DEPTH, 2, FFN_DIM, d), FFN_DIM, DN_BETA)

    conv_w_in = w((n_a, d, 2 * d), d)
    conv_w_dw = w((n_a, CONF_KERNEL, d), CONF_KERNEL)
    conv_ln_g = gain((n_a, d))
    conv_ln_b = bias((n_a, d))
    conv_w_out = w((n_a, d, d), d, DN_BETA)

    sc_w_in = w((n_b, d, 3 * d), d)
    sc_w_conv = w((n_b, SCONV_KERNEL, d), SCONV_KERNEL)
    sc_w_out = w((n_b, d, d), d, DN_BETA)

    dsa_w_dq = w((n_c, d, Q_RANK), d)
    dsa_q_norm = gain((n_c, Q_RANK))
    dsa_w_uq = w((n_c, Q_RANK, ATT_HEADS * ATT_HEAD_DIM), Q_RANK)
    dsa_w_dkv = w((n_c, d, KV_RANK), d)
    dsa_kv_norm = gain((n_c, KV_RANK))
    dsa_w_kr = w((n_c, d, ROPE_DIM), d)
    dsa_w_uk = w((n_c, ATT_HEADS, ATT_NOPE_DIM, KV_RANK), KV_RANK)
    dsa_w_uv = w((n_c, ATT_HEADS, KV_RANK, ATT_HEAD_DIM), KV_RANK)
    dsa_w_o = w((n_c, ATT_HEADS * ATT_HEAD_DIM, d), ATT_HEADS * ATT_HEAD_DIM, DN_BETA)
    dsa_w_iq = w((n_c, Q_RANK, IDX_HEADS * IDX_DIM), Q_RANK)
    dsa_w_ik = w((n_c, d, IDX_DIM), d)
    dsa_ik_ln_g = gain((n_c, IDX_DIM))
    dsa_ik_ln_b = bias((n_c, IDX_DIM))
    dsa_w_iw = w((n_c, d, IDX_HEADS), d)

    gdn_w_in = w((n_d, d, GDN_IN), d)
    gdn_w_conv = w((n_d, GDN_CONV, GDN_QKV), GDN_CONV)
    gdn_a_log = jnp.log(jax.random.uniform(next(ks), (n_d, GDN_V_HEADS), f32, 1.0, 16.0))
    dt = jnp.exp(jax.random.uniform(next(ks), (n_d, GDN_V_HEADS), f32,
                                    math.log(1e-3), math.log(1e-1)))
    gdn_dt_bias = dt + jnp.log(-jnp.expm1(-dt))
    gdn_norm_g = gain((n_d, GDN_V_DIM))
    gdn_w_out = w((n_d, GDN_V_HEADS * GDN_V_DIM, d), GDN_V_HEADS * GDN_V_DIM, DN_BETA)

    return {
        'x': x, 'positions': positions, 'ln_g': ln_g, 'ln_b': ln_b,
        'ffn_w_gate': ffn_w_gate, 'ffn_w_up': ffn_w_up, 'ffn_w_down': ffn_w_down,
        'conv_w_in': conv_w_in, 'conv_w_dw': conv_w_dw, 'conv_ln_g': conv_ln_g,
        'conv_ln_b': conv_ln_b, 'conv_w_out': conv_w_out,
        'sc_w_in': sc_w_in, 'sc_w_conv': sc_w_conv, 'sc_w_out': sc_w_out,
        'dsa_w_dq': dsa_w_dq, 'dsa_q_norm': dsa_q_norm, 'dsa_w_uq': dsa_w_uq,
        'dsa_w_dkv': dsa_w_dkv, 'dsa_kv_norm': dsa_kv_norm, 'dsa_w_kr': dsa_w_kr,
        'dsa_w_uk': dsa_w_uk, 'dsa_w_uv': dsa_w_uv, 'dsa_w_o': dsa_w_o,
        'dsa_w_iq': dsa_w_iq, 'dsa_w_ik': dsa_w_ik, 'dsa_ik_ln_g': dsa_ik_ln_g,
        'dsa_ik_ln_b': dsa_ik_ln_b, 'dsa_w_iw': dsa_w_iw,
        'gdn_w_in': gdn_w_in, 'gdn_w_conv': gdn_w_conv, 'gdn_a_log': gdn_a_log,
        'gdn_dt_bias': gdn_dt_bias, 'gdn_norm_g': gdn_norm_g, 'gdn_w_out': gdn_w_out,
    }


def reference(x, positions, ln_g, ln_b, ffn_w_gate, ffn_w_up, ffn_w_down,
              conv_w_in, conv_w_dw, conv_ln_g, conv_ln_b, conv_w_out,
              sc_w_in, sc_w_conv, sc_w_out,
              dsa_w_dq, dsa_q_norm, dsa_w_uq, dsa_w_dkv, dsa_kv_norm, dsa_w_kr,
              dsa_w_uk, dsa_w_uv, dsa_w_o, dsa_w_iq, dsa_w_ik, dsa_ik_ln_g, dsa_ik_ln_b, dsa_w_iw,
              gdn_w_in, gdn_w_conv, gdn_a_log, gdn_dt_bias, gdn_norm_g, gdn_w_out):
    cos, sin = rope_tables(positions)
    for i in range(DEPTH):
        m, j = i % N_MIXERS, i // N_MIXERS
        x = layer_norm(DN_ALPHA * x + 0.5 * swiglu(x, ffn_w_gate[i, 0], ffn_w_up[i, 0], ffn_w_down[i, 0]),
                       ln_g[i, 0], ln_b[i, 0])
        if m == 0:
            mix = conformer_conv(x, conv_w_in[j], conv_w_dw[j], conv_ln_g[j], conv_ln_b[j], conv_w_out[j])
        elif m == 1:
            mix = short_gated_conv(x, sc_w_in[j], sc_w_conv[j], sc_w_out[j])
        elif m == 2:
            mix = dsa_attention(x, cos, sin, dsa_w_dq[j], dsa_q_norm[j], dsa_w_uq[j], dsa_w_dkv[j],
                                dsa_kv_norm[j], dsa_w_kr[j], dsa_w_uk[j], dsa_w_uv[j], dsa_w_o[j],
                                dsa_w_iq[j], dsa_w_ik[j], dsa_ik_ln_g[j], dsa_ik_ln_b[j], dsa_w_iw[j])
        else:
            mix = gated_deltanet(x, gdn_w_in[j], gdn_w_conv[j], gdn_a_log[j], gdn_dt_bias[j],
                                 gdn_norm_g[j], gdn_w_out[j])
        x = layer_norm(DN_ALPHA * x + mix, ln_g[i, 1], ln_b[i, 1])
        x = layer_norm(DN_ALPHA * x + 0.5 * swiglu(x, ffn_w_gate[i, 1], ffn_w_up[i, 1], ffn_w_down[i, 1]),
                       ln_g[i, 2], ln_b[i, 2])
    return x
```

```python
import numpy as np
import concourse.bass as bass
import concourse.mybir as mybir
from concourse.bass_utils import run_bass_kernel_spmd

F32 = mybir.dt.float32
BF16 = mybir.dt.bfloat16
I32 = mybir.dt.int32
ALU = mybir.AluOpType
AF = mybir.ActivationFunctionType
AX = mybir.AxisListType

D_MODEL = 1024
SEQ = 4096
BATCH = 8
DEPTH = 4
FFN_DIM = 2816
NCH = D_MODEL // 128
NFT = FFN_DIM // 128
DN_ALPHA = (2.0 * DEPTH) ** 0.25
LN_EPS = 1e-5
RMS_EPS = 1e-6
TT = 512
NTT = SEQ // TT

EPOCH = 20000


class Buf:
    __slots__ = ("name", "w", "r")

    def __init__(self, name=""):
        self.name = name
        self.w = None
        self.r = []


class Op:
    __slots__ = ("eng", "idx", "fn", "dma", "deps", "signal", "sig_no", "dsem", "dval")

    def __init__(self, eng, idx, fn, dma):
        self.eng = eng
        self.idx = idx
        self.fn = fn
        self.dma = dma
        self.deps = []
        self.signal = False
        self.sig_no = -1
        self.dsem = -1
        self.dval = 0


class Sched:
    ENGS = ("pe", "act", "dve", "pool", "sp")
    NDMA = {"sp": 16, "pool": 8, "act": 4}

    def __init__(self, nc):
        self.nc = nc
        self.ops = {e: [] for e in self.ENGS}
        self.ndma = {e: 0 for e in self.ENGS}

    def add(self, eng, fn, reads=(), writes=(), dma=False):
        op = Op(eng, len(self.ops[eng]), fn, dma)
        deps = {}
        for b in reads:
            if b.w is not None:
                deps[id(b.w)] = b.w
        for b in writes:
            if b.w is not None:
                deps[id(b.w)] = b.w
            for r in b.r:
                deps[id(r)] = r
        keep = []
        for d in deps.values():
            if d is op:
                continue
            if d.eng == eng and not d.dma:
                if eng == "pe":
                    continue
                if op.idx - d.idx > 1 and not dma:
                    continue
            keep.append(d)
        op.deps = keep
        for d in keep:
            d.signal = True
        if dma:
            n = self.ndma[eng]
            self.ndma[eng] = n + 1
            op.dsem = n % self.NDMA[eng]
            op.dval = 16 * (n // self.NDMA[eng] + 1)
            op.signal = True
        for b in reads:
            b.r.append(op)
        for b in writes:
            b.w = op
            b.r = []
        self.ops[eng].append(op)
        return op

    def emit(self):
        nc = self.nc
        for e in self.ENGS:
            for op in self.ops[e]:
                if not op.dma:
                    op.signal = False
        for e in self.ENGS:
            seen_idx = {}
            for op in self.ops[e]:
                best = {}
                for d in op.deps:
                    if d.dma:
                        continue
                    if seen_idx.get(d.eng, -1) >= d.idx:
                        continue
                    if d.eng not in best or best[d.eng].idx < d.idx:
                        best[d.eng] = d
                for de, d in best.items():
                    d.signal = True
                    seen_idx[de] = d.idx
        nsig = {}
        for e in self.ENGS:
            n = 0
            for op in self.ops[e]:
                if op.signal and not op.dma:
                    op.sig_no = n
                    n += 1
            nsig[e] = n
        import contextlib
        with contextlib.ExitStack() as st:
            csem = {}
            for e in self.ENGS:
                k = (nsig[e] + EPOCH - 1) // EPOCH
                csem[e] = [st.enter_context(nc.semaphore(f"c_{e}_{i}")) for i in range(max(k, 1))]
            dsem = {}
            for e in self.ENGS:
                if self.ndma[e]:
                    dsem[e] = [st.enter_context(nc.semaphore(f"d_{e}_{i}"))
                               for i in range(min(self.NDMA[e], self.ndma[e]))]
            block = st.enter_context(nc.Block())
            ops = self.ops

            def run(eng_name, engine):
                seen = {}
                seen_idx = {}
                for op in ops[eng_name]:
                    waits = {}
                    best = {}
                    for d in op.deps:
                        if d.dma:
                            key = ("d", d.eng, d.dsem)
                            val = d.dval
                            if seen.get(key, 0) >= val:
                                continue
                            if waits.get(key, 0) < val:
                                waits[key] = val
                        else:
                            if seen_idx.get(d.eng, -1) >= d.idx:
                                continue
                            if d.eng not in best or best[d.eng].idx < d.idx:
                                best[d.eng] = d
                    for de, d in best.items():
                        assert d.signal and d.sig_no >= 0
                        seen_idx[de] = d.idx
                        waits[("c", d.eng, d.sig_no // EPOCH)] = d.sig_no % EPOCH + 1
                    if op.dma and op.dval > 16:
                        key = ("d", eng_name, op.dsem)
                        val = op.dval - 16
                        if seen.get(key, 0) < val and waits.get(key, 0) < val:
                            waits[key] = val
                    for key, val in waits.items():
                        sem = dsem[key[1]][key[2]] if key[0] == "d" else csem[key[1]][key[2]]
                        engine.wait_ge(sem, val)
                        seen[key] = val
                    inst = op.fn(engine)
                    if op.dma:
                        inst.then_inc(dsem[eng_name][op.dsem], 16)
                    elif op.signal:
                        inst.then_inc(csem[eng_name][op.sig_no // EPOCH], 1)
                if self.ndma[eng_name]:
                    last = {}
                    for op in ops[eng_name]:
                        if op.dma:
                            last[op.dsem] = op.dval
                    for s, v in last.items():
                        engine.wait_ge(dsem[eng_name][s], v)

            if ops["pe"]:
                @block.tensor
                def _(eng):
                    run("pe", eng)
            if ops["act"]:
                @block.scalar
                def _(eng):
                    run("act", eng)
            if ops["dve"]:
                @block.vector
                def _(eng):
                    run("dve", eng)
            if ops["pool"]:
                @block.gpsimd
                def _(eng):
                    run("pool", eng)
            if ops["sp"]:
                @block.sync
                def _(eng):
                    run("sp", eng)

    def barrier(self):
        pend = []
        for e in self.ENGS:
            lst = self.ops[e]
            if not lst:
                continue
            if not lst[-1].dma:
                pend.append(lst[-1])
            seen = set()
            for op in reversed(lst):
                if op.dma and op.dsem not in seen:
                    seen.add(op.dsem)
                    pend.append(op)
                if len(seen) >= self.NDMA.get(e, 0) or (lst[-1].idx - op.idx) > 64:
                    break
        self._bar = pend
        self._bar_done = set()

    def _bar_deps(self, eng):
        if getattr(self, "_bar", None) and eng not in self._bar_done:
            self._bar_done.add(eng)
            return list(self._bar)
        return []


_orig_add = Sched.add


def _add(self, eng, fn, reads=(), writes=(), dma=False):
    extra = self._bar_deps(eng)
    op = _orig_add(self, eng, fn, reads, writes, dma)
    for d in extra:
        if d is not op and all(d is not k for k in op.deps):
            if d.eng == eng and not d.dma and eng != "pe" and False:
                continue
            op.deps.append(d)
            d.signal = True
    return op


Sched.add = _add


class Ctx:
    def __init__(self, nc, st):
        self.nc = nc
        self.st = st
        self.S = Sched(nc)
        self.base = None
        self.uid = 0
        self.psum = []
        self.pbuf = []
        for i in range(8):
            t = st.enter_context(nc.psum_tensor(f"psb{i}", [128, 512], F32))
            self.psum.append(t)
            self.pbuf.append(Buf(f"ps{i}"))
        self.ARENA0 = 16512 + 64
        self.off = self.ARENA0
        self.persist_off = self.ARENA0

    def reset_arena(self, keep=None):
        self.off = self.persist_off if keep is None else keep

    def sb(self, shape, dtype, name=None):
        esz = 4 if dtype in (F32, I32) else 2
        n = 1
        for s in shape[1:]:
            n *= s
        nbytes = (n * esz + 63) // 64 * 64
        self.uid += 1
        nm = f"{name or 't'}_{self.uid}"
        t = self.nc.alloc_sbuf_tensor_at(nm, list(shape), dtype, offset=self.off)
        self.off += nbytes
        assert self.off <= 228000, f"SBUF arena overflow {self.off}"
        return t


class T:
    __slots__ = ("ap", "b")

    def __init__(self, ap, b=None):
        if ap is not None and "TensorHandle" in type(ap).__name__:
            ap = ap[:]
        self.ap = ap
        self.b = b if b is not None else Buf()


def _bufs(ts):
    return [t.b for t in ts]


def op(C, eng, fn, reads=(), writes=(), dma=False):
    return C.S.add(eng, fn, _bufs(reads), _bufs(writes), dma)


def mm_group(C, ps, pairs, reads):
    n = len(pairs)

    def fn(e):
        inst = None
        for k, (l, r) in enumerate(pairs):
            inst = e.matmul(ps.ap, lhsT=l, rhs=r, start=(k == 0), stop=(k == n - 1))
        return inst
    return op(C, "pe", fn, reads=reads, writes=[ps])


def setup_consts(C, lng_d, lnb_d):
    nc = C.nc
    K = {}
    K["ones1024"] = T(C.sb([128, 128], BF16, "ones1024"))
    op(C, "dve", lambda e: e.memset(K["ones1024"].ap[:], 1.0 / 1024.0), writes=[K["ones1024"]])
    K["neghalf"] = T(C.sb([128, TT], F32, "neghalf"))
    op(C, "pool", lambda e: e.memset(K["neghalf"].ap[:], -0.5), writes=[K["neghalf"]])
    K["lng"] = T(C.sb([128, 12 * NCH], F32, "lng"))
    K["lnb"] = T(C.sb([128, 12 * NCH], F32, "lnb"))
    op(C, "sp", lambda e: e.dma_start(out=K["lng"].ap[:], in_=lng_d), writes=[K["lng"]], dma=True)
    op(C, "sp", lambda e: e.dma_start(out=K["lnb"].ap[:], in_=lnb_d), writes=[K["lnb"]], dma=True)
    C.K = K
    make_masks(C)
    C.persist_off = C.off
    return K


def ln_tail(C, zt, nch, ones, eps, zb, zq, ps_m, ps_q, tmp, g_cols, b_cols, outs, func=None, gb_reads=None):
    K = C.K
    mm_group(C, ps_m, [(ones.ap[:], zb[j].ap) for j in range(nch)], reads=[ones] + zb)
    mm_group(C, ps_q, [(ones.ap[:], zq[j].ap) for j in range(nch)], reads=[ones] + zq)
    op(C, "act", lambda e: e.activation(out=tmp["m2"].ap, in_=ps_m.ap, func=AF.Square), reads=[ps_m], writes=[tmp["m2"]])
    op(C, "act", lambda e: e.activation(out=tmp["mean"].ap, in_=ps_m.ap, func=AF.Copy), reads=[ps_m], writes=[tmp["mean"]])
    op(C, "dve", lambda e: e.scalar_tensor_tensor(out=tmp["vare"].ap, in0=ps_q.ap, scalar=float(eps), in1=tmp["m2"].ap,
                                                  op0=ALU.add, op1=ALU.subtract), reads=[ps_q, tmp["m2"]], writes=[tmp["vare"]])
    op(C, "act", lambda e: e.activation(out=tmp["rstd"].ap, in_=tmp["vare"].ap, func=AF.Ln), reads=[tmp["vare"]], writes=[tmp["rstd"]])
    op(C, "act", lambda e: e.activation(out=tmp["rstd"].ap, in_=tmp["rstd"].ap, func=AF.Exp, scale=-0.5), reads=[tmp["rstd"]], writes=[tmp["rstd"]])
    for j in range(nch):
        zc = tmp["zc%d" % (j % 2)]
        op(C, "dve", (lambda j, zc: lambda e: e.tensor_tensor(out=zc.ap, in0=zt[j].ap, in1=tmp["mean"].ap, op=ALU.subtract))(j, zc),
           reads=[zt[j], tmp["mean"]], writes=[zc])
        op(C, "dve", (lambda j, zc: lambda e: e.tensor_tensor(out=zc.ap, in0=zc.ap, in1=tmp["rstd"].ap, op=ALU.mult))(j, zc),
           reads=[zc, tmp["rstd"]], writes=[zc])
        op(C, "act", (lambda j, zc: lambda e: e.activation(out=outs[j].ap, in_=zc.ap, func=(func or AF.Identity),
                                                          scale=g_cols[j], bias=b_cols[j]))(j, zc),
           reads=[zc] + (gb_reads if gb_reads is not None else [K["lng"], K["lnb"]]), writes=[outs[j]])


def ffn_phase(C, xsrc, xdst, wg_d, wu_d, wd_d, ln_idx):
    nc = C.nc
    K = C.K
    C.S.barrier()
    C.reset_arena()
    xt = [[T(None) for j in range(NCH)] for s in range(2)]
    xt_t = [C.sb([128, NCH, TT], F32, "xt") for s in range(2)]
    for s in range(2):
        for j in range(NCH):
            xt[s][j].ap = xt_t[s][:, j, :]
    xb_t = [C.sb([128, NCH, TT], BF16, "xb") for s in range(2)]
    xb = [[T(xb_t[s][:, j, :]) for j in range(NCH)] for s in range(2)]
    hT_t = C.sb([128, NFT, TT], BF16, "hT")
    hT = [T(hT_t[:, f, :]) for f in range(NFT)]
    NWS = 4
    wg_t = [T(C.sb([128, NCH, 256], BF16, "wg")) for s in range(NWS)]
    wu_t = [T(C.sb([128, NCH, 256], BF16, "wu")) for s in range(NWS)]
    wd_t = [T(C.sb([128, NFT, 128], BF16, "wd")) for s in range(NWS)]
    sg = [T(C.sb([128, TT], F32, "sg")) for s in range(2)]
    zb_t = C.sb([128, NCH, TT], BF16, "zb")
    zq_t = C.sb([128, NCH, TT], BF16, "zq")
    zb = [T(zb_t[:, j, :]) for j in range(NCH)]
    zq = [T(zq_t[:, j, :]) for j in range(NCH)]
    tmp = {k: T(C.sb([128, TT], F32, k)) for k in ("m2", "mean", "vare", "rstd", "zc0", "zc1")}
    for k in tmp:
        tmp[k].ap = tmp[k].ap[:]
    ps = [T(C.psum[i][:], C.pbuf[i]) for i in range(8)]
    g_cols = [K["lng"].ap[:, ln_idx * NCH + j: ln_idx * NCH + j + 1] for j in range(NCH)]
    b_cols = [K["lnb"].ap[:, ln_idx * NCH + j: ln_idx * NCH + j + 1] for j in range(NCH)]
    xs_v = xsrc.rearrange("(c p) t -> p c t", p=128)
    xd_v = xdst.rearrange("(c p) t -> p c t", p=128)
    cres = 0.5 / DN_ALPHA
    eps = LN_EPS / (DN_ALPHA * DN_ALPHA)
    cnt = {"gu": 0, "wd": 0, "w": 0}

    def load(t):
        s = t % 2
        op(C, "sp", lambda e: e.dma_start(out=xt_t[s][:], in_=xs_v[:, :, t * TT:(t + 1) * TT]), writes=xt[s], dma=True)
        for j in range(NCH):
            op(C, "act", (lambda j: lambda e: e.activation(out=xb[s][j].ap, in_=xt[s][j].ap, func=AF.Copy))(j),
               reads=[xt[s][j]], writes=[xb[s][j]])

    import os
    NOW = int(os.environ.get('FFN_NOW', '0'))

    def wload(fg):
        k = cnt["w"] % NWS
        cnt["w"] += 1
        if NOW and cnt["w"] > 2:
            return k
        op(C, "pool", lambda e: e.dma_start(out=wg_t[k].ap[:], in_=wg_d[fg]), writes=[wg_t[k]], dma=True)
        op(C, "pool", lambda e: e.dma_start(out=wu_t[k].ap[:], in_=wu_d[fg]), writes=[wu_t[k]], dma=True)
        return k

    def gu(t, fgs):
        s = t % 2
        for fg in fgs:
            k = wload(fg)
            for ft in range(2):
                f = fg * 2 + ft
                i = cnt["gu"] % 2
                cnt["gu"] += 1
                pg, pu = ps[i], ps[2 + i]
                mm_group(C, pg, [(wg_t[k].ap[:, c, ft * 128:(ft + 1) * 128], xb[s][c].ap) for c in range(NCH)],
                         reads=[wg_t[k]] + xb[s])
                mm_group(C, pu, [(wu_t[k].ap[:, c, ft * 128:(ft + 1) * 128], xb[s][c].ap) for c in range(NCH)],
                         reads=[wu_t[k]] + xb[s])
                op(C, "act", (lambda i, pg: lambda e: e.activation(out=sg[i].ap[:], in_=pg.ap, func=AF.Silu))(i, pg),
                   reads=[pg], writes=[sg[i]])
                op(C, "dve", (lambda i, pu, f: lambda e: e.tensor_tensor(out=hT[f].ap, in0=sg[i].ap[:], in1=pu.ap, op=ALU.mult))(i, pu, f),
                   reads=[sg[i], pu], writes=[hT[f]])

    def down(t):
        s = t % 2
        for j in range(NCH):
            k = cnt["wd"] % NWS
            cnt["wd"] += 1
            if not (NOW and cnt["wd"] > 2):
                op(C, "pool", (lambda k, j: lambda e: e.dma_start(out=wd_t[k].ap[:], in_=wd_d[j]))(k, j), writes=[wd_t[k]], dma=True)
            py = ps[4 + k % 2]
            mm_group(C, py, [(wd_t[k].ap[:, c, :], hT[c].ap) for c in range(NFT)], reads=[wd_t[k]] + hT)
            op(C, "dve", (lambda j, py: lambda e: e.scalar_tensor_tensor(out=xt[s][j].ap, in0=py.ap, scalar=float(cres), in1=xt[s][j].ap,
                                                                          op0=ALU.mult, op1=ALU.add))(j, py),
               reads=[py, xt[s][j]], writes=[xt[s][j]])
            op(C, "act", (lambda j: lambda e: e.activation(out=zb[j].ap, in_=xt[s][j].ap, func=AF.Copy))(j), reads=[xt[s][j]], writes=[zb[j]])
            op(C, "act", (lambda j: lambda e: e.activation(out=zq[j].ap, in_=xt[s][j].ap, func=AF.Square))(j), reads=[xt[s][j]], writes=[zq[j]])

    def tail(t):
        s = t % 2
        ln_tail(C, xt[s], NCH, K["ones1024"], eps, zb, zq, ps[6], ps[7], tmp, g_cols, b_cols, xt[s])
        op(C, "sp", lambda e: e.dma_start(out=xd_v[:, :, t * TT:(t + 1) * TT], in_=xt_t[s][:]), reads=xt[s], dma=True)

    load(0)
    for t in range(NTT):
        gu(t, range(0, 2))
        if t > 0:
            tail(t - 1)
        if t + 1 < NTT:
            load(t + 1)
        gu(t, range(2, 11))
        down(t)
    tail(NTT - 1)


def host_ffn_weights(inputs, i, k):
    wg = np.ascontiguousarray(inputs["ffn_w_gate"][i, k].reshape(NCH, 128, 11, 256).transpose(2, 1, 0, 3))
    wu = np.ascontiguousarray(inputs["ffn_w_up"][i, k].reshape(NCH, 128, 11, 256).transpose(2, 1, 0, 3))
    wd = np.ascontiguousarray(inputs["ffn_w_down"][i, k].reshape(NFT, 128, NCH, 128).transpose(2, 1, 0, 3))
    return wg, wu, wd


def host_ln(inputs):
    g = np.ascontiguousarray(inputs["ln_g"].reshape(12, NCH, 128).transpose(2, 0, 1).reshape(128, 12 * NCH))
    b = np.ascontiguousarray(inputs["ln_b"].reshape(12, NCH, 128).transpose(2, 0, 1).reshape(128, 12 * NCH))
    return g, b


def build_program(plan):
    import contextlib
    nc = bass.Bass("TRN2", target_bir_lowering=False)
    xin = nc.dram_tensor("xin", [D_MODEL, SEQ], F32, kind="ExternalInput").ap()
    xout = nc.dram_tensor("xout", [D_MODEL, SEQ], F32, kind="ExternalOutput").ap()
    lng_d = nc.dram_tensor("lng", [128, 12 * NCH], F32, kind="ExternalInput").ap()
    lnb_d = nc.dram_tensor("lnb", [128, 12 * NCH], F32, kind="ExternalInput").ap()
    scr = []
    if len(plan) > 1:
        scr = [nc.dram_tensor(f"xscr{i}", [D_MODEL, SEQ], F32, kind="Internal").ap() for i in range(2)]
    with contextlib.ExitStack() as st:
        C = Ctx(nc, st)
        setup_consts(C, lng_d, lnb_d)
        src = xin
        for n, ph in enumerate(plan):
            dst = xout if n == len(plan) - 1 else scr[n % 2]
            if ph[0] == "ffn":
                _, i, k = ph
                wg_d = nc.dram_tensor(f"wg_{i}_{k}", [11, 128, NCH, 256], F32, kind="ExternalInput").ap()
                wu_d = nc.dram_tensor(f"wu_{i}_{k}", [11, 128, NCH, 256], F32, kind="ExternalInput").ap()
                wd_d = nc.dram_tensor(f"wd_{i}_{k}", [NCH, 128, NFT, 128], F32, kind="ExternalInput").ap()
                ffn_phase(C, src, dst, wg_d, wu_d, wd_d, i * 3 + (0 if k == 0 else 2))
            else:
                MIXERS[ph[0]](C, src, dst, ph)
            src = dst
        C.S.emit()
    return nc


def host_inputs_for(plan, inputs):
    d = {}
    g, b = host_ln(inputs)
    d["lng"] = g
    d["lnb"] = b
    for ph in plan:
        if ph[0] == "ffn":
            _, i, k = ph
            wg, wu, wd = host_ffn_weights(inputs, i, k)
            d[f"wg_{i}_{k}"] = wg
            d[f"wu_{i}_{k}"] = wu
            d[f"wd_{i}_{k}"] = wd
        else:
            d.update(MIXER_HOST[ph[0]](inputs, ph))
    return d


MIXERS = {}
MIXER_HOST = {}


class MixBase:
    def __init__(self, C, xsrc, xdst, ln_idx, nbuf=2, alloc_z=True, alloc_tmp=True):
        self.C = C
        self.nbuf = nbuf
        K = C.K
        C.S.barrier()
        C.reset_arena()
        self.xt_t = [C.sb([128, NCH, TT], F32, "xt") for s in range(nbuf)]
        self.xt = [[T(self.xt_t[s][:, j, :]) for j in range(NCH)] for s in range(nbuf)]
        self.xb_t = [C.sb([128, NCH, TT], BF16, "xb") for s in range(nbuf)]
        self.xb = [[T(self.xb_t[s][:, j, :]) for j in range(NCH)] for s in range(nbuf)]
        if alloc_z:
            zb_t = C.sb([128, NCH, TT], BF16, "zb")
            zq_t = C.sb([128, NCH, TT], BF16, "zq")
            self.zb = [T(zb_t[:, j, :]) for j in range(NCH)]
            self.zq = [T(zq_t[:, j, :]) for j in range(NCH)]
        if alloc_tmp:
            self.tmp = {k: T(C.sb([128, TT], F32, k)[:]) for k in ("m2", "mean", "vare", "rstd", "zc0", "zc1")}
        self.ps = [T(C.psum[i][:], C.pbuf[i]) for i in range(8)]
        self.g_cols = [K["lng"].ap[:, ln_idx * NCH + j: ln_idx * NCH + j + 1] for j in range(NCH)]
        self.b_cols = [K["lnb"].ap[:, ln_idx * NCH + j: ln_idx * NCH + j + 1] for j in range(NCH)]
        self.xs_v = xsrc.rearrange("(c p) t -> p c t", p=128)
        self.xd_v = xdst.rearrange("(c p) t -> p c t", p=128)
        self.eps = LN_EPS / (DN_ALPHA * DN_ALPHA)

    def load(self, t):
        C = self.C
        s = t % self.nbuf
        op(C, "sp", lambda e: e.dma_start(out=self.xt_t[s][:], in_=self.xs_v[:, :, t * TT:(t + 1) * TT]), writes=self.xt[s], dma=True)
        for j in range(NCH):
            op(C, "act", (lambda j: lambda e: e.activation(out=self.xb[s][j].ap, in_=self.xt[s][j].ap, func=AF.Copy))(j),
               reads=[self.xt[s][j]], writes=[self.xb[s][j]])

    def resid(self, t, j, py):
        C = self.C
        s = t % self.nbuf
        xt = self.xt[s][j]
        op(C, "dve", lambda e: e.scalar_tensor_tensor(out=xt.ap, in0=py.ap, scalar=float(1.0 / DN_ALPHA), in1=xt.ap,
                                                      op0=ALU.mult, op1=ALU.add), reads=[py, xt], writes=[xt])
        op(C, "act", lambda e: e.activation(out=self.zb[j].ap, in_=xt.ap, func=AF.Copy), reads=[xt], writes=[self.zb[j]])
        op(C, "act", lambda e: e.activation(out=self.zq[j].ap, in_=xt.ap, func=AF.Square), reads=[xt], writes=[self.zq[j]])

    def finish(self, t):
        C = self.C
        s = t % self.nbuf
        ln_tail(C, self.xt[s], NCH, C.K["ones1024"], self.eps, self.zb, self.zq, self.ps[6], self.ps[7], self.tmp,
                self.g_cols, self.b_cols, self.xt[s])
        op(C, "sp", lambda e: e.dma_start(out=self.xd_v[:, :, t * TT:(t + 1) * TT], in_=self.xt_t[s][:]), reads=self.xt[s], dma=True)


def load_w_resident(C, w_d, nk, ncols, name, piece=512):
    t = C.sb([128, nk, ncols], BF16, name)
    W = T(t)
    v = w_d.rearrange("(c p) n -> p c n", p=128)
    for c0 in range(0, ncols, piece):
        c1 = min(ncols, c0 + piece)
        op(C, "pool", (lambda c0, c1: lambda e: e.dma_start(out=t[:, :, c0:c1], in_=v[:, :, c0:c1]))(c0, c1), writes=[W], dma=True)
    return W


def load_cols(C, v_d, ncol, name):
    t = T(C.sb([128, ncol], F32, name))
    op(C, "sp", lambda e: e.dma_start(out=t.ap[:], in_=v_d), writes=[t], dma=True)
    return t


CONF_K = 31


def conv_phase(C, xsrc, xdst, ph):
    nc = C.nc
    M = MixBase(C, xsrc, xdst, 0 * 3 + 1)
    w_in_d = nc.dram_tensor("conv_w_in", [D_MODEL, 2 * D_MODEL], F32, kind="ExternalInput").ap()
    w_out_d = nc.dram_tensor("conv_w_out", [D_MODEL, D_MODEL], F32, kind="ExternalInput").ap()
    wdw_d = nc.dram_tensor("conv_wdw", [128, CONF_K * NCH], F32, kind="ExternalInput").ap()
    cg_d = nc.dram_tensor("conv_lng", [128, NCH], F32, kind="ExternalInput").ap()
    cb_d = nc.dram_tensor("conv_lnb", [128, NCH], F32, kind="ExternalInput").ap()
    w_in = load_w_resident(C, w_in_d, NCH, 2 * D_MODEL, "cwin")
    w_out = load_w_resident(C, w_out_d, NCH, D_MODEL, "cwout")
    wdw = load_cols(C, wdw_d, CONF_K * NCH, "wdw")
    wdw2_d = nc.dram_tensor("conv_wdw2", [128, NCH * CONF_K], F32, kind="ExternalInput").ap()
    wdw2 = load_cols(C, wdw2_d, CONF_K * NCH, "wdw2")
    dg = [T(C.sb([128, CONF_K, 128], BF16, "dg")) for i in range(2)]
    cg = load_cols(C, cg_d, NCH, "cg")
    cb = load_cols(C, cb_d, NCH, "cb")
    H = CONF_K - 1
    vx_t = C.sb([128, NCH, H + TT], BF16, "vext")
    vx = [T(vx_t[:, j, :]) for j in range(NCH)]
    op(C, "pool", lambda e: e.memset(vx_t[:], 0.0), writes=vx)
    sgm = [T(C.sb([128, TT], F32, "sgm")[:]) for i in range(2)]
    acc = [T(C.sb([128, TT], F32, "acc")[:]) for i in range(NCH)]
    ub_t = C.sb([128, NCH, TT], BF16, "ub")
    uq_t = C.sb([128, NCH, TT], BF16, "uq")
    ub = [T(ub_t[:, j, :]) for j in range(NCH)]
    uq = [T(uq_t[:, j, :]) for j in range(NCH)]
    sT_t = C.sb([128, NCH, TT], BF16, "sT")
    sT = [T(sT_t[:, j, :]) for j in range(NCH)]
    ps = M.ps
    cgc = [cg.ap[:, j:j + 1] for j in range(NCH)]
    cbc = [cb.ap[:, j:j + 1] for j in range(NCH)]
    M.load(0)
    for t in range(NTT):
        s = t % 2
        xb = M.xb[s]
        if t + 1 < NTT:
            M.load(t + 1)
        for oc in range(NCH):
            i = oc % 2
            pa, pg = ps[i], ps[2 + i]
            mm_group(C, pa, [(w_in.ap[:, c, oc * 128:(oc + 1) * 128], xb[c].ap) for c in range(NCH)], reads=[w_in] + xb)
            mm_group(C, pg, [(w_in.ap[:, c, D_MODEL + oc * 128:D_MODEL + (oc + 1) * 128], xb[c].ap) for c in range(NCH)], reads=[w_in] + xb)
            op(C, "act", (lambda i, pg: lambda e: e.activation(out=sgm[i].ap, in_=pg.ap, func=AF.Sigmoid))(i, pg), reads=[pg], writes=[sgm[i]])
            op(C, "dve", (lambda i, pa, oc: lambda e: e.tensor_tensor(out=vx_t[:, oc, H:H + TT], in0=sgm[i].ap, in1=pa.ap, op=ALU.mult))(i, pa, oc),
               reads=[sgm[i], pa, vx[oc]], writes=[vx[oc]])
        for oc in range(NCH):
            d = dg[oc % 2]
            op(C, "pool", (lambda d, oc: lambda e: e.tensor_tensor(out=d.ap, in0=C.K["identb"].ap.unsqueeze(1).to_broadcast([128, CONF_K, 128]),
                                                                   in1=wdw2.ap[:, oc * CONF_K:(oc + 1) * CONF_K].unsqueeze(2).to_broadcast([128, CONF_K, 128]),
                                                                   op=ALU.mult))(d, oc), reads=[C.K["identb"], wdw2], writes=[d])
            pcv = ps[oc % 4]
            mm_group(C, pcv, [(d.ap[:, k, :], vx_t[:, oc, k:k + TT]) for k in range(CONF_K)], reads=[d, vx[oc]])
            act(C, acc[oc], pcv, AF.Copy)
        for oc in range(NCH):
            op(C, "pool", (lambda oc: lambda e: e.tensor_copy(out=vx_t[:, oc, 0:H], in_=vx_t[:, oc, TT:TT + H]))(oc), reads=[vx[oc]], writes=[vx[oc]])
            op(C, "act", (lambda oc: lambda e: e.activation(out=ub[oc].ap, in_=acc[oc].ap, func=AF.Copy))(oc), reads=[acc[oc]], writes=[ub[oc]])
            op(C, "act", (lambda oc: lambda e: e.activation(out=uq[oc].ap, in_=acc[oc].ap, func=AF.Square))(oc), reads=[acc[oc]], writes=[uq[oc]])
        ln_tail(C, acc, NCH, C.K["ones1024"], LN_EPS, ub, uq, ps[6], ps[7], M.tmp, cgc, cbc, sT, func=AF.Silu, gb_reads=[cg, cb])
        for j in range(NCH):
            py = ps[4 + j % 2]
            mm_group(C, py, [(w_out.ap[:, c, j * 128:(j + 1) * 128], sT[c].ap) for c in range(NCH)], reads=[w_out] + sT)
            M.resid(t, j, py)
        M.finish(t)


def conv_host(inputs, ph):
    d = {}
    d["conv_w_in"] = np.ascontiguousarray(inputs["conv_w_in"][0])
    d["conv_w_out"] = np.ascontiguousarray(inputs["conv_w_out"][0])
    d["conv_wdw"] = np.ascontiguousarray(inputs["conv_w_dw"][0].reshape(CONF_K, NCH, 128).transpose(2, 0, 1).reshape(128, CONF_K * NCH))
    d["conv_wdw2"] = np.ascontiguousarray(inputs["conv_w_dw"][0].reshape(CONF_K, NCH, 128).transpose(2, 1, 0).reshape(128, NCH * CONF_K))
    d["conv_lng"] = np.ascontiguousarray(inputs["conv_ln_g"][0].reshape(NCH, 128).T)
    d["conv_lnb"] = np.ascontiguousarray(inputs["conv_ln_b"][0].reshape(NCH, 128).T)
    return d


MIXERS["conv"] = conv_phase
MIXER_HOST["conv"] = conv_host


def sconv_phase(C, xsrc, xdst, ph):
    nc = C.nc
    M = MixBase(C, xsrc, xdst, 1 * 3 + 1)
    w_in_d = nc.dram_tensor("sc_w_in", [D_MODEL, 3 * D_MODEL], F32, kind="ExternalInput").ap()
    w_out_d = nc.dram_tensor("sc_w_out", [D_MODEL, D_MODEL], F32, kind="ExternalInput").ap()
    wc_d = nc.dram_tensor("sc_wc", [128, 3 * NCH], F32, kind="ExternalInput").ap()
    w_in = load_w_resident(C, w_in_d, NCH, 3 * D_MODEL, "swin")
    w_out = load_w_resident(C, w_out_d, NCH, D_MODEL, "swout")
    wc = load_cols(C, wc_d, 3 * NCH, "swc")
    H = 2
    cv_t = C.sb([128, NCH, H + TT], BF16, "cvext")
    wc2_d = nc.dram_tensor("sc_wc2", [128, NCH * 3], F32, kind="ExternalInput").ap()
    wc2 = load_cols(C, wc2_d, 3 * NCH, "swc2")
    dg3 = T(C.sb([128, NCH * 3, 128], BF16, "dg3"))
    op(C, "pool", lambda e: e.tensor_tensor(out=dg3.ap, in0=C.K["identb"].ap.unsqueeze(1).to_broadcast([128, NCH * 3, 128]),
                                            in1=wc2.ap.unsqueeze(2).to_broadcast([128, NCH * 3, 128]), op=ALU.mult),
       reads=[C.K["identb"], wc2], writes=[dg3])
    cv = [T(cv_t[:, j, :]) for j in range(NCH)]
    op(C, "pool", lambda e: e.memset(cv_t[:], 0.0), writes=cv)
    cs = [T(C.sb([128, TT], F32, "cs")[:]) for i in range(2)]
    gb = [T(C.sb([128, TT], F32, "gb")[:]) for i in range(2)]
    acc = [T(C.sb([128, TT], F32, "acc")[:]) for i in range(2)]
    yb_t = C.sb([128, NCH, TT], BF16, "yb")
    yb = [T(yb_t[:, j, :]) for j in range(NCH)]
    ps = M.ps
    M.load(0)
    for t in range(NTT):
        s = t % 2
        xb = M.xb[s]
        if t + 1 < NTT:
            M.load(t + 1)
        for oc in range(NCH):
            i = oc % 2
            pc, pv, pb = ps[i], ps[2 + i], ps[4]
            mm_group(C, pc, [(w_in.ap[:, c, D_MODEL + oc * 128:D_MODEL + (oc + 1) * 128], xb[c].ap) for c in range(NCH)], reads=[w_in] + xb)
            mm_group(C, pv, [(w_in.ap[:, c, 2 * D_MODEL + oc * 128:2 * D_MODEL + (oc + 1) * 128], xb[c].ap) for c in range(NCH)], reads=[w_in] + xb)
            mm_group(C, pb, [(w_in.ap[:, c, oc * 128:(oc + 1) * 128], xb[c].ap) for c in range(NCH)], reads=[w_in] + xb)
            op(C, "act", (lambda i, pc: lambda e: e.activation(out=cs[i].ap, in_=pc.ap, func=AF.Copy))(i, pc), reads=[pc], writes=[cs[i]])
            op(C, "act", (lambda i, pb: lambda e: e.activation(out=gb[i].ap, in_=pb.ap, func=AF.Copy))(i, pb), reads=[pb], writes=[gb[i]])
            op(C, "dve", (lambda i, pv, oc: lambda e: e.tensor_tensor(out=cv_t[:, oc, H:H + TT], in0=cs[i].ap, in1=pv.ap, op=ALU.mult))(i, pv, oc),
               reads=[cs[i], pv, cv[oc]], writes=[cv[oc]])
            pcv = ps[6 + i]
            mm_group(C, pcv, [(dg3.ap[:, oc * 3 + k, :], cv_t[:, oc, k:k + TT]) for k in range(3)], reads=[dg3, cv[oc]])
            op(C, "dve", (lambda i, oc, pcv: lambda e: e.tensor_tensor(out=yb[oc].ap, in0=gb[i].ap, in1=pcv.ap, op=ALU.mult))(i, oc, pcv),
               reads=[pcv, gb[i]], writes=[yb[oc]])
            op(C, "pool", (lambda oc: lambda e: e.tensor_copy(out=cv_t[:, oc, 0:H], in_=cv_t[:, oc, TT:TT + H]))(oc), reads=[cv[oc]], writes=[cv[oc]])
        for j in range(NCH):
            py = ps[5]
            mm_group(C, py, [(w_out.ap[:, c, j * 128:(j + 1) * 128], yb[c].ap) for c in range(NCH)], reads=[w_out] + yb)
            M.resid(t, j, py)
        M.finish(t)


def sconv_host(inputs, ph):
    d = {}
    d["sc_w_in"] = np.ascontiguousarray(inputs["sc_w_in"][0])
    d["sc_w_out"] = np.ascontiguousarray(inputs["sc_w_out"][0])
    d["sc_wc"] = np.ascontiguousarray(inputs["sc_w_conv"][0].reshape(3, NCH, 128).transpose(2, 0, 1).reshape(128, 3 * NCH))
    d["sc_wc2"] = np.ascontiguousarray(inputs["sc_w_conv"][0].reshape(3, NCH, 128).transpose(2, 1, 0).reshape(128, 3 * NCH))
    return d


MIXERS["sconv"] = sconv_phase
MIXER_HOST["sconv"] = sconv_host


def tv(t, ap):
    return T(ap, t.b)


def act(C, out, in_, func, extra=(), **kw):
    return op(C, "act", lambda e: e.activation(out=out.ap, in_=in_.ap, func=func, **kw), reads=[in_] + list(extra), writes=[out])


def tt(C, eng, out, a, b, alu):
    return op(C, eng, lambda e: e.tensor_tensor(out=out.ap, in0=a.ap, in1=b.ap, op=alu), reads=[a, b], writes=[out])


def ts(C, eng, out, a, s1, s2, op0, op1=None, extra=(), accum=None):
    def fn(e):
        kw = {}
        if op1 is not None:
            kw["op1"] = op1
        if accum is not None:
            kw["accum_out"] = accum.ap
        return e.tensor_scalar(out=out.ap, in0=a.ap, scalar1=s1, scalar2=s2, op0=op0, **kw)
    return op(C, eng, fn, reads=[a] + list(extra), writes=[out] + ([accum] if accum is not None else []))


def stt(C, out, a, scalar, b, op0, op1, extra=()):
    return op(C, "dve", lambda e: e.scalar_tensor_tensor(out=out.ap, in0=a.ap, scalar=scalar, in1=b.ap, op0=op0, op1=op1),
              reads=[a, b] + list(extra), writes=[out])


def mm1(C, ps, lhsT, rhs, extra=()):
    return op(C, "pe", lambda e: e.matmul(ps.ap, lhsT=lhsT.ap, rhs=rhs.ap, start=True, stop=True), reads=[lhsT, rhs] + list(extra), writes=[ps])


def rsqrt_act(C, t):
    op(C, "act", lambda e: e.activation(out=t.ap, in_=t.ap, func=AF.Ln), reads=[t], writes=[t])
    op(C, "act", lambda e: e.activation(out=t.ap, in_=t.ap, func=AF.Exp, scale=-0.5), reads=[t], writes=[t])


def make_masks(C):
    K = C.K
    onesf = T(C.sb([128, 128], F32, "onesf"))
    op(C, "pool", lambda e: e.memset(onesf.ap[:], 1.0), writes=[onesf])
    K["onesf"] = onesf

    def sel(name, pattern, base, cm, cmp_):
        t = T(C.sb([128, 128], F32, name))
        op(C, "pool", lambda e: e.affine_select(out=t.ap[:], in_=onesf.ap[:], pattern=pattern, compare_op=cmp_, fill=0.0,
                                                base=base, channel_multiplier=cm), reads=[onesf], writes=[t])
        K[name] = t
        return t
    sel("identf", [[1, 128]], 0, -1, ALU.is_equal)
    sel("Ls", [[-1, 128]], -1, 1, ALU.is_ge)
    sel("Lt", [[-1, 128]], 0, 1, ALU.is_ge)
    sel("Ut", [[1, 128]], 0, -1, ALU.is_ge)
    idb = T(C.sb([128, 128], BF16, "identb"))
    op(C, "pool", lambda e: e.tensor_copy(out=idb.ap[:], in_=K["identf"].ap[:]), reads=[K["identf"]], writes=[idb])
    K["identb"] = idb
    o128 = T(C.sb([128, 128], BF16, "ones128"))
    op(C, "dve", lambda e: e.memset(o128.ap[:], 1.0 / 128.0), writes=[o128])
    K["ones128"] = o128
    o1 = T(C.sb([128, 128], BF16, "ones1"))
    op(C, "dve", lambda e: e.memset(o1.ap[:], 1.0), writes=[o1])
    K["ones1"] = o1


GDN_HK, GDN_HV = 8, 16
GDN_IN = 6176


def gdn_phase(C, xsrc, xdst, ph):
    import os
    STOP = int(os.environ.get('GDN_STOP', '99'))
    nc = C.nc
    K = C.K
    M = MixBase(C, xsrc, xdst, 3 * 3 + 1, nbuf=1)
    w_in_d = nc.dram_tensor("gdn_w_in", [D_MODEL, 6144], F32, kind="ExternalInput").ap().rearrange("(c p) n -> p c n", p=128)
    w_ba_d = nc.dram_tensor("gdn_w_ba", [D_MODEL, 64], F32, kind="ExternalInput").ap().rearrange("(c p) n -> p c n", p=128)
    w_out_d = nc.dram_tensor("gdn_w_out", [2048, D_MODEL], F32, kind="ExternalInput").ap()
    w_out_v = w_out_d.rearrange("(h p) n -> p h n", p=128)
    wcv_d = nc.dram_tensor("gdn_wconv", [128, 4 * 32], F32, kind="ExternalInput").ap()
    adt_d = nc.dram_tensor("gdn_adt", [128, 2], F32, kind="ExternalInput").ap()
    ng_d = nc.dram_tensor("gdn_ng", [128, 1], F32, kind="ExternalInput").ap()
    gb_scr = nc.dram_tensor("gdn_gb_scr", [32, TT], F32, kind="Internal").ap()
    gbs = T(None)
    wcv = load_cols(C, wcv_d, 128, "wcv")
    adt = load_cols(C, adt_d, 2, "adt")
    ng = load_cols(C, ng_d, 1, "ng")
    nea = T(C.sb([128, 1], F32, "nea"))
    act(C, nea, tv(adt, adt.ap[:, 0:1]), AF.Exp)
    ts(C, "dve", nea, nea, -1.0, None, ALU.mult)
    w_ba = T(C.sb([128, NCH, 64], BF16, "wba"))
    op(C, "pool", lambda e: e.dma_start(out=w_ba.ap[:], in_=w_ba_d), writes=[w_ba], dma=True)
    qk_t = C.sb([128, 16, TT], BF16, "qk")
    vv_t = C.sb([128, 16, TT], BF16, "vv")
    og_t = C.sb([128, 16, TT], BF16, "og")
    qk = [T(qk_t[:, i, :]) for i in range(16)]
    vv = [T(vv_t[:, i, :]) for i in range(16)]
    ogb = [Buf() for i in range(4)]
    og = [T(og_t[:, i, :], ogb[i // 4]) for i in range(16)]
    win = [T(C.sb([128, NCH, 128], BF16, "win")) for i in range(4)]
    wout = [T(C.sb([128, 16, 128], BF16, "wout")) for i in range(2)]
    S_t = C.sb([128, 16, 128], F32, "S")
    Sb_t = C.sb([128, 16, 128], BF16, "Sb")
    Sst = [T(S_t[:, h, :]) for h in range(16)]
    Sbt = [T(Sb_t[:, h, :]) for h in range(16)]
    op(C, "pool", lambda e: e.memset(S_t[:], 0.0), writes=Sst)
    op(C, "pool", lambda e: e.memset(Sb_t[:], 0.0), writes=Sbt)
    halo_t = C.sb([128, 32, 4], BF16, "halo")
    halo = [T(halo_t[:, i, :]) for i in range(32)]
    op(C, "pool", lambda e: e.memset(halo_t[:], 0.0), writes=halo)
    rext = [T(C.sb([128, 4 + TT], BF16, "rext")) for i in range(4)]
    dg4 = [T(C.sb([128, 4, 128], BF16, "dg4")) for i in range(4)]
    wcv2_d = nc.dram_tensor("gdn_wconv2", [128, 32 * 4], F32, kind="ExternalInput").ap()
    wcv2 = load_cols(C, wcv2_d, 128, "wcv2")
    cacc = [T(C.sb([128, TT], F32, "cacc")) for i in range(4)]
    sqb = [T(C.sb([128, TT], BF16, "sqb")) for i in range(4)]
    betaT = T(C.sb([128, TT], F32, "betaT"))
    gT = T(C.sb([128, TT], F32, "gT"))
    gc = T(C.sb([128, TT], F32, "gc"))
    spt = T(C.sb([128, TT], F32, "spt"))
    cols_t = C.sb([128, 4, 32], F32, "cols")
    cols = T(cols_t)
    ecol_t = C.sb([128, 4, 48], F32, "ecol")
    ecol = T(ecol_t)
    G = 4
    f32n = ("X", "Eg", "ET", "ETm", "ETn", "E", "Tt", "u", "oT", "rs")
    b16n = ("N", "Nt", "P0", "P1", "Q0", "Q1", "Tb", "kbg", "kt", "vb", "wT", "vnb", "iT", "qg", "sq2")
    L = {k: T(C.sb([128, G, 128], F32, k)) for k in f32n}
    L.update({k: T(C.sb([128, G, 128], BF16, k)) for k in b16n})
    Grow = T(C.sb([128, G, 128], F32, "Grow"))
    Brow = T(C.sb([128, G, 128], F32, "Brow"))
    ps = M.ps
    pbank = [T(C.psum[4 + i][:].rearrange("p (g f) -> p g f", g=4), C.pbuf[4 + i]) for i in range(2)]
    qn = {"i": 0}

    def nq():
        qn["i"] += 1
        return pbank[qn["i"] % 2]
    pfull = {"i": 0}

    def nf():
        pfull["i"] += 1
        return ps[pfull["i"] % 4]
    wn = {"i": 0, "o": 0}
    identb = K["identb"]
    Us = T(C.sb([128, 128], F32, "Us"))
    op(C, "pool", lambda e: e.affine_select(out=Us.ap, in_=K["onesf"].ap, pattern=[[1, 128]], compare_op=ALU.is_ge, fill=0.0,
                                            base=-1, channel_multiplier=-1), reads=[K["onesf"]], writes=[Us])
    A = lambda t: T(t.ap[:], t.b)
    bc = lambda t, ap: T(ap.unsqueeze(1).to_broadcast([128, G, 128]), t.b)

    def mmG(pt, pairs, reads):
        def fn(e):
            inst = None
            for g, (l, r) in enumerate(pairs):
                inst = e.matmul(pt.ap[:, g, :], lhsT=l, rhs=r, start=True, stop=True)
            return inst
        return op(C, "pe", fn, reads=reads, writes=[pt])

    for t in range(NTT):
        s = 0
        M.load(t)
        xb = M.xb[s]
        for pc2 in range(12):
            ocs = [pc2 * 4 + i for i in range(4)]
            pps = []
            for i4 in range(4):
                oc = pc2 * 4 + i4
                wk = win[wn["i"] % 4]
                wn["i"] += 1
                op(C, "pool", (lambda wk, oc: lambda e: e.dma_start(out=wk.ap[:], in_=w_in_d[:, :, oc * 128:(oc + 1) * 128]))(wk, oc), writes=[wk], dma=True)
                pp = nf()
                pps.append(pp)
                mm_group(C, pp, [(wk.ap[:, c, :], xb[c].ap) for c in range(NCH)], reads=[wk] + xb)
                if oc >= 32:
                    act(C, og[oc - 32], pp, AF.Silu)
                else:
                    r = rext[i4]
                    act(C, tv(r, r.ap[:, 3:3 + TT]), pp, AF.Copy)
            if ocs[0] >= 32:
                continue
            pcs = []
            for i, oc in enumerate(ocs):
                r = rext[i]
                op(C, "pool", (lambda r, oc: lambda e: e.tensor_copy(out=r.ap[:, 0:3], in_=halo_t[:, oc, 0:3]))(r, oc), reads=[halo[oc], r], writes=[r])
                op(C, "pool", (lambda r, oc: lambda e: e.tensor_copy(out=halo_t[:, oc, 0:3], in_=r.ap[:, TT:TT + 3]))(r, oc), reads=[r, halo[oc]], writes=[halo[oc]])
                d = dg4[i]
                op(C, "pool", (lambda d, oc: lambda e: e.tensor_tensor(out=d.ap, in0=K["identb"].ap.unsqueeze(1).to_broadcast([128, 4, 128]),
                                                                       in1=wcv2.ap[:, oc * 4:(oc + 1) * 4].unsqueeze(2).to_broadcast([128, 4, 128]),
                                                                       op=ALU.mult))(d, oc), reads=[K["identb"], wcv2], writes=[d])
            for i, oc in enumerate(ocs):
                pcv = nf()
                pcs.append(pcv)
                mm_group(C, pcv, [(dg4[i].ap[:, k, :], rext[i].ap[:, k:k + TT]) for k in range(4)], reads=[dg4[i], rext[i]])
                if oc >= 16:
                    act(C, vv[oc - 16], pcv, AF.Silu)
                else:
                    act(C, qk[oc], pcv, AF.Silu)
            if ocs[0] >= 16:
                continue
            for i, oc in enumerate(ocs):
                act(C, A(sqb[i]), qk[oc], AF.Square)
            pns = []
            for i, oc in enumerate(ocs):
                pn = nf()
                pns.append(pn)
                mm1(C, pn, A(K["ones1"]), A(sqb[i]))
                ts(C, "dve", A(cacc[i]), pn, float(RMS_EPS), None, ALU.add)
            for i, oc in enumerate(ocs):
                op(C, "act", (lambda t_: lambda e: e.activation(out=t_.ap, in_=t_.ap, func=AF.Ln))(A(cacc[i])), reads=[cacc[i]], writes=[cacc[i]])
            for i, oc in enumerate(ocs):
                op(C, "act", (lambda t_: lambda e: e.activation(out=t_.ap, in_=t_.ap, func=AF.Exp, scale=-0.5))(A(cacc[i])), reads=[cacc[i]], writes=[cacc[i]])
            for i, oc in enumerate(ocs):
                stt(C, qk[oc], qk[oc], float(128.0 ** -0.5 if oc < 8 else 1.0), A(cacc[i]), ALU.mult, ALU.mult)
        pb = nf()
        pbv = T(pb.ap[0:64, :], pb.b)
        mm_group(C, pbv, [(w_ba.ap[:, c, :], xb[c].ap) for c in range(NCH)], reads=[w_ba] + xb)
        act(C, tv(betaT, betaT.ap[0:16, :]), tv(pb, pb.ap[0:16, :]), AF.Sigmoid)
        act(C, tv(spt, spt.ap[32:48, :]), tv(pb, pb.ap[32:48, :]), AF.Exp, extra=[adt], bias=adt.ap[32:48, 1:2])
        act(C, tv(spt, spt.ap[32:48, :]), tv(spt, spt.ap[32:48, :]), AF.Ln, bias=1.0)
        ts(C, "dve", tv(gT, gT.ap[32:48, :]), tv(spt, spt.ap[32:48, :]), nea.ap[32:48, 0:1], None, ALU.mult, extra=[nea])
        for n in range(4):
            op(C, "dve", (lambda n: lambda e: e.tensor_tensor_scan(out=gc.ap[32:48, n * 128:(n + 1) * 128], data0=K["onesf"].ap[32:48, :],
                                                                    data1=gT.ap[32:48, n * 128:(n + 1) * 128], initial=0.0,
                                                                    op0=ALU.mult, op1=ALU.add))(n), reads=[gT, K["onesf"]], writes=[gc])
        op(C, "sp", lambda e: e.dma_start(out=gb_scr[0:16, :], in_=betaT.ap[0:16, :]), reads=[betaT], writes=[gbs], dma=True)
        op(C, "sp", lambda e: e.dma_start(out=gb_scr[16:32, :], in_=gc.ap[32:48, :]), reads=[gc], writes=[gbs], dma=True)
        for n in range(4):
            op(C, "sp", (lambda n: lambda e: e.dma_start(out=cols_t[:, n, :], in_=gb_scr[:, n * 128:(n + 1) * 128].rearrange("r p -> p r"),
                                                         allow_slow_non_contiguous=True))(n), reads=[gbs], writes=[cols], dma=True)
        act(C, tv(ecol, ecol_t[:, :, 0:16]), tv(cols, cols_t[:, :, 16:32]), AF.Exp)
        tt(C, "dve", tv(ecol, ecol_t[:, :, 16:32]), tv(ecol, ecol_t[:, :, 0:16]), tv(cols, cols_t[:, :, 0:16]), ALU.mult)
        ts(C, "dve", tv(ecol, ecol_t[:, :, 32:48]), tv(cols, cols_t[:, :, 0:16]), -1.0, None, ALU.mult)
        for h0 in range(0, 16 if STOP > 1 else 0, G):
            hs = list(range(h0, h0 + G))
            for n in range(4 if STOP > 2 else 0):
                cs = slice(n * 128, (n + 1) * 128)
                qcs = [qk[h // 2].ap[:, cs] for h in hs]
                kcs = [qk[8 + h // 2].ap[:, cs] for h in hs]
                qkr = [qk[h // 2] for h in hs] + [qk[8 + h // 2] for h in hs]
                colb = lambda t, ap: T(ap.unsqueeze(2).to_broadcast([128, G, 128]), t.b)
                gcolb = colb(cols, cols_t[:, n, 16 + h0:16 + h0 + G])
                bcolb = colb(cols, cols_t[:, n, h0:h0 + G])
                bgcolb = colb(ecol, ecol_t[:, n, 16 + h0:16 + h0 + G])
                nbcolb = colb(ecol, ecol_t[:, n, 32 + h0:32 + h0 + G])
                op(C, "sp", (lambda h0, n: lambda e: e.dma_start(out=Grow.ap, in_=gb_scr[16 + h0:16 + h0 + G, n * 128:(n + 1) * 128].partition_broadcast(128)))(h0, n),
                   reads=[gbs], writes=[Grow], dma=True)
                op(C, "sp", (lambda h0, n: lambda e: e.dma_start(out=Brow.ap, in_=gb_scr[h0:h0 + G, n * 128:(n + 1) * 128].partition_broadcast(128)))(h0, n),
                   reads=[gbs], writes=[Brow], dma=True)
                Gr = Grow
                Br = Brow
                act(C, L["Eg"], Gr, AF.Exp)
                tt(C, "dve", L["X"], Gr, gcolb, ALU.subtract)
                ts(C, "dve", L["ET"], L["X"], 0.0, None, ALU.min)
                act(C, L["ET"], L["ET"], AF.Exp)
                tt(C, "pool", L["ETm"], L["ET"], bc(K["Ut"], K["Ut"].ap), ALU.mult)
                tt(C, "pool", L["ETn"], L["ET"], bc(Us, Us.ap), ALU.mult)
                tt(C, "pool", L["ETn"], L["ETn"], Br, ALU.mult)
                ts(C, "dve", L["E"], L["X"], 0.0, None, ALU.max)
                act(C, L["E"], L["E"], AF.Exp, scale=-1.0)
                tt(C, "pool", L["E"], L["E"], bc(K["Ls"], K["Ls"].ap), ALU.mult)
                if STOP <= 3:
                    continue
                pk = nq()
                mmG(pk, [(kcs[g], kcs[g]) for g in range(G)], reads=qkr)
                tt(C, "dve", L["X"], pk, L["E"], ALU.mult)
                tt(C, "dve", L["N"], L["X"], nbcolb, ALU.mult)
                stt(C, L["Nt"], pk, -1.0, L["ETn"], ALU.mult, ALU.mult)
                tt(C, "dve", L["Tt"], L["Nt"], bc(K["identf"], K["identf"].ap), ALU.add)
                act(C, L["Tb"], L["Tt"], AF.Copy)
                cP, cQ = "N", "Nt"
                if STOP <= 4:
                    continue
                def SQ(step, cP, cQ):
                    nP, nQ = "P%d" % (step % 2), "Q%d" % (step % 2)
                    pa = nq()
                    mmG(pa, [(L[cQ].ap[:, g, :], L[cP].ap[:, g, :]) for g in range(G)], reads=[L[cQ], L[cP]])
                    act(C, L[nP], pa, AF.Copy)
                    if step < 5:
                        pb2 = nq()
                        mmG(pb2, [(L[cP].ap[:, g, :], L[cQ].ap[:, g, :]) for g in range(G)], reads=[L[cQ], L[cP]])
                        op(C, "dve", (lambda o, i: lambda e: e.tensor_copy(out=o.ap, in_=i.ap))(L[nQ], pb2), reads=[pb2], writes=[L[nQ]])
                    return nP, nQ

                def TU(nP):
                    pc2 = nq()
                    mmG(pc2, [(L[nP].ap[:, g, :], L["Tb"].ap[:, g, :]) for g in range(G)], reads=[L[nP], L["Tb"]])
                    tt(C, "dve", L["Tt"], pc2, L["Tt"], ALU.add)
                    act(C, L["Tb"], L["Tt"], AF.Copy)
                prevP = None
                for step in range(6):
                    nP, nQ = SQ(step, cP, cQ)
                    if prevP is not None:
                        TU(prevP)
                    prevP = nP
                    cP, cQ = nP, nQ
                TU(prevP)
                if STOP <= 5:
                    continue
                pk2 = nq()
                mmG(pk2, [(kcs[g], identb.ap) for g in range(G)], reads=qkr + [identb])
                tt(C, "dve", L["kbg"], pk2, bgcolb, ALU.mult)
                tt(C, "dve", L["kt"], pk2, T(L["ETm"].ap[:, :, 127:128].to_broadcast([128, G, 128]), L["ETm"].b), ALU.mult)
                pv = nq()
                mmG(pv, [(vv[h].ap[:, cs], identb.ap) for h in hs], reads=[vv[h] for h in hs] + [identb])
                tt(C, "dve", L["vb"], pv, bcolb, ALU.mult)
                pw = nq()
                mmG(pw, [(L["kbg"].ap[:, g, :], L["Tb"].ap[:, g, :]) for g in range(G)], reads=[L["kbg"], L["Tb"]])
                act(C, L["wT"], pw, AF.Copy)
                pu = nq()
                mmG(pu, [(L["Tb"].ap[:, g, :], L["vb"].ap[:, g, :]) for g in range(G)], reads=[L["vb"], L["Tb"]])
                act(C, L["u"], pu, AF.Copy)
                pi = nq()
                mmG(pi, [(kcs[g], qcs[g]) for g in range(G)], reads=qkr)
                tt(C, "dve", L["iT"], pi, L["ETm"], ALU.mult)
                for g in range(G):
                    tt(C, "dve", tv(L["qg"], L["qg"].ap[:, g, :]), T(qcs[g], qk[hs[g] // 2].b), tv(L["Eg"], L["Eg"].ap[:, g, :]), ALU.mult)
                if STOP <= 6:
                    continue
                Sg = T(S_t[:, h0:h0 + G, :], Sst[h0].b)
                Sbg = T(Sb_t[:, h0:h0 + G, :], Sbt[h0].b)
                pws = nq()
                mmG(pws, [(L["wT"].ap[:, g, :], Sb_t[:, h0 + g, :]) for g in range(G)], reads=[L["wT"], Sbg])
                tt(C, "dve", L["u"], L["u"], pws, ALU.subtract)
                act(C, L["vnb"], L["u"], AF.Copy)
                po = nq()

                def fn_po(e, po=po, h0=h0):
                    inst = None
                    for g in range(G):
                        e.matmul(po.ap[:, g, :], lhsT=Sb_t[:, h0 + g, :], rhs=L["qg"].ap[:, g, :], start=True, stop=False)
                        inst = e.matmul(po.ap[:, g, :], lhsT=L["vnb"].ap[:, g, :], rhs=L["iT"].ap[:, g, :], start=False, stop=True)
                    return inst
                op(C, "pe", fn_po, reads=[Sbg, L["qg"], L["vnb"], L["iT"]], writes=[po])
                act(C, L["oT"], po, AF.Copy)
                act(C, L["sq2"], po, AF.Square)
                pss = nq()
                mmG(pss, [(L["kt"].ap[:, g, :], L["vnb"].ap[:, g, :]) for g in range(G)], reads=[L["kt"], L["vnb"]])
                tt(C, "dve", Sg, Sg, T(L["Eg"].ap[:, :, 127:128].to_broadcast([128, G, 128]), L["Eg"].b), ALU.mult)
                tt(C, "dve", Sg, Sg, pss, ALU.add)
                act(C, Sbg, Sg, AF.Copy)
                pm = nq()
                op(C, "pe", (lambda pm: lambda e: e.matmul(pm.ap.rearrange("p g f -> p (g f)"), lhsT=K["ones128"].ap, rhs=L["sq2"].ap.rearrange("p g f -> p (g f)"),
                                                           start=True, stop=True))(pm), reads=[K["ones128"], L["sq2"]], writes=[pm])
                ts(C, "dve", L["rs"], pm, float(RMS_EPS), None, ALU.add)
                rsqrt_act(C, L["rs"])
                stt(C, L["oT"], L["oT"], ng.ap[:, 0:1], L["rs"], ALU.mult, ALU.mult, extra=[ng])
                ogc = T(og_t[:, h0:h0 + G, cs], og[h0].b)
                tt(C, "dve", ogc, L["oT"], ogc, ALU.mult)
        for j in range(NCH):
            py = ps[j % 2]
            wo = wout[wn["o"] % 2]
            wn["o"] += 1
            op(C, "pool", (lambda wo, j: lambda e: e.dma_start(out=wo.ap[:], in_=w_out_v[:, :, j * 128:(j + 1) * 128]))(wo, j), writes=[wo], dma=True)
            mm_group(C, py, [(wo.ap[:, hh, :], og[hh].ap) for hh in range(16)], reads=[wo] + og)
            M.resid(t, j, py)
        M.finish(t)


def gdn_host(inputs, ph):
    d = {}
    w = inputs["gdn_w_in"][0]
    d["gdn_w_in"] = np.ascontiguousarray(w[:, :6144])
    ba = np.zeros((D_MODEL, 64), np.float32)
    ba[:, 0:16] = w[:, 6144:6160]
    ba[:, 32:48] = w[:, 6160:6176]
    d["gdn_w_ba"] = ba
    d["gdn_w_out"] = np.ascontiguousarray(inputs["gdn_w_out"][0])
    d["gdn_wconv"] = np.ascontiguousarray(inputs["gdn_w_conv"][0].reshape(4, 32, 128).transpose(2, 0, 1).reshape(128, 128))
    d["gdn_wconv2"] = np.ascontiguousarray(inputs["gdn_w_conv"][0].reshape(4, 32, 128).transpose(2, 1, 0).reshape(128, 128))
    adt = np.zeros((128, 2), np.float32)
    adt[32:48, 0] = inputs["gdn_a_log"][0]
    adt[32:48, 1] = inputs["gdn_dt_bias"][0]
    d["gdn_adt"] = adt
    d["gdn_ng"] = np.ascontiguousarray(inputs["gdn_norm_g"][0].reshape(128, 1))
    return d


MIXERS["gdn"] = gdn_phase
MIXER_HOST["gdn"] = gdn_host


TOPK = 256
NIT = 15
ROPE_THETA = 500000.0


def dsa_consts():
    inv = np.zeros((128, 1), np.float32)
    rot = np.zeros((128, 128), np.float32)
    freqs = (ROPE_THETA ** (-np.arange(0, 16, 2, dtype=np.float32) / 16.0)).astype(np.float32)
    for p in range(128):
        r = p % 64
        if r < 8:
            inv[p, 0] = freqs[r]
            rot[p + 8, p] = -1.0
        elif r < 16:
            inv[p, 0] = freqs[r - 8]
            rot[p - 8, p] = 1.0
    pw = np.tile((0.5 ** np.arange(1, NIT + 2, dtype=np.float32))[None, :], (128, 1)).astype(np.float32)
    return inv, rot, pw


def dsa_phase(C, xsrc, xdst, ph):
    import os
    import math
    STOP = int(os.environ.get("DSA_STOP", "99"))
    nc = C.nc
    DBG = int(os.environ.get("DSA_DBG", "-1"))
    if DBG >= 0:
        dbg0 = nc.dram_tensor("dbg0", [128, SEQ], F32, kind="ExternalOutput").ap()
        dbg1 = nc.dram_tensor("dbg1", [128, SEQ], BF16, kind="ExternalOutput").ap()
        dbg2 = nc.dram_tensor("dbg2", [128, 8], F32, kind="ExternalOutput").ap()
    K = C.K
    M = MixBase(C, xsrc, xdst, 2 * 3 + 1, nbuf=1, alloc_z=False, alloc_tmp=False)
    dt_in = lambda n, shp: nc.dram_tensor(n, shp, F32, kind="ExternalInput").ap()
    w_dq = load_w_resident(C, dt_in("dsa_w_dq", [D_MODEL, 256]), NCH, 256, "wdq")
    w_uq = load_w_resident(C, dt_in("dsa_w_uq", [256, 1024]), 2, 1024, "wuq")
    w_iq = load_w_resident(C, dt_in("dsa_w_iq", [256, 512]), 2, 512, "wiq")
    w_dkv = load_w_resident(C, dt_in("dsa_w_dkv", [D_MODEL, 128]), NCH, 128, "wdkv")
    w_kr = load_w_resident(C, dt_in("dsa_w_kr_pad", [D_MODEL, 128]), NCH, 128, "wkr")
    w_ik = load_w_resident(C, dt_in("dsa_w_ik_pad", [D_MODEL, 128]), NCH, 128, "wik")
    w_iw = load_w_resident(C, dt_in("dsa_w_iw", [D_MODEL, 8]), NCH, 8, "wiw")
    w_uk = load_w_resident(C, dt_in("dsa_w_uk_pad", [16 * 128, 128]), 16, 128, "wuk")
    w_uv = load_w_resident(C, dt_in("dsa_w_uv", [16 * 128, 64]), 16, 64, "wuv")
    w_o_d = dt_in("dsa_w_o", [D_MODEL, D_MODEL]).rearrange("(h p) n -> p h n", p=128)
    qng = load_cols(C, dt_in("dsa_qng", [128, 2]), 2, "qng")
    kvg = load_cols(C, dt_in("dsa_kvg", [128, 1]), 1, "kvg")
    ikg = load_cols(C, dt_in("dsa_ikgb", [128, 2]), 2, "ikgb")
    invf = load_cols(C, dt_in("dsa_inv", [128, 1]), 1, "invf")
    pw2 = load_cols(C, dt_in("dsa_pw2", [128, NIT + 1]), NIT + 1, "pw2")
    rot = load_w_resident(C, dt_in("dsa_rot", [128, 128]), 1, 128, "rot")
    pos_d = nc.dram_tensor("dsa_pos", [1, SEQ], I32, kind="ExternalInput").ap()
    o256 = T(C.sb([128, 128], BF16, "ones256"))
    op(C, "dve", lambda e: e.memset(o256.ap, 1.0 / 256.0), writes=[o256])
    cmask = T(C.sb([128, 128], F32, "cmask"))
    zer = T(C.sb([128, 128], F32, "zer"))
    op(C, "pool", lambda e: e.memset(zer.ap, 0.0), writes=[zer])
    op(C, "pool", lambda e: e.affine_select(out=cmask.ap, in_=zer.ap, pattern=[[-1, 128]], compare_op=ALU.is_ge, fill=-1.0e30,
                                            base=0, channel_multiplier=1), reads=[zer], writes=[cmask])
    ckvT_t = C.sb([128, SEQ], BF16, "ckvT")
    ckvT = T(ckvT_t)
    ckvk_t = C.sb([128, SEQ // 128, 128], BF16, "ckvtok")
    ckvk = T(ckvk_t)
    krA_t = C.sb([128, SEQ], BF16, "krA")
    krB_t = C.sb([128, SEQ], BF16, "krB")
    kiA_t = C.sb([128, SEQ], BF16, "kiA")
    kiB_t = C.sb([128, SEQ], BF16, "kiB")
    krA, krB, kiA, kiB = T(krA_t), T(krB_t), T(kiA_t), T(kiB_t)
    for tl in (krA, krB, kiA, kiB):
        op(C, "pool", (lambda tl: lambda e: e.memset(tl.ap, 0.0))(tl), writes=[tl])
    selT_t = C.sb([128, 32, 256], BF16, "selT")
    selT = T(selT_t)
    acc = T(C.sb([128, SEQ], F32, "acc"))
    sel = T(C.sb([128, SEQ], BF16, "sel"))
    qr_t = C.sb([128, NCH, TT], BF16, "qr")
    qr = [T(qr_t[:, j, :]) for j in range(NCH)]
    qi_t = C.sb([128, 4, TT], BF16, "qi")
    qi = [T(qi_t[:, j, :]) for j in range(4)]
    cq_t = C.sb([128, 2, TT], BF16, "cq")
    cq = [T(cq_t[:, j, :]) for j in range(2)]
    cqs = [T(C.sb([128, TT], BF16, "cqs")) for j in range(2)]
    wi4 = T(C.sb([128, 4, 8], F32, "wi4"))
    cosE = T(C.sb([128, TT], F32, "cosE"))
    sinE = T(C.sb([128, TT], F32, "sinE"))
    posi = T(C.sb([128, TT], I32, "posi"))
    ang = T(C.sb([128, TT], F32, "ang"))
    ti = posi
    rf = [T(C.sb([128, TT], F32, "rf")) for i in range(2)]
    rb = [T(C.sb([128, TT], BF16, "rb")) for i in range(2)]
    r1 = [T(C.sb([128, TT], F32, "r1")) for i in range(2)]
    tq = r1[0]
    tf = rf[0]
    cqf = rf
    relu1 = T(C.sb([128, 2, TT], F32, "relu"))
    relu_t = [relu1, relu1]
    M.tmp = {"m2": rf[0], "mean": rf[1], "vare": r1[0], "rstd": r1[1], "zc0": T(relu1.ap[:, 0, :], relu1.b), "zc1": T(relu1.ap[:, 1, :], relu1.b)}
    kdst = T(C.sb([128, TT], BF16, "kdst"))
    M.zq = qr
    M.zb = [T(sel.ap[:, j * TT:(j + 1) * TT], sel.b) for j in range(NCH)]
    qlat = [T(C.sb([128, TT], BF16, "qlat")) for i in range(2)]
    PT = [T(C.sb([128, 4, 256], BF16, "PT")) for i in range(2)]
    olat2 = [T(C.sb([128, 256], BF16, "olat")) for i in range(2)]
    oT_t = C.sb([128, NCH, TT], BF16, "oTd")
    oT = [T(oT_t[:, j, :]) for j in range(NCH)]
    rden = T(C.sb([128, 256], F32, "rden"))
    wo_s = [T(C.sb([128, NCH, 128], BF16, "wo")) for i in range(4)]
    sm = {k: T(C.sb([128, 1], F32, k)) for k in ("lo", "hi", "rng", "mid", "cnt", "ge", "nmid", "sg")}
    selbufA, selbufB = Buf(), Buf()
    hwt = T(C.sb([128, NIT + 1], F32, "hwt"))
    ps = M.ps
    pobuf = [Buf() for i in range(4)]
    pctr = {"f": 0, "s": 0, "o": 0, "w": 0}

    def nf():
        pctr["f"] += 1
        return ps[pctr["f"] % 2]

    def pair_bank(i):
        b = 2 + 2 * (i % 2)
        return b

    def rope(src_ps, dst):
        i = pctr["s"] % 2
        pctr["s"] += 1
        act(C, rf[i], src_ps, AF.Copy)
        act(C, rb[i], src_ps, AF.Copy)
        pr = nf()
        mm1(C, pr, tv(rot, rot.ap[:, 0, :]), rb[i])
        tt(C, "dve", r1[i], rf[i], cosE, ALU.mult)
        tt(C, "dve", rf[i], pr, sinE, ALU.mult)
        tt(C, "dve", dst, r1[i], rf[i], ALU.add)

    def rope_sb(src_f32, dst):
        i = pctr["s"] % 2
        pctr["s"] += 1
        act(C, rb[i], src_f32, AF.Copy)
        pr = nf()
        mm1(C, pr, tv(rot, rot.ap[:, 0, :]), rb[i])
        tt(C, "dve", r1[i], src_f32, cosE, ALU.mult)
        tt(C, "dve", rf[i], pr, sinE, ALU.mult)
        tt(C, "dve", dst, r1[i], rf[i], ALU.add)

    TWO_PI = 2.0 * math.pi
    for t in range(NTT):
        c0 = t * TT
        M.load(t)
        xb = M.xb[0]
        op(C, "sp", lambda e, c0=c0: e.dma_start(out=posi.ap, in_=pos_d[0:1, c0:c0 + TT].partition_broadcast(128)), writes=[posi], dma=True)
        op(C, "dve", lambda e: e.tensor_copy(out=ang.ap, in_=posi.ap), reads=[posi], writes=[ang])
        ts(C, "dve", ang, ang, invf.ap[:, 0:1], None, ALU.mult, extra=[invf])
        for (dst, off) in ((sinE, 0.5), (cosE, 0.75)):
            ts(C, "dve", tq, ang, float(1.0 / TWO_PI), float(off), ALU.mult, ALU.add)
            op(C, "dve", lambda e: e.tensor_copy(out=ti.ap, in_=tq.ap), reads=[tq], writes=[ti])
            op(C, "dve", lambda e: e.tensor_copy(out=tf.ap, in_=ti.ap), reads=[ti], writes=[tf])
            tt(C, "dve", tq, tq, tf, ALU.subtract)
            stt(C, tq, tq, 0.0, tq, ALU.is_lt, ALU.add)
            ts(C, "dve", tq, tq, 1.0, None, ALU.min)
            ts(C, "dve", tq, tq, float(TWO_PI), float(-math.pi), ALU.mult, ALU.add)
            ts(C, "dve", tq, tq, float(math.pi), float(-math.pi), ALU.min, ALU.max)
            act(C, dst, tq, AF.Sin)
        pk = nf()
        mm_group(C, pk, [(w_dkv.ap[:, c, :], xb[c].ap) for c in range(NCH)], reads=[w_dkv] + xb)
        act(C, rf[0], pk, AF.Copy)
        act(C, rb[0], pk, AF.Square)
        pq_ = nf()
        mm1(C, pq_, K["ones128"], rb[0])
        ts(C, "dve", r1[0], pq_, float(RMS_EPS), None, ALU.add)
        rsqrt_act(C, r1[0])
        stt(C, tv(ckvT, ckvT_t[:, c0:c0 + TT]), rf[0], kvg.ap[:, 0:1], r1[0], ALU.mult, ALU.mult, extra=[kvg])
        pt4 = nf()
        pt4v = T(pt4.ap.rearrange("p (g f) -> p g f", g=4), pt4.b)

        def fn_tr(e, c0=c0, pt4v=pt4v):
            inst = None
            for g in range(4):
                inst = e.matmul(pt4v.ap[:, g, :], lhsT=ckvT_t[:, c0 + g * 128:c0 + (g + 1) * 128], rhs=K["identb"].ap, start=True, stop=True)
            return inst
        op(C, "pe", fn_tr, reads=[ckvT, K["identb"]], writes=[pt4])
        act(C, tv(ckvk, ckvk_t[:, 4 * t:4 * t + 4, :]), pt4v, AF.Copy)
        pkr = nf()
        mm_group(C, pkr, [(w_kr.ap[:, c, :], xb[c].ap) for c in range(NCH)], reads=[w_kr] + xb)
        rope(pkr, kdst)
        op(C, "pool", lambda e, c0=c0: e.tensor_copy(out=krA_t[0:16, c0:c0 + TT], in_=kdst.ap[0:16, :]), reads=[kdst], writes=[krA])
        op(C, "pool", lambda e, c0=c0: e.tensor_copy(out=krB_t[64:80, c0:c0 + TT], in_=kdst.ap[64:80, :]), reads=[kdst], writes=[krB])
        pik = nf()
        mm_group(C, pik, [(w_ik.ap[:, c, :], xb[c].ap) for c in range(NCH)], reads=[w_ik] + xb)
        act(C, rf[0], pik, AF.Copy)
        act(C, rb[0], pik, AF.Copy)
        act(C, rb[1], pik, AF.Square)
        pm_ = nf()
        mm1(C, pm_, K["ones128"], rb[0])
        pq2 = nf()
        mm1(C, pq2, K["ones128"], rb[1])
        act(C, r1[0], pm_, AF.Square)
        act(C, ang, pm_, AF.Copy)
        stt(C, r1[0], pq2, float(LN_EPS), r1[0], ALU.add, ALU.subtract)
        rsqrt_act(C, r1[0])
        tt(C, "dve", rf[0], rf[0], ang, ALU.subtract)
        tt(C, "dve", rf[0], rf[0], r1[0], ALU.mult)
        act(C, ang, rf[0], AF.Identity, extra=[ikg], scale=ikg.ap[:, 0:1], bias=ikg.ap[:, 1:2])
        rope_sb(ang, kdst)
        op(C, "pool", lambda e, c0=c0: e.tensor_copy(out=kiA_t[0:64, c0:c0 + TT], in_=kdst.ap[0:64, :]), reads=[kdst], writes=[kiA])
        op(C, "pool", lambda e, c0=c0: e.tensor_copy(out=kiB_t[64:128, c0:c0 + TT], in_=kdst.ap[64:128, :]), reads=[kdst], writes=[kiB])
        for j in range(2):
            pc_ = nf()
            mm_group(C, pc_, [(w_dq.ap[:, c, j * 128:(j + 1) * 128], xb[c].ap) for c in range(NCH)], reads=[w_dq] + xb)
            act(C, cqf[j], pc_, AF.Copy)
            act(C, cqs[j], pc_, AF.Square)
        pq3 = nf()
        mm_group(C, pq3, [(o256.ap, cqs[j].ap) for j in range(2)], reads=[o256] + cqs)
        ts(C, "dve", r1[0], pq3, float(RMS_EPS), None, ALU.add)
        rsqrt_act(C, r1[0])
        for j in range(2):
            stt(C, cq[j], cqf[j], qng.ap[:, j:j + 1], r1[0], ALU.mult, ALU.mult, extra=[qng])
        for j in range(NCH):
            pj = nf()
            mm_group(C, pj, [(w_uq.ap[:, c, j * 128:(j + 1) * 128], cq[c].ap) for c in range(2)], reads=[w_uq] + cq)
            rope(pj, qr[j])
        for j in range(4):
            pj = nf()
            mm_group(C, pj, [(w_iq.ap[:, c, j * 128:(j + 1) * 128], cq[c].ap) for c in range(2)], reads=[w_iq] + cq)
            rope(pj, qi[j])
        pw_ = nf()
        pwv = T(pw_.ap[:, 0:32].rearrange("p (g f) -> p g f", g=4), pw_.b)

        def fn_wi(e, pwv=pwv, xb=xb):
            inst = None
            for qs in range(4):
                for c in range(NCH):
                    inst = e.matmul(pwv.ap[:, qs, :], lhsT=xb[c].ap[:, qs * 128:(qs + 1) * 128], rhs=w_iw.ap[:, c, :], start=(c == 0), stop=(c == NCH - 1))
            return inst
        op(C, "pe", fn_wi, reads=[w_iw] + xb, writes=[pw_])
        ts(C, "dve", wi4, pwv, float(512.0 ** -0.5), None, ALU.mult)
        if STOP <= 1:
            continue
        for qs in range(4):
            qt = 4 * t + qs
            Lk = (qt + 1) * 128
            q0 = qs * 128
            nblk = (Lk + TT - 1) // TT
            for blk in range(nblk):
                b0 = blk * TT
                w = min(TT, Lk - b0)
                for hp in range(4):
                    i = pctr["o"] % 2
                    pctr["o"] += 1
                    bk = 2 + 2 * i
                    pA, pB = ps[bk], ps[bk + 1]
                    mm1(C, tv(pA, pA.ap[:, 0:w]), tv(qi[hp], qi[hp].ap[:, q0:q0 + 128]), tv(kiA, kiA_t[:, b0:b0 + w]))
                    mm1(C, tv(pB, pB.ap[:, 0:w]), tv(qi[hp], qi[hp].ap[:, q0:q0 + 128]), tv(kiB, kiB_t[:, b0:b0 + w]))
                    rl = relu_t[i]
                    act(C, tv(rl, rl.ap[:, 0, 0:w]), tv(pA, pA.ap[:, 0:w]), AF.Relu)
                    act(C, tv(rl, rl.ap[:, 1, 0:w]), tv(pB, pB.ap[:, 0:w]), AF.Relu)
                    accv = tv(acc, acc.ap[:, b0:b0 + w])
                    for ab in range(2):
                        h = 2 * hp + ab
                        wcol = wi4.ap[:, qs, h:h + 1]
                        if h == 0:
                            ts(C, "dve", accv, tv(rl, rl.ap[:, ab, 0:w]), wcol, None, ALU.mult, extra=[wi4])
                        else:
                            stt(C, accv, tv(rl, rl.ap[:, ab, 0:w]), wcol, accv, ALU.mult, ALU.add, extra=[wi4])
            accL = tv(acc, acc.ap[:, 0:Lk])
            selL = tv(sel, sel.ap[:, 0:Lk])
            dg = tv(acc, acc.ap[:, qt * 128:(qt + 1) * 128])
            if qt >= 2:
                op(C, "dve", lambda e, accL=accL: e.tensor_reduce(out=sm["lo"].ap, in_=accL.ap, axis=AX.X, op=ALU.min), reads=[accL], writes=[sm["lo"]])
            tt(C, "dve", dg, dg, cmask, ALU.add)
            if qt >= 2:
                op(C, "dve", lambda e, accL=accL: e.tensor_reduce(out=sm["hi"].ap, in_=accL.ap, axis=AX.X, op=ALU.max), reads=[accL], writes=[sm["hi"]])
                tt(C, "dve", sm["rng"], sm["hi"], sm["lo"], ALU.subtract)
                ts(C, "dve", hwt, pw2, sm["rng"].ap[:, 0:1], None, ALU.mult, extra=[sm["rng"]])
                tt(C, "dve", sm["mid"], sm["lo"], tv(hwt, hwt.ap[:, 0:1]), ALU.add)
                Lh = ((Lk // 2 + 127) // 128) * 128
                nB = Lk - Lh
                thr = float(TOPK) - 0.5 * nB
                accA, accB = T(acc.ap[:, 0:Lh], acc.b), T(acc.ap[:, Lh:Lk], acc.b)
                jA, jB = T(sel.ap[:, 0:Lh], selbufA), T(sel.ap[:, Lh:Lk], selbufB)
                stt(C, sm["nmid"], sm["lo"], -1.0, tv(hwt, hwt.ap[:, 0:1]), ALU.mult, ALU.subtract, extra=[hwt])
                for it in range(NIT):
                    if it == 0:
                        op(C, "dve", lambda e: e.tensor_copy(out=sm["ge"].ap, in_=sm["ge"].ap), reads=[sm["ge"]], writes=[sm["ge"], sel, jA, jB])
                    ts(C, "dve", jA, accA, sm["mid"].ap[:, 0:1], None, ALU.is_ge, ALU.add, extra=[sm["mid"]], accum=sm["cnt"])
                    op(C, "act", lambda e, jB=jB, accB=accB: e.activation(out=jB.ap, in_=accB.ap, func=AF.Sign, bias=sm["nmid"].ap[:, 0:1], scale=1.0,
                                                                      accum_out=sm["sg"].ap), reads=[accB, sm["nmid"]], writes=[jB, sm["sg"]])
                    stt(C, sm["cnt"], sm["sg"], 0.5, sm["cnt"], ALU.mult, ALU.add)
                    ts(C, "dve", sm["ge"], sm["cnt"], thr, None, ALU.is_ge)
                    stt(C, sm["lo"], sm["ge"], hwt.ap[:, it:it + 1], sm["lo"], ALU.mult, ALU.add, extra=[hwt])
                    tt(C, "dve", sm["mid"], sm["lo"], tv(hwt, hwt.ap[:, it + 1:it + 2]), ALU.add)
                    stt(C, sm["nmid"], sm["lo"], -1.0, tv(hwt, hwt.ap[:, it + 1:it + 2]), ALU.mult, ALU.subtract, extra=[hwt])
                op(C, "dve", lambda e, selL=selL, accL=accL: e.tensor_scalar(out=selL.ap, in0=accL.ap, scalar1=sm["lo"].ap[:, 0:1], scalar2=None, op0=ALU.is_ge),
                   reads=[accL, sm["lo"], jA, jB], writes=[selL, jA, jB])
            else:
                ts(C, "dve", selL, accL, -1.0e29, None, ALU.is_ge)
            if DBG == qt:
                op(C, "sp", lambda e: e.dma_start(out=dbg0, in_=acc.ap), reads=[acc], dma=True)
                op(C, "sp", lambda e: e.dma_start(out=dbg1, in_=sel.ap), reads=[sel], dma=True)
                op(C, "sp", lambda e: e.dma_start(allow_slow_non_contiguous=True, out=dbg2[:, 0:1], in_=sm["lo"].ap), reads=[sm["lo"]], dma=True)
                op(C, "sp", lambda e: e.dma_start(allow_slow_non_contiguous=True, out=dbg2[:, 1:2], in_=sm["hi"].ap), reads=[sm["hi"]], dma=True)
                op(C, "sp", lambda e: e.dma_start(allow_slow_non_contiguous=True, out=dbg2[:, 2:3], in_=sm["cnt"].ap), reads=[sm["cnt"]], dma=True)
            half = qs // 2
            qc0 = (qs % 2) * 128
            nkt_tile = 4 * t + 2 * half + 2
            for kb0 in range(0, nkt_tile, 4):
                kbs = list(range(kb0, min(kb0 + 4, nkt_tile)))
                valid = [kb for kb in kbs if kb <= qt]
                pz = nf()
                pzv = T(pz.ap.rearrange("p (g f) -> p g f", g=4), pz.b)
                dstv = T(selT_t[:, kb0:kb0 + len(kbs), qc0:qc0 + 128], selT.b)
                if valid:
                    def fn_st(e, valid=valid, kb0=kb0, pzv=pzv):
                        inst = None
                        for kb in valid:
                            inst = e.matmul(pzv.ap[:, kb - kb0, :], lhsT=sel.ap[:, kb * 128:(kb + 1) * 128], rhs=K["identb"].ap, start=True, stop=True)
                        return inst
                    op(C, "pe", fn_st, reads=[sel, K["identb"], T(sel.ap, selbufA), T(sel.ap, selbufB)], writes=[pz])
                    act(C, T(selT_t[:, kb0:kb0 + len(valid), qc0:qc0 + 128], selT.b), T(pzv.ap[:, 0:len(valid), :], pz.b), AF.Copy)
                if len(valid) < len(kbs):
                    nz = len(kbs) - len(valid)
                    op(C, "pool", lambda e, kb0=kb0, nv=len(valid), nz=nz, qc0=qc0: e.memset(selT_t[:, kb0 + nv:kb0 + nv + nz, qc0:qc0 + 128], 0.0), writes=[selT])
            if STOP <= 2 or qs % 2 == 0:
                continue
            hq0 = half * 256
            nkt = nkt_tile
            items = []
            for h in range(16):
                ngrp = (nkt + 3) // 4
                for gi in range(ngrp):
                    items.append((h, gi, list(range(gi * 4, min(gi * 4 + 4, nkt))), gi == ngrp - 1))

            def prologue(h):
                hp = h // 2
                ql = qlat[h % 2]
                pl = ps[0]
                mm1(C, tv(pl, pl.ap[:, 0:256]), tv(w_uk, w_uk.ap[:, h, :]), tv(qr[hp], qr[hp].ap[:, hq0:hq0 + 256]))
                act(C, tv(ql, ql.ap[:, 0:256]), tv(pl, pl.ap[:, 0:256]), AF.Copy)

            def emitS(item, i):
                h, gi, kts, last = item
                hp, ab = h // 2, h % 2
                ql = qlat[h % 2]
                kr_t = krA_t if ab == 0 else krB_t
                kr_T = krA if ab == 0 else krB
                bk = 2 + 2 * i
                pst = T(C.psum[bk][:, :], C.pbuf[bk])
                pst2 = T(C.psum[bk + 1][:, :], C.pbuf[bk + 1])

                def fn_s(e, kts=kts, bk=bk, ql=ql, kr_t=kr_t, hp=hp, hq0=hq0):
                    inst = None
                    for n_, kt in enumerate(kts):
                        dst = C.psum[bk + n_ // 2][:, (n_ % 2) * 256:(n_ % 2) * 256 + 256]
                        e.matmul(dst, lhsT=ckvT_t[:, kt * 128:(kt + 1) * 128], rhs=ql.ap[:, 0:256], start=True, stop=False)
                        inst = e.matmul(dst, lhsT=kr_t[:, kt * 128:(kt + 1) * 128], rhs=qr_t[:, hp, hq0:hq0 + 256], start=False, stop=True)
                    return inst
                op(C, "pe", fn_s, reads=[ckvT, ql, kr_T, qr[hp]], writes=[pst, pst2])

            def emitEM(item, i):
                h, gi, kts, last = item
                bk = 2 + 2 * i
                pt_ = PT[i]
                n2 = len(kts)
                act(C, T(pt_.ap[:, 0:min(n2, 2), :], pt_.b), T(C.psum[bk][:, 0:min(n2, 2) * 256].rearrange("p (g f) -> p g f", f=256), C.pbuf[bk]), AF.Exp, scale=0.125)
                if n2 > 2:
                    act(C, T(pt_.ap[:, 2:n2, :], pt_.b), T(C.psum[bk + 1][:, 0:(n2 - 2) * 256].rearrange("p (g f) -> p g f", f=256), C.pbuf[bk + 1]), AF.Exp, scale=0.125)
                ptv = T(pt_.ap[:, 0:n2, :], pt_.b)
                tt(C, "dve", ptv, ptv, T(selT_t[:, kts[0]:kts[0] + n2, :], selT.b), ALU.mult)

            def emitPV(item, i):
                h, gi, kts, last = item
                hp, ab = h // 2, h % 2
                pt_ = PT[i]
                hc = (h % 2) * 256
                pov = T(C.psum[6][:, hc:hc + 256], pobuf[h % 2])
                pdv = T(C.psum[7][:, hc:hc + 256], pobuf[2 + h % 2])

                def fn_pv(e, kts=kts, pt_=pt_, pov=pov, pdv=pdv, nkt=nkt):
                    inst = None
                    for n_, kt in enumerate(kts):
                        e.matmul(pov.ap, lhsT=ckvk_t[:, kt, :], rhs=pt_.ap[:, n_, :], start=(kt == 0), stop=(kt == nkt - 1))
                    for n_, kt in enumerate(kts):
                        inst = e.matmul(pdv.ap, lhsT=K["ones1"].ap, rhs=pt_.ap[:, n_, :], start=(kt == 0), stop=(kt == nkt - 1))
                    return inst
                op(C, "pe", fn_pv, reads=[ckvk, pt_, K["ones1"]], writes=[pov, pdv])
                if last:
                    op(C, "dve", lambda e, pdv=pdv: e.reciprocal(out=rden.ap, in_=pdv.ap), reads=[pdv], writes=[rden])
                    ol = olat2[h % 2]
                    tt(C, "dve", ol, pov, rden, ALU.mult)
                    pv_ = ps[1]
                    mm1(C, T(pv_.ap[ab * 64:(ab + 1) * 64, 0:256], pv_.b), tv(w_uv, w_uv.ap[:, h, :]), ol)
                    if ab == 1:
                        act(C, tv(oT[hp], oT[hp].ap[:, hq0:hq0 + 256]), tv(pv_, pv_.ap[:, 0:256]), AF.Copy)

            prologue(items[0][0])
            par0 = pctr["o"] % 2
            emitS(items[0], par0)
            for n_it, item in enumerate(items):
                i = (par0 + n_it) % 2
                if n_it + 1 < len(items):
                    nxt = items[n_it + 1]
                    if nxt[0] != item[0]:
                        prologue(nxt[0])
                    emitS(nxt, 1 - i)
                emitEM(item, i)
                emitPV(item, i)
            pctr["o"] += len(items)
        if DBG == 100 + t:
            op(C, "sp", lambda e: e.dma_start(out=dbg1.rearrange("p (c t) -> p c t", c=NCH), in_=oT_t[:]), reads=oT, dma=True)
        if STOP <= 3:
            continue
        for j in range(NCH):
            py = ps[j % 2]
            wo = wo_s[pctr["w"] % 4]
            pctr["w"] += 1
            op(C, "pool", (lambda wo, j: lambda e: e.dma_start(out=wo.ap, in_=w_o_d[:, :, j * 128:(j + 1) * 128]))(wo, j), writes=[wo], dma=True)
            mm_group(C, py, [(wo.ap[:, hh, :], oT[hh].ap) for hh in range(NCH)], reads=[wo] + oT)
            M.resid(t, j, py)
        M.finish(t)


def dsa_host(inputs, ph):
    d = {}
    g = lambda k: np.ascontiguousarray(inputs[k][0])
    d["dsa_w_dq"] = g("dsa_w_dq")
    d["dsa_w_uq"] = g("dsa_w_uq")
    d["dsa_w_iq"] = g("dsa_w_iq")
    d["dsa_w_dkv"] = g("dsa_w_dkv")
    kr = np.zeros((D_MODEL, 128), np.float32)
    kr[:, 0:16] = inputs["dsa_w_kr"][0]
    kr[:, 64:80] = inputs["dsa_w_kr"][0]
    d["dsa_w_kr_pad"] = kr
    ik = np.zeros((D_MODEL, 128), np.float32)
    ik[:, 0:64] = inputs["dsa_w_ik"][0]
    ik[:, 64:128] = inputs["dsa_w_ik"][0]
    d["dsa_w_ik_pad"] = ik
    d["dsa_w_iw"] = g("dsa_w_iw")
    uk = np.zeros((16, 128, 128), np.float32)
    for h in range(16):
        base = (h % 2) * 64
        uk[h, base + 16:base + 64, :] = inputs["dsa_w_uk"][0][h]
    d["dsa_w_uk_pad"] = uk.reshape(16 * 128, 128)
    d["dsa_w_uv"] = np.ascontiguousarray(inputs["dsa_w_uv"][0].reshape(16 * 128, 64))
    d["dsa_w_o"] = g("dsa_w_o")
    d["dsa_qng"] = np.ascontiguousarray(inputs["dsa_q_norm"][0].reshape(2, 128).T)
    d["dsa_kvg"] = np.ascontiguousarray(inputs["dsa_kv_norm"][0].reshape(128, 1))
    gb = np.zeros((128, 2), np.float32)
    gb[:, 0] = np.tile(inputs["dsa_ik_ln_g"][0], 2)
    gb[:, 1] = np.tile(inputs["dsa_ik_ln_b"][0], 2)
    d["dsa_ikgb"] = gb
    inv, rot, pw = dsa_consts()
    d["dsa_inv"] = inv
    d["dsa_rot"] = rot
    d["dsa_pw2"] = pw[:, :NIT + 1]
    return d


MIXERS["dsa"] = dsa_phase
MIXER_HOST["dsa"] = dsa_host


MIX_OF_LAYER = ("conv", "sconv", "dsa", "gdn")
FUSED = True


def full_plan():
    plan = []
    for i in range(DEPTH):
        plan += [("ffn", i, 0), (MIX_OF_LAYER[i % 4],), ("ffn", i, 1)]
    return plan


def _run(plan, inputs, xT):
    nc = build_program(plan)
    hw = host_inputs_for(plan, inputs)
    has_dsa = any(p[0] == "dsa" for p in plan)
    pos = np.ascontiguousarray(inputs["positions"]).astype(np.int32)
    in_maps = []
    for b in range(BATCH):
        m = dict(hw)
        m["xin"] = xT[b]
        if has_dsa:
            m["dsa_pos"] = np.ascontiguousarray(pos[b:b + 1])
        in_maps.append(m)
    res = run_bass_kernel_spmd(nc, in_maps, core_ids=list(range(BATCH)))
    return [np.asarray(res.results[b]["xout"]) for b in range(BATCH)]


def kernel(**inputs):
    inputs = {k: np.asarray(v) for k, v in inputs.items()}
    x = inputs["x"].astype(np.float32)
    xT = [np.ascontiguousarray(x[b].T) for b in range(BATCH)]
    plan = full_plan()
    if FUSED:
        xT = _run(plan, inputs, xT)
    else:
        for ph in plan:
            xT = _run([ph], inputs, xT)
    return np.ascontiguousarray(np.stack([xT[b].T for b in range(BATCH)]).astype(np.float32))
```

```python
import numpy as np
import concourse.bass as bass
import concourse.mybir as mybir
from concourse.bass_utils import run_bass_kernel_spmd

F32 = mybir.dt.float32
BF16 = mybir.dt.bfloat16
I32 = mybir.dt.int32
ALU = mybir.AluOpType
AF = mybir.ActivationFunctionType
AX = mybir.AxisListType

D_MODEL = 1024
SEQ = 4096
BATCH = 8
DEPTH = 4
FFN_DIM = 2816
NCH = D_MODEL // 128
NFT = FFN_DIM // 128
DN_ALPHA = (2.0 * DEPTH) ** 0.25
LN_EPS = 1e-5
RMS_EPS = 1e-6
TT = 512
NTT = SEQ // TT

EPOCH = 20000


class Buf:
    __slots__ = ("name", "w", "r")

    def __init__(self, name=""):
        self.name = name
        self.w = None
        self.r = []


class Op:
    __slots__ = ("eng", "idx", "fn", "dma", "deps", "signal", "sig_no", "dsem", "dval")

    def __init__(self, eng, idx, fn, dma):
        self.eng = eng
        self.idx = idx
        self.fn = fn
        self.dma = dma
        self.deps = []
        self.signal = False
        self.sig_no = -1
        self.dsem = -1
        self.dval = 0


class Sched:
    ENGS = ("pe", "act", "dve", "pool", "sp")
    NDMA = {"sp": 16, "pool": 8, "act": 4}

    def __init__(self, nc):
        self.nc = nc
        self.ops = {e: [] for e in self.ENGS}
        self.ndma = {e: 0 for e in self.ENGS}

    def add(self, eng, fn, reads=(), writes=(), dma=False):
        op = Op(eng, len(self.ops[eng]), fn, dma)
        deps = {}
        for b in reads:
            if b.w is not None:
                deps[id(b.w)] = b.w
        for b in writes:
            if b.w is not None:
                deps[id(b.w)] = b.w
            for r in b.r:
                deps[id(r)] = r
        keep = []
        for d in deps.values():
            if d is op:
                continue
            if d.eng == eng and not d.dma:
                if eng == "pe":
                    continue
                if op.idx - d.idx > 1 and not dma:
                    continue
            keep.append(d)
        op.deps = keep
        for d in keep:
            d.signal = True
        if dma:
            n = self.ndma[eng]
            self.ndma[eng] = n + 1
            op.dsem = n % self.NDMA[eng]
            op.dval = 16 * (n // self.NDMA[eng] + 1)
            op.signal = True
        for b in reads:
            b.r.append(op)
        for b in writes:
            b.w = op
            b.r = []
        self.ops[eng].append(op)
        return op

    def emit(self):
        nc = self.nc
        for e in self.ENGS:
            for op in self.ops[e]:
                if not op.dma:
                    op.signal = False
        for e in self.ENGS:
            seen_idx = {}
            for op in self.ops[e]:
                best = {}
                for d in op.deps:
                    if d.dma:
                        continue
                    if seen_idx.get(d.eng, -1) >= d.idx:
                        continue
                    if d.eng not in best or best[d.eng].idx < d.idx:
                        best[d.eng] = d
                for de, d in best.items():
                    d.signal = True
                    seen_idx[de] = d.idx
        nsig = {}
        for e in self.ENGS:
            n = 0
            for op in self.ops[e]:
                if op.signal and not op.dma:
                    op.sig_no = n
                    n += 1
            nsig[e] = n
        import contextlib
        with contextlib.ExitStack() as st:
            csem = {}
            for e in self.ENGS:
                k = (nsig[e] + EPOCH - 1) // EPOCH
                csem[e] = [st.enter_context(nc.semaphore(f"c_{e}_{i}")) for i in range(max(k, 1))]
            dsem = {}
            for e in self.ENGS:
                if self.ndma[e]:
                    dsem[e] = [st.enter_context(nc.semaphore(f"d_{e}_{i}"))
                               for i in range(min(self.NDMA[e], self.ndma[e]))]
            block = st.enter_context(nc.Block())
            ops = self.ops

            def run(eng_name, engine):
                seen = {}
                seen_idx = {}
                for op in ops[eng_name]:
                    waits = {}
                    best = {}
                    for d in op.deps:
                        if d.dma:
                            key = ("d", d.eng, d.dsem)
                            val = d.dval
                            if seen.get(key, 0) >= val:
                                continue
                            if waits.get(key, 0) < val:
                                waits[key] = val
                        else:
                            if seen_idx.get(d.eng, -1) >= d.idx:
                                continue
                            if d.eng not in best or best[d.eng].idx < d.idx:
                                best[d.eng] = d
                    for de, d in best.items():
                        assert d.signal and d.sig_no >= 0
                        seen_idx[de] = d.idx
                        waits[("c", d.eng, d.sig_no // EPOCH)] = d.sig_no % EPOCH + 1
                    if op.dma and op.dval > 16:
                        key = ("d", eng_name, op.dsem)
                        val = op.dval - 16
                        if seen.get(key, 0) < val and waits.get(key, 0) < val:
                            waits[key] = val
                    for key, val in waits.items():
                        sem = dsem[key[1]][key[2]] if key[0] == "d" else csem[key[1]][key[2]]
                        engine.wait_ge(sem, val)
                        seen[key] = val
                    inst = op.fn(engine)
                    if op.dma:
                        inst.then_inc(dsem[eng_name][op.dsem], 16)
                    elif op.signal:
                        inst.then_inc(csem[eng_name][op.sig_no // EPOCH], 1)
                if self.ndma[eng_name]:
                    last = {}
                    for op in ops[eng_name]:
                        if op.dma:
                            last[op.dsem] = op.dval
                    for s, v in last.items():
                        engine.wait_ge(dsem[eng_name][s], v)

            if ops["pe"]:
                @block.tensor
                def _(eng):
                    run("pe", eng)
            if ops["act"]:
                @block.scalar
                def _(eng):
                    run("act", eng)
            if ops["dve"]:
                @block.vector
                def _(eng):
                    run("dve", eng)
            if ops["pool"]:
                @block.gpsimd
                def _(eng):
                    run("pool", eng)
            if ops["sp"]:
                @block.sync
                def _(eng):
                    run("sp", eng)

    def barrier(self):
        pend = []
        for e in self.ENGS:
            lst = self.ops[e]
            if not lst:
                continue
            if not lst[-1].dma:
                pend.append(lst[-1])
            seen = set()
            for op in reversed(lst):
                if op.dma and op.dsem not in seen:
                    seen.add(op.dsem)
                    pend.append(op)
                if len(seen) >= self.NDMA.get(e, 0) or (lst[-1].idx - op.idx) > 64:
                    break
        self._bar = pend
        self._bar_done = set()

    def _bar_deps(self, eng):
        if getattr(self, "_bar", None) and eng not in self._bar_done:
            self._bar_done.add(eng)
            return list(self._bar)
        return []


_orig_add = Sched.add


def _add(self, eng, fn, reads=(), writes=(), dma=False):
    extra = self._bar_deps(eng)
    op = _orig_add(self, eng, fn, reads, writes, dma)
    for d in extra:
        if d is not op and all(d is not k for k in op.deps):
            if d.eng == eng and not d.dma and eng != "pe" and False:
                continue
            op.deps.append(d)
            d.signal = True
    return op


Sched.add = _add


class Ctx:
    def __init__(self, nc, st):
        self.nc = nc
        self.st = st
        self.S = Sched(nc)
        self.base = None
        self.uid = 0
        self.psum = []
        self.pbuf = []
        for i in range(8):
            t = st.enter_context(nc.psum_tensor(f"psb{i}", [128, 512], F32))
            self.psum.append(t)
            self.pbuf.append(Buf(f"ps{i}"))
        self.ARENA0 = 16512 + 64
        self.off = self.ARENA0
        self.persist_off = self.ARENA0

    def reset_arena(self, keep=None):
        self.off = self.persist_off if keep is None else keep

    def sb(self, shape, dtype, name=None):
        esz = 4 if dtype in (F32, I32) else 2
        n = 1
        for s in shape[1:]:
            n *= s
        nbytes = (n * esz + 63) // 64 * 64
        self.uid += 1
        nm = f"{name or 't'}_{self.uid}"
        t = self.nc.alloc_sbuf_tensor_at(nm, list(shape), dtype, offset=self.off)
        self.off += nbytes
        assert self.off <= 228000, f"SBUF arena overflow {self.off}"
        return t


class T:
    __slots__ = ("ap", "b")

    def __init__(self, ap, b=None):
        if ap is not None and "TensorHandle" in type(ap).__name__:
            ap = ap[:]
        self.ap = ap
        self.b = b if b is not None else Buf()


def _bufs(ts):
    return [t.b for t in ts]


def op(C, eng, fn, reads=(), writes=(), dma=False):
    return C.S.add(eng, fn, _bufs(reads), _bufs(writes), dma)


def mm_group(C, ps, pairs, reads):
    n = len(pairs)

    def fn(e):
        inst = None
        for k, (l, r) in enumerate(pairs):
            inst = e.matmul(ps.ap, lhsT=l, rhs=r, start=(k == 0), stop=(k == n - 1))
        return inst
    return op(C, "pe", fn, reads=reads, writes=[ps])


def setup_consts(C, lng_d, lnb_d):
    nc = C.nc
    K = {}
    K["ones1024"] = T(C.sb([128, 128], BF16, "ones1024"))
    op(C, "dve", lambda e: e.memset(K["ones1024"].ap[:], 1.0 / 1024.0), writes=[K["ones1024"]])
    K["neghalf"] = T(C.sb([128, TT], F32, "neghalf"))
    op(C, "pool", lambda e: e.memset(K["neghalf"].ap[:], -0.5), writes=[K["neghalf"]])
    K["lng"] = T(C.sb([128, 12 * NCH], F32, "lng"))
    K["lnb"] = T(C.sb([128, 12 * NCH], F32, "lnb"))
    op(C, "sp", lambda e: e.dma_start(out=K["lng"].ap[:], in_=lng_d), writes=[K["lng"]], dma=True)
    op(C, "sp", lambda e: e.dma_start(out=K["lnb"].ap[:], in_=lnb_d), writes=[K["lnb"]], dma=True)
    C.K = K
    make_masks(C)
    C.persist_off = C.off
    return K


def ln_tail(C, zt, nch, ones, eps, zb, zq, ps_m, ps_q, tmp, g_cols, b_cols, outs, func=None, gb_reads=None):
    K = C.K
    mm_group(C, ps_m, [(ones.ap[:], zb[j].ap) for j in range(nch)], reads=[ones] + zb)
    mm_group(C, ps_q, [(ones.ap[:], zq[j].ap) for j in range(nch)], reads=[ones] + zq)
    op(C, "act", lambda e: e.activation(out=tmp["m2"].ap, in_=ps_m.ap, func=AF.Square), reads=[ps_m], writes=[tmp["m2"]])
    op(C, "act", lambda e: e.activation(out=tmp["mean"].ap, in_=ps_m.ap, func=AF.Copy), reads=[ps_m], writes=[tmp["mean"]])
    op(C, "dve", lambda e: e.scalar_tensor_tensor(out=tmp["vare"].ap, in0=ps_q.ap, scalar=float(eps), in1=tmp["m2"].ap,
                                                  op0=ALU.add, op1=ALU.subtract), reads=[ps_q, tmp["m2"]], writes=[tmp["vare"]])
    op(C, "act", lambda e: e.activation(out=tmp["rstd"].ap, in_=tmp["vare"].ap, func=AF.Ln), reads=[tmp["vare"]], writes=[tmp["rstd"]])
    op(C, "act", lambda e: e.activation(out=tmp["rstd"].ap, in_=tmp["rstd"].ap, func=AF.Exp, scale=-0.5), reads=[tmp["rstd"]], writes=[tmp["rstd"]])
    for j in range(nch):
        zc = tmp["zc%d" % (j % 2)]
        op(C, "dve", (lambda j, zc: lambda e: e.tensor_tensor(out=zc.ap, in0=zt[j].ap, in1=tmp["mean"].ap, op=ALU.subtract))(j, zc),
           reads=[zt[j], tmp["mean"]], writes=[zc])
        op(C, "dve", (lambda j, zc: lambda e: e.tensor_tensor(out=zc.ap, in0=zc.ap, in1=tmp["rstd"].ap, op=ALU.mult))(j, zc),
           reads=[zc, tmp["rstd"]], writes=[zc])
        op(C, "act", (lambda j, zc: lambda e: e.activation(out=outs[j].ap, in_=zc.ap, func=(func or AF.Identity),
                                                          scale=g_cols[j], bias=b_cols[j]))(j, zc),
           reads=[zc] + (gb_reads if gb_reads is not None else [K["lng"], K["lnb"]]), writes=[outs[j]])


def ffn_phase(C, xsrc, xdst, wg_d, wu_d, wd_d, ln_idx):
    nc = C.nc
    K = C.K
    C.S.barrier()
    C.reset_arena()
    xt = [[T(None) for j in range(NCH)] for s in range(2)]
    xt_t = [C.sb([128, NCH, TT], F32, "xt") for s in range(2)]
    for s in range(2):
        for j in range(NCH):
            xt[s][j].ap = xt_t[s][:, j, :]
    xb_t = [C.sb([128, NCH, TT], BF16, "xb") for s in range(2)]
    xb = [[T(xb_t[s][:, j, :]) for j in range(NCH)] for s in range(2)]
    hT_t = C.sb([128, NFT, TT], BF16, "hT")
    hT = [T(hT_t[:, f, :]) for f in range(NFT)]
    NWS = 6
    wg_t = [T(C.sb([128, NCH, 256], BF16, "wg")) for s in range(NWS)]
    wu_t = [T(C.sb([128, NCH, 256], BF16, "wu")) for s in range(NWS)]
    wd_t = [T(C.sb([128, NFT, 128], BF16, "wd")) for s in range(NWS)]
    sg = [T(C.sb([128, TT], F32, "sg")) for s in range(2)]
    zb_t = C.sb([128, NCH, TT], BF16, "zb")
    zq_t = C.sb([128, NCH, TT], BF16, "zq")
    zb = [T(zb_t[:, j, :]) for j in range(NCH)]
    zq = [T(zq_t[:, j, :]) for j in range(NCH)]
    tmp = {k: T(C.sb([128, TT], F32, k)) for k in ("m2", "mean", "vare", "rstd", "zc0", "zc1")}
    for k in tmp:
        tmp[k].ap = tmp[k].ap[:]
    ps = [T(C.psum[i][:], C.pbuf[i]) for i in range(8)]
    g_cols = [K["lng"].ap[:, ln_idx * NCH + j: ln_idx * NCH + j + 1] for j in range(NCH)]
    b_cols = [K["lnb"].ap[:, ln_idx * NCH + j: ln_idx * NCH + j + 1] for j in range(NCH)]
    xs_v = xsrc.rearrange("(c p) t -> p c t", p=128)
    xd_v = xdst.rearrange("(c p) t -> p c t", p=128)
    cres = 0.5 / DN_ALPHA
    eps = LN_EPS / (DN_ALPHA * DN_ALPHA)
    cnt = {"gu": 0, "wd": 0, "w": 0}

    def load(t):
        s = t % 2
        op(C, "sp", lambda e: e.dma_start(out=xt_t[s][:], in_=xs_v[:, :, t * TT:(t + 1) * TT]), writes=xt[s], dma=True)
        for j in range(NCH):
            op(C, "act", (lambda j: lambda e: e.activation(out=xb[s][j].ap, in_=xt[s][j].ap, func=AF.Copy))(j),
               reads=[xt[s][j]], writes=[xb[s][j]])

    import os
    NOW = int(os.environ.get('FFN_NOW', '0'))

    def wload(fg):
        k = cnt["w"] % NWS
        cnt["w"] += 1
        if NOW and cnt["w"] > 2:
            return k
        op(C, "pool", lambda e: e.dma_start(out=wg_t[k].ap[:], in_=wg_d[fg]), writes=[wg_t[k]], dma=True)
        op(C, "pool", lambda e: e.dma_start(out=wu_t[k].ap[:], in_=wu_d[fg]), writes=[wu_t[k]], dma=True)
        return k

    def gu(t, fgs):
        s = t % 2
        for fg in fgs:
            k = wload(fg)
            for ft in range(2):
                f = fg * 2 + ft
                i = cnt["gu"] % 2
                cnt["gu"] += 1
                pg, pu = ps[i], ps[2 + i]
                mm_group(C, pg, [(wg_t[k].ap[:, c, ft * 128:(ft + 1) * 128], xb[s][c].ap) for c in range(NCH)],
                         reads=[wg_t[k]] + xb[s])
                mm_group(C, pu, [(wu_t[k].ap[:, c, ft * 128:(ft + 1) * 128], xb[s][c].ap) for c in range(NCH)],
                         reads=[wu_t[k]] + xb[s])
                op(C, "act", (lambda i, pg: lambda e: e.activation(out=sg[i].ap[:], in_=pg.ap, func=AF.Silu))(i, pg),
                   reads=[pg], writes=[sg[i]])
                op(C, "dve", (lambda i, pu, f: lambda e: e.tensor_tensor(out=hT[f].ap, in0=sg[i].ap[:], in1=pu.ap, op=ALU.mult))(i, pu, f),
                   reads=[sg[i], pu], writes=[hT[f]])

    def down(t):
        s = t % 2
        for j in range(NCH):
            k = cnt["wd"] % NWS
            cnt["wd"] += 1
            if not (NOW and cnt["wd"] > 2):
                op(C, "pool", (lambda k, j: lambda e: e.dma_start(out=wd_t[k].ap[:], in_=wd_d[j]))(k, j), writes=[wd_t[k]], dma=True)
            py = ps[4 + k % 2]
            mm_group(C, py, [(wd_t[k].ap[:, c, :], hT[c].ap) for c in range(NFT)], reads=[wd_t[k]] + hT)
            op(C, "dve", (lambda j, py: lambda e: e.scalar_tensor_tensor(out=xt[s][j].ap, in0=py.ap, scalar=float(cres), in1=xt[s][j].ap,
                                                                          op0=ALU.mult, op1=ALU.add))(j, py),
               reads=[py, xt[s][j]], writes=[xt[s][j]])
            op(C, "act", (lambda j: lambda e: e.activation(out=zb[j].ap, in_=xt[s][j].ap, func=AF.Copy))(j), reads=[xt[s][j]], writes=[zb[j]])
            op(C, "act", (lambda j: lambda e: e.activation(out=zq[j].ap, in_=xt[s][j].ap, func=AF.Square))(j), reads=[xt[s][j]], writes=[zq[j]])

    def tail(t):
        s = t % 2
        ln_tail(C, xt[s], NCH, K["ones1024"], eps, zb, zq, ps[6], ps[7], tmp, g_cols, b_cols, xt[s])
        op(C, "sp", lambda e: e.dma_start(out=xd_v[:, :, t * TT:(t + 1) * TT], in_=xt_t[s][:]), reads=xt[s], dma=True)

    load(0)
    for t in range(NTT):
        gu(t, range(0, 2))
        if t > 0:
            tail(t - 1)
        if t + 1 < NTT:
            load(t + 1)
        gu(t, range(2, 11))
        down(t)
    tail(NTT - 1)


def host_ffn_weights(inputs, i, k):
    wg = np.ascontiguousarray(inputs["ffn_w_gate"][i, k].reshape(NCH, 128, 11, 256).transpose(2, 1, 0, 3))
    wu = np.ascontiguousarray(inputs["ffn_w_up"][i, k].reshape(NCH, 128, 11, 256).transpose(2, 1, 0, 3))
    wd = np.ascontiguousarray(inputs["ffn_w_down"][i, k].reshape(NFT, 128, NCH, 128).transpose(2, 1, 0, 3))
    return wg, wu, wd


def host_ln(inputs):
    g = np.ascontiguousarray(inputs["ln_g"].reshape(12, NCH, 128).transpose(2, 0, 1).reshape(128, 12 * NCH))
    b = np.ascontiguousarray(inputs["ln_b"].reshape(12, NCH, 128).transpose(2, 0, 1).reshape(128, 12 * NCH))
    return g, b


def build_program(plan):
    import contextlib
    nc = bass.Bass("TRN2", target_bir_lowering=False)
    xin = nc.dram_tensor("xin", [D_MODEL, SEQ], F32, kind="ExternalInput").ap()
    xout = nc.dram_tensor("xout", [D_MODEL, SEQ], F32, kind="ExternalOutput").ap()
    lng_d = nc.dram_tensor("lng", [128, 12 * NCH], F32, kind="ExternalInput").ap()
    lnb_d = nc.dram_tensor("lnb", [128, 12 * NCH], F32, kind="ExternalInput").ap()
    scr = []
    if len(plan) > 1:
        scr = [nc.dram_tensor(f"xscr{i}", [D_MODEL, SEQ], F32, kind="Internal").ap() for i in range(2)]
    with contextlib.ExitStack() as st:
        C = Ctx(nc, st)
        setup_consts(C, lng_d, lnb_d)
        src = xin
        for n, ph in enumerate(plan):
            dst = xout if n == len(plan) - 1 else scr[n % 2]
            if ph[0] == "ffn":
                _, i, k = ph
                wg_d = nc.dram_tensor(f"wg_{i}_{k}", [11, 128, NCH, 256], F32, kind="ExternalInput").ap()
                wu_d = nc.dram_tensor(f"wu_{i}_{k}", [11, 128, NCH, 256], F32, kind="ExternalInput").ap()
                wd_d = nc.dram_tensor(f"wd_{i}_{k}", [NCH, 128, NFT, 128], F32, kind="ExternalInput").ap()
                ffn_phase(C, src, dst, wg_d, wu_d, wd_d, i * 3 + (0 if k == 0 else 2))
            else:
                MIXERS[ph[0]](C, src, dst, ph)
            src = dst
        C.S.emit()
    return nc


def host_inputs_for(plan, inputs):
    d = {}
    g, b = host_ln(inputs)
    d["lng"] = g
    d["lnb"] = b
    for ph in plan:
        if ph[0] == "ffn":
            _, i, k = ph
            wg, wu, wd = host_ffn_weights(inputs, i, k)
            d[f"wg_{i}_{k}"] = wg
            d[f"wu_{i}_{k}"] = wu
            d[f"wd_{i}_{k}"] = wd
        else:
            d.update(MIXER_HOST[ph[0]](inputs, ph))
    return d


MIXERS = {}
MIXER_HOST = {}


class MixBase:
    def __init__(self, C, xsrc, xdst, ln_idx, nbuf=2, alloc_z=True, alloc_tmp=True):
        self.C = C
        self.nbuf = nbuf
        K = C.K
        C.S.barrier()
        C.reset_arena()
        self.xt_t = [C.sb([128, NCH, TT], F32, "xt") for s in range(nbuf)]
        self.xt = [[T(self.xt_t[s][:, j, :]) for j in range(NCH)] for s in range(nbuf)]
        self.xb_t = [C.sb([128, NCH, TT], BF16, "xb") for s in range(nbuf)]
        self.xb = [[T(self.xb_t[s][:, j, :]) for j in range(NCH)] for s in range(nbuf)]
        if alloc_z:
            zb_t = C.sb([128, NCH, TT], BF16, "zb")
            zq_t = C.sb([128, NCH, TT], BF16, "zq")
            self.zb = [T(zb_t[:, j, :]) for j in range(NCH)]
            self.zq = [T(zq_t[:, j, :]) for j in range(NCH)]
        if alloc_tmp:
            self.tmp = {k: T(C.sb([128, TT], F32, k)[:]) for k in ("m2", "mean", "vare", "rstd", "zc0", "zc1")}
        self.ps = [T(C.psum[i][:], C.pbuf[i]) for i in range(8)]
        self.g_cols = [K["lng"].ap[:, ln_idx * NCH + j: ln_idx * NCH + j + 1] for j in range(NCH)]
        self.b_cols = [K["lnb"].ap[:, ln_idx * NCH + j: ln_idx * NCH + j + 1] for j in range(NCH)]
        self.xs_v = xsrc.rearrange("(c p) t -> p c t", p=128)
        self.xd_v = xdst.rearrange("(c p) t -> p c t", p=128)
        self.eps = LN_EPS / (DN_ALPHA * DN_ALPHA)

    def load(self, t):
        C = self.C
        s = t % self.nbuf
        op(C, "sp", lambda e: e.dma_start(out=self.xt_t[s][:], in_=self.xs_v[:, :, t * TT:(t + 1) * TT]), writes=self.xt[s], dma=True)
        for j in range(NCH):
            op(C, "act", (lambda j: lambda e: e.activation(out=self.xb[s][j].ap, in_=self.xt[s][j].ap, func=AF.Copy))(j),
               reads=[self.xt[s][j]], writes=[self.xb[s][j]])

    def resid(self, t, j, py):
        C = self.C
        s = t % self.nbuf
        xt = self.xt[s][j]
        op(C, "dve", lambda e: e.scalar_tensor_tensor(out=xt.ap, in0=py.ap, scalar=float(1.0 / DN_ALPHA), in1=xt.ap,
                                                      op0=ALU.mult, op1=ALU.add), reads=[py, xt], writes=[xt])
        op(C, "act", lambda e: e.activation(out=self.zb[j].ap, in_=xt.ap, func=AF.Copy), reads=[xt], writes=[self.zb[j]])
        op(C, "act", lambda e: e.activation(out=self.zq[j].ap, in_=xt.ap, func=AF.Square), reads=[xt], writes=[self.zq[j]])

    def finish(self, t):
        C = self.C
        s = t % self.nbuf
        ln_tail(C, self.xt[s], NCH, C.K["ones1024"], self.eps, self.zb, self.zq, self.ps[6], self.ps[7], self.tmp,
                self.g_cols, self.b_cols, self.xt[s])
        op(C, "sp", lambda e: e.dma_start(out=self.xd_v[:, :, t * TT:(t + 1) * TT], in_=self.xt_t[s][:]), reads=self.xt[s], dma=True)


def load_w_resident(C, w_d, nk, ncols, name, piece=512):
    t = C.sb([128, nk, ncols], BF16, name)
    W = T(t)
    v = w_d.rearrange("(c p) n -> p c n", p=128)
    for c0 in range(0, ncols, piece):
        c1 = min(ncols, c0 + piece)
        op(C, "pool", (lambda c0, c1: lambda e: e.dma_start(out=t[:, :, c0:c1], in_=v[:, :, c0:c1]))(c0, c1), writes=[W], dma=True)
    return W


def load_cols(C, v_d, ncol, name):
    t = T(C.sb([128, ncol], F32, name))
    op(C, "sp", lambda e: e.dma_start(out=t.ap[:], in_=v_d), writes=[t], dma=True)
    return t


CONF_K = 31


def conv_phase(C, xsrc, xdst, ph):
    nc = C.nc
    M = MixBase(C, xsrc, xdst, 0 * 3 + 1)
    w_in_d = nc.dram_tensor("conv_w_in", [D_MODEL, 2 * D_MODEL], F32, kind="ExternalInput").ap()
    w_out_d = nc.dram_tensor("conv_w_out", [D_MODEL, D_MODEL], F32, kind="ExternalInput").ap()
    wdw_d = nc.dram_tensor("conv_wdw", [128, CONF_K * NCH], F32, kind="ExternalInput").ap()
    cg_d = nc.dram_tensor("conv_lng", [128, NCH], F32, kind="ExternalInput").ap()
    cb_d = nc.dram_tensor("conv_lnb", [128, NCH], F32, kind="ExternalInput").ap()
    w_in = load_w_resident(C, w_in_d, NCH, 2 * D_MODEL, "cwin")
    w_out = load_w_resident(C, w_out_d, NCH, D_MODEL, "cwout")
    wdw = load_cols(C, wdw_d, CONF_K * NCH, "wdw")
    wdw2_d = nc.dram_tensor("conv_wdw2", [128, NCH * CONF_K], F32, kind="ExternalInput").ap()
    wdw2 = load_cols(C, wdw2_d, CONF_K * NCH, "wdw2")
    dg = [T(C.sb([128, CONF_K, 128], BF16, "dg")) for i in range(2)]
    cg = load_cols(C, cg_d, NCH, "cg")
    cb = load_cols(C, cb_d, NCH, "cb")
    H = CONF_K - 1
    vx_t = C.sb([128, NCH, H + TT], BF16, "vext")
    vx = [T(vx_t[:, j, :]) for j in range(NCH)]
    op(C, "pool", lambda e: e.memset(vx_t[:], 0.0), writes=vx)
    sgm = [T(C.sb([128, TT], F32, "sgm")[:]) for i in range(2)]
    acc = [T(C.sb([128, TT], F32, "acc")[:]) for i in range(NCH)]
    ub_t = C.sb([128, NCH, TT], BF16, "ub")
    uq_t = C.sb([128, NCH, TT], BF16, "uq")
    ub = [T(ub_t[:, j, :]) for j in range(NCH)]
    uq = [T(uq_t[:, j, :]) for j in range(NCH)]
    sT_t = C.sb([128, NCH, TT], BF16, "sT")
    sT = [T(sT_t[:, j, :]) for j in range(NCH)]
    ps = M.ps
    cgc = [cg.ap[:, j:j + 1] for j in range(NCH)]
    cbc = [cb.ap[:, j:j + 1] for j in range(NCH)]
    M.load(0)
    for t in range(NTT):
        s = t % 2
        xb = M.xb[s]
        if t + 1 < NTT:
            M.load(t + 1)
        for oc in range(NCH):
            i = oc % 2
            pa, pg = ps[i], ps[2 + i]
            mm_group(C, pa, [(w_in.ap[:, c, oc * 128:(oc + 1) * 128], xb[c].ap) for c in range(NCH)], reads=[w_in] + xb)
            mm_group(C, pg, [(w_in.ap[:, c, D_MODEL + oc * 128:D_MODEL + (oc + 1) * 128], xb[c].ap) for c in range(NCH)], reads=[w_in] + xb)
            op(C, "act", (lambda i, pg: lambda e: e.activation(out=sgm[i].ap, in_=pg.ap, func=AF.Sigmoid))(i, pg), reads=[pg], writes=[sgm[i]])
            op(C, "dve", (lambda i, pa, oc: lambda e: e.tensor_tensor(out=vx_t[:, oc, H:H + TT], in0=sgm[i].ap, in1=pa.ap, op=ALU.mult))(i, pa, oc),
               reads=[sgm[i], pa, vx[oc]], writes=[vx[oc]])
        for oc in range(NCH):
            d = dg[oc % 2]
            op(C, "pool", (lambda d, oc: lambda e: e.tensor_tensor(out=d.ap, in0=C.K["identb"].ap.unsqueeze(1).to_broadcast([128, CONF_K, 128]),
                                                                   in1=wdw2.ap[:, oc * CONF_K:(oc + 1) * CONF_K].unsqueeze(2).to_broadcast([128, CONF_K, 128]),
                                                                   op=ALU.mult))(d, oc), reads=[C.K["identb"], wdw2], writes=[d])
            pcv = ps[oc % 4]
            mm_group(C, pcv, [(d.ap[:, k, :], vx_t[:, oc, k:k + TT]) for k in range(CONF_K)], reads=[d, vx[oc]])
            act(C, acc[oc], pcv, AF.Copy)
        for oc in range(NCH):
            op(C, "pool", (lambda oc: lambda e: e.tensor_copy(out=vx_t[:, oc, 0:H], in_=vx_t[:, oc, TT:TT + H]))(oc), reads=[vx[oc]], writes=[vx[oc]])
            op(C, "act", (lambda oc: lambda e: e.activation(out=ub[oc].ap, in_=acc[oc].ap, func=AF.Copy))(oc), reads=[acc[oc]], writes=[ub[oc]])
            op(C, "act", (lambda oc: lambda e: e.activation(out=uq[oc].ap, in_=acc[oc].ap, func=AF.Square))(oc), reads=[acc[oc]], writes=[uq[oc]])
        ln_tail(C, acc, NCH, C.K["ones1024"], LN_EPS, ub, uq, ps[6], ps[7], M.tmp, cgc, cbc, sT, func=AF.Silu, gb_reads=[cg, cb])
        for j in range(NCH):
            py = ps[4 + j % 2]
            mm_group(C, py, [(w_out.ap[:, c, j * 128:(j + 1) * 128], sT[c].ap) for c in range(NCH)], reads=[w_out] + sT)
            M.resid(t, j, py)
        M.finish(t)


def conv_host(inputs, ph):
    d = {}
    d["conv_w_in"] = np.ascontiguousarray(inputs["conv_w_in"][0])
    d["conv_w_out"] = np.ascontiguousarray(inputs["conv_w_out"][0])
    d["conv_wdw"] = np.ascontiguousarray(inputs["conv_w_dw"][0].reshape(CONF_K, NCH, 128).transpose(2, 0, 1).reshape(128, CONF_K * NCH))
    d["conv_wdw2"] = np.ascontiguousarray(inputs["conv_w_dw"][0].reshape(CONF_K, NCH, 128).transpose(2, 1, 0).reshape(128, NCH * CONF_K))
    d["conv_lng"] = np.ascontiguousarray(inputs["conv_ln_g"][0].reshape(NCH, 128).T)
    d["conv_lnb"] = np.ascontiguousarray(inputs["conv_ln_b"][0].reshape(NCH, 128).T)
    return d


MIXERS["conv"] = conv_phase
MIXER_HOST["conv"] = conv_host


def sconv_phase(C, xsrc, xdst, ph):
    nc = C.nc
    M = MixBase(C, xsrc, xdst, 1 * 3 + 1)
    w_in_d = nc.dram_tensor("sc_w_in", [D_MODEL, 3 * D_MODEL], F32, kind="ExternalInput").ap()
    w_out_d = nc.dram_tensor("sc_w_out", [D_MODEL, D_MODEL], F32, kind="ExternalInput").ap()
    wc_d = nc.dram_tensor("sc_wc", [128, 3 * NCH], F32, kind="ExternalInput").ap()
    w_in = load_w_resident(C, w_in_d, NCH, 3 * D_MODEL, "swin")
    w_out = load_w_resident(C, w_out_d, NCH, D_MODEL, "swout")
    wc = load_cols(C, wc_d, 3 * NCH, "swc")
    H = 2
    cv_t = C.sb([128, NCH, H + TT], BF16, "cvext")
    wc2_d = nc.dram_tensor("sc_wc2", [128, NCH * 3], F32, kind="ExternalInput").ap()
    wc2 = load_cols(C, wc2_d, 3 * NCH, "swc2")
    dg3 = T(C.sb([128, NCH * 3, 128], BF16, "dg3"))
    op(C, "pool", lambda e: e.tensor_tensor(out=dg3.ap, in0=C.K["identb"].ap.unsqueeze(1).to_broadcast([128, NCH * 3, 128]),
                                            in1=wc2.ap.unsqueeze(2).to_broadcast([128, NCH * 3, 128]), op=ALU.mult),
       reads=[C.K["identb"], wc2], writes=[dg3])
    cv = [T(cv_t[:, j, :]) for j in range(NCH)]
    op(C, "pool", lambda e: e.memset(cv_t[:], 0.0), writes=cv)
    cs = [T(C.sb([128, TT], F32, "cs")[:]) for i in range(2)]
    gb = [T(C.sb([128, TT], F32, "gb")[:]) for i in range(2)]
    acc = [T(C.sb([128, TT], F32, "acc")[:]) for i in range(2)]
    yb_t = C.sb([128, NCH, TT], BF16, "yb")
    yb = [T(yb_t[:, j, :]) for j in range(NCH)]
    ps = M.ps
    M.load(0)
    for t in range(NTT):
        s = t % 2
        xb = M.xb[s]
        if t + 1 < NTT:
            M.load(t + 1)
        for oc in range(NCH):
            i = oc % 2
            pc, pv, pb = ps[i], ps[2 + i], ps[4]
            mm_group(C, pc, [(w_in.ap[:, c, D_MODEL + oc * 128:D_MODEL + (oc + 1) * 128], xb[c].ap) for c in range(NCH)], reads=[w_in] + xb)
            mm_group(C, pv, [(w_in.ap[:, c, 2 * D_MODEL + oc * 128:2 * D_MODEL + (oc + 1) * 128], xb[c].ap) for c in range(NCH)], reads=[w_in] + xb)
            mm_group(C, pb, [(w_in.ap[:, c, oc * 128:(oc + 1) * 128], xb[c].ap) for c in range(NCH)], reads=[w_in] + xb)
            op(C, "act", (lambda i, pc: lambda e: e.activation(out=cs[i].ap, in_=pc.ap, func=AF.Copy))(i, pc), reads=[pc], writes=[cs[i]])
            op(C, "act", (lambda i, pb: lambda e: e.activation(out=gb[i].ap, in_=pb.ap, func=AF.Copy))(i, pb), reads=[pb], writes=[gb[i]])
            op(C, "dve", (lambda i, pv, oc: lambda e: e.tensor_tensor(out=cv_t[:, oc, H:H + TT], in0=cs[i].ap, in1=pv.ap, op=ALU.mult))(i, pv, oc),
               reads=[cs[i], pv, cv[oc]], writes=[cv[oc]])
            pcv = ps[6 + i]
            mm_group(C, pcv, [(dg3.ap[:, oc * 3 + k, :], cv_t[:, oc, k:k + TT]) for k in range(3)], reads=[dg3, cv[oc]])
            op(C, "dve", (lambda i, oc, pcv: lambda e: e.tensor_tensor(out=yb[oc].ap, in0=gb[i].ap, in1=pcv.ap, op=ALU.mult))(i, oc, pcv),
               reads=[pcv, gb[i]], writes=[yb[oc]])
            op(C, "pool", (lambda oc: lambda e: e.tensor_copy(out=cv_t[:, oc, 0:H], in_=cv_t[:, oc, TT:TT + H]))(oc), reads=[cv[oc]], writes=[cv[oc]])
        for j in range(NCH):
            py = ps[5]
            mm_group(C, py, [(w_out.ap[:, c, j * 128:(j + 1) * 128], yb[c].ap) for c in range(NCH)], reads=[w_out] + yb)
            M.resid(t, j, py)
        M.finish(t)


def sconv_host(inputs, ph):
    d = {}
    d["sc_w_in"] = np.ascontiguousarray(inputs["sc_w_in"][0])
    d["sc_w_out"] = np.ascontiguousarray(inputs["sc_w_out"][0])
    d["sc_wc"] = np.ascontiguousarray(inputs["sc_w_conv"][0].reshape(3, NCH, 128).transpose(2, 0, 1).reshape(128, 3 * NCH))
    d["sc_wc2"] = np.ascontiguousarray(inputs["sc_w_conv"][0].reshape(3, NCH, 128).transpose(2, 1, 0).reshape(128, 3 * NCH))
    return d


MIXERS["sconv"] = sconv_phase
MIXER_HOST["sconv"] = sconv_host


def tv(t, ap):
    return T(ap, t.b)


def act(C, out, in_, func, extra=(), **kw):
    return op(C, "act", lambda e: e.activation(out=out.ap, in_=in_.ap, func=func, **kw), reads=[in_] + list(extra), writes=[out])


def tt(C, eng, out, a, b, alu):
    return op(C, eng, lambda e: e.tensor_tensor(out=out.ap, in0=a.ap, in1=b.ap, op=alu), reads=[a, b], writes=[out])


def ts(C, eng, out, a, s1, s2, op0, op1=None, extra=(), accum=None):
    def fn(e):
        kw = {}
        if op1 is not None:
            kw["op1"] = op1
        if accum is not None:
            kw["accum_out"] = accum.ap
        return e.tensor_scalar(out=out.ap, in0=a.ap, scalar1=s1, scalar2=s2, op0=op0, **kw)
    return op(C, eng, fn, reads=[a] + list(extra), writes=[out] + ([accum] if accum is not None else []))


def stt(C, out, a, scalar, b, op0, op1, extra=()):
    return op(C, "dve", lambda e: e.scalar_tensor_tensor(out=out.ap, in0=a.ap, scalar=scalar, in1=b.ap, op0=op0, op1=op1),
              reads=[a, b] + list(extra), writes=[out])


def mm1(C, ps, lhsT, rhs, extra=()):
    return op(C, "pe", lambda e: e.matmul(ps.ap, lhsT=lhsT.ap, rhs=rhs.ap, start=True, stop=True), reads=[lhsT, rhs] + list(extra), writes=[ps])


def rsqrt_act(C, t):
    op(C, "act", lambda e: e.activation(out=t.ap, in_=t.ap, func=AF.Ln), reads=[t], writes=[t])
    op(C, "act", lambda e: e.activation(out=t.ap, in_=t.ap, func=AF.Exp, scale=-0.5), reads=[t], writes=[t])


def make_masks(C):
    K = C.K
    onesf = T(C.sb([128, 128], F32, "onesf"))
    op(C, "pool", lambda e: e.memset(onesf.ap[:], 1.0), writes=[onesf])
    K["onesf"] = onesf

    def sel(name, pattern, base, cm, cmp_):
        t = T(C.sb([128, 128], F32, name))
        op(C, "pool", lambda e: e.affine_select(out=t.ap[:], in_=onesf.ap[:], pattern=pattern, compare_op=cmp_, fill=0.0,
                                                base=base, channel_multiplier=cm), reads=[onesf], writes=[t])
        K[name] = t
        return t
    sel("identf", [[1, 128]], 0, -1, ALU.is_equal)
    sel("Ls", [[-1, 128]], -1, 1, ALU.is_ge)
    sel("Lt", [[-1, 128]], 0, 1, ALU.is_ge)
    sel("Ut", [[1, 128]], 0, -1, ALU.is_ge)
    idb = T(C.sb([128, 128], BF16, "identb"))
    op(C, "pool", lambda e: e.tensor_copy(out=idb.ap[:], in_=K["identf"].ap[:]), reads=[K["identf"]], writes=[idb])
    K["identb"] = idb
    o128 = T(C.sb([128, 128], BF16, "ones128"))
    op(C, "dve", lambda e: e.memset(o128.ap[:], 1.0 / 128.0), writes=[o128])
    K["ones128"] = o128
    o1 = T(C.sb([128, 128], BF16, "ones1"))
    op(C, "dve", lambda e: e.memset(o1.ap[:], 1.0), writes=[o1])
    K["ones1"] = o1


GDN_HK, GDN_HV = 8, 16
GDN_IN = 6176


def gdn_phase(C, xsrc, xdst, ph):
    import os
    STOP = int(os.environ.get('GDN_STOP', '99'))
    nc = C.nc
    K = C.K
    M = MixBase(C, xsrc, xdst, 3 * 3 + 1, nbuf=1)
    w_in_d = nc.dram_tensor("gdn_w_in", [D_MODEL, 6144], F32, kind="ExternalInput").ap().rearrange("(c p) n -> p c n", p=128)
    w_ba_d = nc.dram_tensor("gdn_w_ba", [D_MODEL, 64], F32, kind="ExternalInput").ap().rearrange("(c p) n -> p c n", p=128)
    w_out_d = nc.dram_tensor("gdn_w_out", [2048, D_MODEL], F32, kind="ExternalInput").ap()
    w_out_v = w_out_d.rearrange("(h p) n -> p h n", p=128)
    wcv_d = nc.dram_tensor("gdn_wconv", [128, 4 * 32], F32, kind="ExternalInput").ap()
    adt_d = nc.dram_tensor("gdn_adt", [128, 2], F32, kind="ExternalInput").ap()
    ng_d = nc.dram_tensor("gdn_ng", [128, 1], F32, kind="ExternalInput").ap()
    gb_scr = nc.dram_tensor("gdn_gb_scr", [32, TT], F32, kind="Internal").ap()
    gbs = T(None)
    wcv = load_cols(C, wcv_d, 128, "wcv")
    adt = load_cols(C, adt_d, 2, "adt")
    ng = load_cols(C, ng_d, 1, "ng")
    nea = T(C.sb([128, 1], F32, "nea"))
    act(C, nea, tv(adt, adt.ap[:, 0:1]), AF.Exp)
    ts(C, "dve", nea, nea, -1.0, None, ALU.mult)
    w_ba = T(C.sb([128, NCH, 64], BF16, "wba"))
    op(C, "pool", lambda e: e.dma_start(out=w_ba.ap[:], in_=w_ba_d), writes=[w_ba], dma=True)
    qk_t = C.sb([128, 16, TT], BF16, "qk")
    vv_t = C.sb([128, 16, TT], BF16, "vv")
    og_t = C.sb([128, 16, TT], BF16, "og")
    qk = [T(qk_t[:, i, :]) for i in range(16)]
    vv = [T(vv_t[:, i, :]) for i in range(16)]
    ogb = [Buf() for i in range(4)]
    og = [T(og_t[:, i, :], ogb[i // 4]) for i in range(16)]
    win = [T(C.sb([128, NCH, 128], BF16, "win")) for i in range(4)]
    wout = [T(C.sb([128, 16, 128], BF16, "wout")) for i in range(2)]
    S_t = C.sb([128, 16, 128], F32, "S")
    Sb_t = C.sb([128, 16, 128], BF16, "Sb")
    Sst = [T(S_t[:, h, :]) for h in range(16)]
    Sbt = [T(Sb_t[:, h, :]) for h in range(16)]
    op(C, "pool", lambda e: e.memset(S_t[:], 0.0), writes=Sst)
    op(C, "pool", lambda e: e.memset(Sb_t[:], 0.0), writes=Sbt)
    halo_t = C.sb([128, 32, 4], BF16, "halo")
    halo = [T(halo_t[:, i, :]) for i in range(32)]
    op(C, "pool", lambda e: e.memset(halo_t[:], 0.0), writes=halo)
    rext = [T(C.sb([128, 4 + TT], BF16, "rext")) for i in range(4)]
    dg4 = [T(C.sb([128, 4, 128], BF16, "dg4")) for i in range(4)]
    wcv2_d = nc.dram_tensor("gdn_wconv2", [128, 32 * 4], F32, kind="ExternalInput").ap()
    wcv2 = load_cols(C, wcv2_d, 128, "wcv2")
    cacc = [T(C.sb([128, TT], F32, "cacc")) for i in range(4)]
    sqb = [T(C.sb([128, TT], BF16, "sqb")) for i in range(4)]
    betaT = T(C.sb([128, TT], F32, "betaT"))
    gT = T(C.sb([128, TT], F32, "gT"))
    gc = T(C.sb([128, TT], F32, "gc"))
    spt = T(C.sb([128, TT], F32, "spt"))
    cols_t = C.sb([128, 4, 32], F32, "cols")
    cols = T(cols_t)
    ecol_t = C.sb([128, 4, 48], F32, "ecol")
    ecol = T(ecol_t)
    G = 4
    f32n = ("X", "Eg", "ET", "ETm", "ETn", "E", "Tt", "u", "oT", "rs")
    b16n = ("N", "Nt", "P0", "P1", "Q0", "Q1", "Tb", "kbg", "kt", "vb", "wT", "vnb", "iT", "qg", "sq2")
    L = {k: T(C.sb([128, G, 128], F32, k)) for k in f32n}
    L.update({k: T(C.sb([128, G, 128], BF16, k)) for k in b16n})
    Grow = T(C.sb([128, G, 128], F32, "Grow"))
    Brow = T(C.sb([128, G, 128], F32, "Brow"))
    ps = M.ps
    pbank = [T(C.psum[4 + i][:].rearrange("p (g f) -> p g f", g=4), C.pbuf[4 + i]) for i in range(2)]
    qn = {"i": 0}

    def nq():
        qn["i"] += 1
        return pbank[qn["i"] % 2]
    pfull = {"i": 0}

    def nf():
        pfull["i"] += 1
        return ps[pfull["i"] % 4]
    wn = {"i": 0, "o": 0}
    identb = K["identb"]
    Us = T(C.sb([128, 128], F32, "Us"))
    op(C, "pool", lambda e: e.affine_select(out=Us.ap, in_=K["onesf"].ap, pattern=[[1, 128]], compare_op=ALU.is_ge, fill=0.0,
                                            base=-1, channel_multiplier=-1), reads=[K["onesf"]], writes=[Us])
    A = lambda t: T(t.ap[:], t.b)
    bc = lambda t, ap: T(ap.unsqueeze(1).to_broadcast([128, G, 128]), t.b)

    def mmG(pt, pairs, reads):
        def fn(e):
            inst = None
            for g, (l, r) in enumerate(pairs):
                inst = e.matmul(pt.ap[:, g, :], lhsT=l, rhs=r, start=True, stop=True)
            return inst
        return op(C, "pe", fn, reads=reads, writes=[pt])

    for t in range(NTT):
        s = 0
        M.load(t)
        xb = M.xb[s]
        for pc2 in range(12):
            ocs = [pc2 * 4 + i for i in range(4)]
            pps = []
            for i4 in range(4):
                oc = pc2 * 4 + i4
                wk = win[wn["i"] % 4]
                wn["i"] += 1
                op(C, "pool", (lambda wk, oc: lambda e: e.dma_start(out=wk.ap[:], in_=w_in_d[:, :, oc * 128:(oc + 1) * 128]))(wk, oc), writes=[wk], dma=True)
                pp = nf()
                pps.append(pp)
                mm_group(C, pp, [(wk.ap[:, c, :], xb[c].ap) for c in range(NCH)], reads=[wk] + xb)
                if oc >= 32:
                    act(C, og[oc - 32], pp, AF.Silu)
                else:
                    r = rext[i4]
                    act(C, tv(r, r.ap[:, 3:3 + TT]), pp, AF.Copy)
            if ocs[0] >= 32:
                continue
            pcs = []
            for i, oc in enumerate(ocs):
                r = rext[i]
                op(C, "pool", (lambda r, oc: lambda e: e.tensor_copy(out=r.ap[:, 0:3], in_=halo_t[:, oc, 0:3]))(r, oc), reads=[halo[oc], r], writes=[r])
                op(C, "pool", (lambda r, oc: lambda e: e.tensor_copy(out=halo_t[:, oc, 0:3], in_=r.ap[:, TT:TT + 3]))(r, oc), reads=[r, halo[oc]], writes=[halo[oc]])
                d = dg4[i]
                op(C, "pool", (lambda d, oc: lambda e: e.tensor_tensor(out=d.ap, in0=K["identb"].ap.unsqueeze(1).to_broadcast([128, 4, 128]),
                                                                       in1=wcv2.ap[:, oc * 4:(oc + 1) * 4].unsqueeze(2).to_broadcast([128, 4, 128]),
                                                                       op=ALU.mult))(d, oc), reads=[K["identb"], wcv2], writes=[d])
            for i, oc in enumerate(ocs):
                pcv = nf()
                pcs.append(pcv)
                mm_group(C, pcv, [(dg4[i].ap[:, k, :], rext[i].ap[:, k:k + TT]) for k in range(4)], reads=[dg4[i], rext[i]])
                if oc >= 16:
                    act(C, vv[oc - 16], pcv, AF.Silu)
                else:
                    act(C, qk[oc], pcv, AF.Silu)
            if ocs[0] >= 16:
                continue
            for i, oc in enumerate(ocs):
                act(C, A(sqb[i]), qk[oc], AF.Square)
            pns = []
            for i, oc in enumerate(ocs):
                pn = nf()
                pns.append(pn)
                mm1(C, pn, A(K["ones1"]), A(sqb[i]))
                ts(C, "dve", A(cacc[i]), pn, float(RMS_EPS), None, ALU.add)
            for i, oc in enumerate(ocs):
                op(C, "act", (lambda t_: lambda e: e.activation(out=t_.ap, in_=t_.ap, func=AF.Ln))(A(cacc[i])), reads=[cacc[i]], writes=[cacc[i]])
            for i, oc in enumerate(ocs):
                op(C, "act", (lambda t_: lambda e: e.activation(out=t_.ap, in_=t_.ap, func=AF.Exp, scale=-0.5))(A(cacc[i])), reads=[cacc[i]], writes=[cacc[i]])
            for i, oc in enumerate(ocs):
                stt(C, qk[oc], qk[oc], float(128.0 ** -0.5 if oc < 8 else 1.0), A(cacc[i]), ALU.mult, ALU.mult)
        pb = nf()
        pbv = T(pb.ap[0:64, :], pb.b)
        mm_group(C, pbv, [(w_ba.ap[:, c, :], xb[c].ap) for c in range(NCH)], reads=[w_ba] + xb)
        act(C, tv(betaT, betaT.ap[0:16, :]), tv(pb, pb.ap[0:16, :]), AF.Sigmoid)
        act(C, tv(spt, spt.ap[32:48, :]), tv(pb, pb.ap[32:48, :]), AF.Exp, extra=[adt], bias=adt.ap[32:48, 1:2])
        act(C, tv(spt, spt.ap[32:48, :]), tv(spt, spt.ap[32:48, :]), AF.Ln, bias=1.0)
        ts(C, "dve", tv(gT, gT.ap[32:48, :]), tv(spt, spt.ap[32:48, :]), nea.ap[32:48, 0:1], None, ALU.mult, extra=[nea])
        for n in range(4):
            op(C, "dve", (lambda n: lambda e: e.tensor_tensor_scan(out=gc.ap[32:48, n * 128:(n + 1) * 128], data0=K["onesf"].ap[32:48, :],
                                                                    data1=gT.ap[32:48, n * 128:(n + 1) * 128], initial=0.0,
                                                                    op0=ALU.mult, op1=ALU.add))(n), reads=[gT, K["onesf"]], writes=[gc])
        op(C, "sp", lambda e: e.dma_start(out=gb_scr[0:16, :], in_=betaT.ap[0:16, :]), reads=[betaT], writes=[gbs], dma=True)
        op(C, "sp", lambda e: e.dma_start(out=gb_scr[16:32, :], in_=gc.ap[32:48, :]), reads=[gc], writes=[gbs], dma=True)
        for n in range(4):
            op(C, "sp", (lambda n: lambda e: e.dma_start(out=cols_t[:, n, :], in_=gb_scr[:, n * 128:(n + 1) * 128].rearrange("r p -> p r"),
                                                         allow_slow_non_contiguous=True))(n), reads=[gbs], writes=[cols], dma=True)
        act(C, tv(ecol, ecol_t[:, :, 0:16]), tv(cols, cols_t[:, :, 16:32]), AF.Exp)
        tt(C, "dve", tv(ecol, ecol_t[:, :, 16:32]), tv(ecol, ecol_t[:, :, 0:16]), tv(cols, cols_t[:, :, 0:16]), ALU.mult)
        ts(C, "dve", tv(ecol, ecol_t[:, :, 32:48]), tv(cols, cols_t[:, :, 0:16]), -1.0, None, ALU.mult)
        for h0 in range(0, 16 if STOP > 1 else 0, G):
            hs = list(range(h0, h0 + G))
            for n in range(4 if STOP > 2 else 0):
                cs = slice(n * 128, (n + 1) * 128)
                qcs = [qk[h // 2].ap[:, cs] for h in hs]
                kcs = [qk[8 + h // 2].ap[:, cs] for h in hs]
                qkr = [qk[h // 2] for h in hs] + [qk[8 + h // 2] for h in hs]
                colb = lambda t, ap: T(ap.unsqueeze(2).to_broadcast([128, G, 128]), t.b)
                gcolb = colb(cols, cols_t[:, n, 16 + h0:16 + h0 + G])
                bcolb = colb(cols, cols_t[:, n, h0:h0 + G])
                bgcolb = colb(ecol, ecol_t[:, n, 16 + h0:16 + h0 + G])
                nbcolb = colb(ecol, ecol_t[:, n, 32 + h0:32 + h0 + G])
                op(C, "sp", (lambda h0, n: lambda e: e.dma_start(out=Grow.ap, in_=gb_scr[16 + h0:16 + h0 + G, n * 128:(n + 1) * 128].partition_broadcast(128)))(h0, n),
                   reads=[gbs], writes=[Grow], dma=True)
                op(C, "sp", (lambda h0, n: lambda e: e.dma_start(out=Brow.ap, in_=gb_scr[h0:h0 + G, n * 128:(n + 1) * 128].partition_broadcast(128)))(h0, n),
                   reads=[gbs], writes=[Brow], dma=True)
                Gr = Grow
                Br = Brow
                act(C, L["Eg"], Gr, AF.Exp)
                tt(C, "dve", L["X"], Gr, gcolb, ALU.subtract)
                ts(C, "dve", L["ET"], L["X"], 0.0, None, ALU.min)
                act(C, L["ET"], L["ET"], AF.Exp)
                tt(C, "pool", L["ETm"], L["ET"], bc(K["Ut"], K["Ut"].ap), ALU.mult)
                tt(C, "pool", L["ETn"], L["ET"], bc(Us, Us.ap), ALU.mult)
                tt(C, "pool", L["ETn"], L["ETn"], Br, ALU.mult)
                ts(C, "dve", L["E"], L["X"], 0.0, None, ALU.max)
                act(C, L["E"], L["E"], AF.Exp, scale=-1.0)
                tt(C, "pool", L["E"], L["E"], bc(K["Ls"], K["Ls"].ap), ALU.mult)
                if STOP <= 3:
                    continue
                pk = nq()
                mmG(pk, [(kcs[g], kcs[g]) for g in range(G)], reads=qkr)
                tt(C, "dve", L["X"], pk, L["E"], ALU.mult)
                tt(C, "dve", L["N"], L["X"], nbcolb, ALU.mult)
                stt(C, L["Nt"], pk, -1.0, L["ETn"], ALU.mult, ALU.mult)
                tt(C, "dve", L["Tt"], L["Nt"], bc(K["identf"], K["identf"].ap), ALU.add)
                act(C, L["Tb"], L["Tt"], AF.Copy)
                cP, cQ = "N", "Nt"
                if STOP <= 4:
                    continue
                def SQ(step, cP, cQ):
                    nP, nQ = "P%d" % (step % 2), "Q%d" % (step % 2)
                    pa = nq()
                    mmG(pa, [(L[cQ].ap[:, g, :], L[cP].ap[:, g, :]) for g in range(G)], reads=[L[cQ], L[cP]])
                    act(C, L[nP], pa, AF.Copy)
                    if step < 5:
                        pb2 = nq()
                        mmG(pb2, [(L[cP].ap[:, g, :], L[cQ].ap[:, g, :]) for g in range(G)], reads=[L[cQ], L[cP]])
                        op(C, "dve", (lambda o, i: lambda e: e.tensor_copy(out=o.ap, in_=i.ap))(L[nQ], pb2), reads=[pb2], writes=[L[nQ]])
                    return nP, nQ

                def TU(nP):
                    pc2 = nq()
                    mmG(pc2, [(L[nP].ap[:, g, :], L["Tb"].ap[:, g, :]) for g in range(G)], reads=[L[nP], L["Tb"]])
                    tt(C, "dve", L["Tt"], pc2, L["Tt"], ALU.add)
                    act(C, L["Tb"], L["Tt"], AF.Copy)
                prevP = None
                for step in range(6):
                    nP, nQ = SQ(step, cP, cQ)
                    if prevP is not None:
                        TU(prevP)
                    prevP = nP
                    cP, cQ = nP, nQ
                TU(prevP)
                if STOP <= 5:
                    continue
                pk2 = nq()
                mmG(pk2, [(kcs[g], identb.ap) for g in range(G)], reads=qkr + [identb])
                tt(C, "dve", L["kbg"], pk2, bgcolb, ALU.mult)
                tt(C, "dve", L["kt"], pk2, T(L["ETm"].ap[:, :, 127:128].to_broadcast([128, G, 128]), L["ETm"].b), ALU.mult)
                pv = nq()
                mmG(pv, [(vv[h].ap[:, cs], identb.ap) for h in hs], reads=[vv[h] for h in hs] + [identb])
                tt(C, "dve", L["vb"], pv, bcolb, ALU.mult)
                pw = nq()
                mmG(pw, [(L["kbg"].ap[:, g, :], L["Tb"].ap[:, g, :]) for g in range(G)], reads=[L["kbg"], L["Tb"]])
                act(C, L["wT"], pw, AF.Copy)
                pu = nq()
                mmG(pu, [(L["Tb"].ap[:, g, :], L["vb"].ap[:, g, :]) for g in range(G)], reads=[L["vb"], L["Tb"]])
                act(C, L["u"], pu, AF.Copy)
                pi = nq()
                mmG(pi, [(kcs[g], qcs[g]) for g in range(G)], reads=qkr)
                tt(C, "dve", L["iT"], pi, L["ETm"], ALU.mult)
                for g in range(G):
                    tt(C, "dve", tv(L["qg"], L["qg"].ap[:, g, :]), T(qcs[g], qk[hs[g] // 2].b), tv(L["Eg"], L["Eg"].ap[:, g, :]), ALU.mult)
                if STOP <= 6:
                    continue
                Sg = T(S_t[:, h0:h0 + G, :], Sst[h0].b)
                Sbg = T(Sb_t[:, h0:h0 + G, :], Sbt[h0].b)
                pws = nq()
                mmG(pws, [(L["wT"].ap[:, g, :], Sb_t[:, h0 + g, :]) for g in range(G)], reads=[L["wT"], Sbg])
                tt(C, "dve", L["u"], L["u"], pws, ALU.subtract)
                act(C, L["vnb"], L["u"], AF.Copy)
                po = nq()

                def fn_po(e, po=po, h0=h0):
                    inst = None
                    for g in range(G):
                        e.matmul(po.ap[:, g, :], lhsT=Sb_t[:, h0 + g, :], rhs=L["qg"].ap[:, g, :], start=True, stop=False)
                        inst = e.matmul(po.ap[:, g, :], lhsT=L["vnb"].ap[:, g, :], rhs=L["iT"].ap[:, g, :], start=False, stop=True)
                    return inst
                op(C, "pe", fn_po, reads=[Sbg, L["qg"], L["vnb"], L["iT"]], writes=[po])
                act(C, L["oT"], po, AF.Copy)
                act(C, L["sq2"], po, AF.Square)
                pss = nq()
                mmG(pss, [(L["kt"].ap[:, g, :], L["vnb"].ap[:, g, :]) for g in range(G)], reads=[L["kt"], L["vnb"]])
                tt(C, "dve", Sg, Sg, T(L["Eg"].ap[:, :, 127:128].to_broadcast([128, G, 128]), L["Eg"].b), ALU.mult)
                tt(C, "dve", Sg, Sg, pss, ALU.add)
                act(C, Sbg, Sg, AF.Copy)
                pm = nq()
                op(C, "pe", (lambda pm: lambda e: e.matmul(pm.ap.rearrange("p g f -> p (g f)"), lhsT=K["ones128"].ap, rhs=L["sq2"].ap.rearrange("p g f -> p (g f)"),
                                                           start=True, stop=True))(pm), reads=[K["ones128"], L["sq2"]], writes=[pm])
                ts(C, "dve", L["rs"], pm, float(RMS_EPS), None, ALU.add)
                rsqrt_act(C, L["rs"])
                stt(C, L["oT"], L["oT"], ng.ap[:, 0:1], L["rs"], ALU.mult, ALU.mult, extra=[ng])
                ogc = T(og_t[:, h0:h0 + G, cs], og[h0].b)
                tt(C, "dve", ogc, L["oT"], ogc, ALU.mult)
        for j in range(NCH):
            py = ps[j % 2]
            wo = wout[wn["o"] % 2]
            wn["o"] += 1
            op(C, "pool", (lambda wo, j: lambda e: e.dma_start(out=wo.ap[:], in_=w_out_v[:, :, j * 128:(j + 1) * 128]))(wo, j), writes=[wo], dma=True)
            mm_group(C, py, [(wo.ap[:, hh, :], og[hh].ap) for hh in range(16)], reads=[wo] + og)
            M.resid(t, j, py)
        M.finish(t)


def gdn_host(inputs, ph):
    d = {}
    w = inputs["gdn_w_in"][0]
    d["gdn_w_in"] = np.ascontiguousarray(w[:, :6144])
    ba = np.zeros((D_MODEL, 64), np.float32)
    ba[:, 0:16] = w[:, 6144:6160]
    ba[:, 32:48] = w[:, 6160:6176]
    d["gdn_w_ba"] = ba
    d["gdn_w_out"] = np.ascontiguousarray(inputs["gdn_w_out"][0])
    d["gdn_wconv"] = np.ascontiguousarray(inputs["gdn_w_conv"][0].reshape(4, 32, 128).transpose(2, 0, 1).reshape(128, 128))
    d["gdn_wconv2"] = np.ascontiguousarray(inputs["gdn_w_conv"][0].reshape(4, 32, 128).transpose(2, 1, 0).reshape(128, 128))
    adt = np.zeros((128, 2), np.float32)
    adt[32:48, 0] = inputs["gdn_a_log"][0]
    adt[32:48, 1] = inputs["gdn_dt_bias"][0]
    d["gdn_adt"] = adt
    d["gdn_ng"] = np.ascontiguousarray(inputs["gdn_norm_g"][0].reshape(128, 1))
    return d


MIXERS["gdn"] = gdn_phase
MIXER_HOST["gdn"] = gdn_host


TOPK = 256
NIT = 15
ROPE_THETA = 500000.0


def dsa_consts():
    inv = np.zeros((128, 1), np.float32)
    rot = np.zeros((128, 128), np.float32)
    freqs = (ROPE_THETA ** (-np.arange(0, 16, 2, dtype=np.float32) / 16.0)).astype(np.float32)
    for p in range(128):
        r = p % 64
        if r < 8:
            inv[p, 0] = freqs[r]
            rot[p + 8, p] = -1.0
        elif r < 16:
            inv[p, 0] = freqs[r - 8]
            rot[p - 8, p] = 1.0
    pw = np.tile((0.5 ** np.arange(1, NIT + 2, dtype=np.float32))[None, :], (128, 1)).astype(np.float32)
    return inv, rot, pw


def dsa_phase(C, xsrc, xdst, ph):
    import os
    import math
    STOP = int(os.environ.get("DSA_STOP", "99"))
    nc = C.nc
    DBG = int(os.environ.get("DSA_DBG", "-1"))
    if DBG >= 0:
        dbg0 = nc.dram_tensor("dbg0", [128, SEQ], F32, kind="ExternalOutput").ap()
        dbg1 = nc.dram_tensor("dbg1", [128, SEQ], BF16, kind="ExternalOutput").ap()
        dbg2 = nc.dram_tensor("dbg2", [128, 8], F32, kind="ExternalOutput").ap()
    K = C.K
    M = MixBase(C, xsrc, xdst, 2 * 3 + 1, nbuf=1, alloc_z=False, alloc_tmp=False)
    dt_in = lambda n, shp: nc.dram_tensor(n, shp, F32, kind="ExternalInput").ap()
    w_dq = load_w_resident(C, dt_in("dsa_w_dq", [D_MODEL, 256]), NCH, 256, "wdq")
    w_uq = load_w_resident(C, dt_in("dsa_w_uq", [256, 1024]), 2, 1024, "wuq")
    w_iq = load_w_resident(C, dt_in("dsa_w_iq", [256, 512]), 2, 512, "wiq")
    w_dkv = load_w_resident(C, dt_in("dsa_w_dkv", [D_MODEL, 128]), NCH, 128, "wdkv")
    w_kr = load_w_resident(C, dt_in("dsa_w_kr_pad", [D_MODEL, 128]), NCH, 128, "wkr")
    w_ik = load_w_resident(C, dt_in("dsa_w_ik_pad", [D_MODEL, 128]), NCH, 128, "wik")
    w_iw = load_w_resident(C, dt_in("dsa_w_iw", [D_MODEL, 8]), NCH, 8, "wiw")
    w_uk = load_w_resident(C, dt_in("dsa_w_uk_pad", [16 * 128, 128]), 16, 128, "wuk")
    w_uv = load_w_resident(C, dt_in("dsa_w_uv", [16 * 128, 64]), 16, 64, "wuv")
    w_o_d = dt_in("dsa_w_o", [D_MODEL, D_MODEL]).rearrange("(h p) n -> p h n", p=128)
    qng = load_cols(C, dt_in("dsa_qng", [128, 2]), 2, "qng")
    kvg = load_cols(C, dt_in("dsa_kvg", [128, 1]), 1, "kvg")
    ikg = load_cols(C, dt_in("dsa_ikgb", [128, 2]), 2, "ikgb")
    invf = load_cols(C, dt_in("dsa_inv", [128, 1]), 1, "invf")
    pw2 = load_cols(C, dt_in("dsa_pw2", [128, NIT + 1]), NIT + 1, "pw2")
    rot = load_w_resident(C, dt_in("dsa_rot", [128, 128]), 1, 128, "rot")
    pos_d = nc.dram_tensor("dsa_pos", [1, SEQ], I32, kind="ExternalInput").ap()
    o256 = T(C.sb([128, 128], BF16, "ones256"))
    op(C, "dve", lambda e: e.memset(o256.ap, 1.0 / 256.0), writes=[o256])
    cmask = T(C.sb([128, 128], F32, "cmask"))
    zer = T(C.sb([128, 128], F32, "zer"))
    op(C, "pool", lambda e: e.memset(zer.ap, 0.0), writes=[zer])
    op(C, "pool", lambda e: e.affine_select(out=cmask.ap, in_=zer.ap, pattern=[[-1, 128]], compare_op=ALU.is_ge, fill=-1.0e30,
                                            base=0, channel_multiplier=1), reads=[zer], writes=[cmask])
    ckvT_t = C.sb([128, SEQ], BF16, "ckvT")
    ckvT = T(ckvT_t)
    ckvk_t = C.sb([128, SEQ // 128, 128], BF16, "ckvtok")
    ckvk = T(ckvk_t)
    krA_t = C.sb([128, SEQ], BF16, "krA")
    krB_t = C.sb([128, SEQ], BF16, "krB")
    kiA_t = C.sb([128, SEQ], BF16, "kiA")
    kiB_t = C.sb([128, SEQ], BF16, "kiB")
    krA, krB, kiA, kiB = T(krA_t), T(krB_t), T(kiA_t), T(kiB_t)
    for tl in (krA, krB, kiA, kiB):
        op(C, "pool", (lambda tl: lambda e: e.memset(tl.ap, 0.0))(tl), writes=[tl])
    selT_t = C.sb([128, 32, 256], BF16, "selT")
    selT = T(selT_t)
    acc = T(C.sb([128, SEQ], F32, "acc"))
    sel = T(C.sb([128, SEQ], BF16, "sel"))
    qr_t = C.sb([128, NCH, TT], BF16, "qr")
    qr = [T(qr_t[:, j, :]) for j in range(NCH)]
    qi_t = C.sb([128, 4, TT], BF16, "qi")
    qi = [T(qi_t[:, j, :]) for j in range(4)]
    cq_t = C.sb([128, 2, TT], BF16, "cq")
    cq = [T(cq_t[:, j, :]) for j in range(2)]
    cqs = [T(C.sb([128, TT], BF16, "cqs")) for j in range(2)]
    wi4 = T(C.sb([128, 4, 8], F32, "wi4"))
    cosE = T(C.sb([128, TT], F32, "cosE"))
    sinE = T(C.sb([128, TT], F32, "sinE"))
    posi = T(C.sb([128, TT], I32, "posi"))
    ang = T(C.sb([128, TT], F32, "ang"))
    ti = posi
    rf = [T(C.sb([128, TT], F32, "rf")) for i in range(2)]
    rb = [T(C.sb([128, TT], BF16, "rb")) for i in range(2)]
    r1 = [T(C.sb([128, TT], F32, "r1")) for i in range(2)]
    tq = r1[0]
    tf = rf[0]
    cqf = rf
    relu1 = T(C.sb([128, 2, TT], F32, "relu"))
    relu_t = [relu1, relu1]
    M.tmp = {"m2": rf[0], "mean": rf[1], "vare": r1[0], "rstd": r1[1], "zc0": T(relu1.ap[:, 0, :], relu1.b), "zc1": T(relu1.ap[:, 1, :], relu1.b)}
    kdst = T(C.sb([128, TT], BF16, "kdst"))
    M.zq = qr
    M.zb = [T(sel.ap[:, j * TT:(j + 1) * TT], sel.b) for j in range(NCH)]
    qlat = [T(C.sb([128, TT], BF16, "qlat")) for i in range(2)]
    PT = [T(C.sb([128, 4, 256], BF16, "PT")) for i in range(2)]
    olat2 = [T(C.sb([128, 256], BF16, "olat")) for i in range(2)]
    oT_t = C.sb([128, NCH, TT], BF16, "oTd")
    oT = [T(oT_t[:, j, :]) for j in range(NCH)]
    rden = T(C.sb([128, 256], F32, "rden"))
    wo_s = [T(C.sb([128, NCH, 128], BF16, "wo")) for i in range(4)]
    sm = {k: T(C.sb([128, 1], F32, k)) for k in ("lo", "hi", "rng", "mid", "cnt", "ge", "nmid", "sg")}
    selbufA, selbufB = Buf(), Buf()
    hwt = T(C.sb([128, NIT + 1], F32, "hwt"))
    ps = M.ps
    pobuf = [Buf() for i in range(4)]
    pctr = {"f": 0, "s": 0, "o": 0, "w": 0}

    def nf():
        pctr["f"] += 1
        return ps[pctr["f"] % 2]

    def pair_bank(i):
        b = 2 + 2 * (i % 2)
        return b

    def rope(src_ps, dst):
        i = pctr["s"] % 2
        pctr["s"] += 1
        act(C, rf[i], src_ps, AF.Copy)
        act(C, rb[i], src_ps, AF.Copy)
        pr = nf()
        mm1(C, pr, tv(rot, rot.ap[:, 0, :]), rb[i])
        tt(C, "dve", r1[i], rf[i], cosE, ALU.mult)
        tt(C, "dve", rf[i], pr, sinE, ALU.mult)
        tt(C, "dve", dst, r1[i], rf[i], ALU.add)

    def rope_sb(src_f32, dst):
        i = pctr["s"] % 2
        pctr["s"] += 1
        act(C, rb[i], src_f32, AF.Copy)
        pr = nf()
        mm1(C, pr, tv(rot, rot.ap[:, 0, :]), rb[i])
        tt(C, "dve", r1[i], src_f32, cosE, ALU.mult)
        tt(C, "dve", rf[i], pr, sinE, ALU.mult)
        tt(C, "dve", dst, r1[i], rf[i], ALU.add)

    TWO_PI = 2.0 * math.pi
    for t in range(NTT):
        c0 = t * TT
        M.load(t)
        xb = M.xb[0]
        op(C, "sp", lambda e, c0=c0: e.dma_start(out=posi.ap, in_=pos_d[0:1, c0:c0 + TT].partition_broadcast(128)), writes=[posi], dma=True)
        op(C, "dve", lambda e: e.tensor_copy(out=ang.ap, in_=posi.ap), reads=[posi], writes=[ang])
        ts(C, "dve", ang, ang, invf.ap[:, 0:1], None, ALU.mult, extra=[invf])
        for (dst, off) in ((sinE, 0.5), (cosE, 0.75)):
            ts(C, "dve", tq, ang, float(1.0 / TWO_PI), float(off), ALU.mult, ALU.add)
            op(C, "dve", lambda e: e.tensor_copy(out=ti.ap, in_=tq.ap), reads=[tq], writes=[ti])
            op(C, "dve", lambda e: e.tensor_copy(out=tf.ap, in_=ti.ap), reads=[ti], writes=[tf])
            tt(C, "dve", tq, tq, tf, ALU.subtract)
            stt(C, tq, tq, 0.0, tq, ALU.is_lt, ALU.add)
            ts(C, "dve", tq, tq, 1.0, None, ALU.min)
            ts(C, "dve", tq, tq, float(TWO_PI), float(-math.pi), ALU.mult, ALU.add)
            ts(C, "dve", tq, tq, float(math.pi), float(-math.pi), ALU.min, ALU.max)
            act(C, dst, tq, AF.Sin)
        pk = nf()
        mm_group(C, pk, [(w_dkv.ap[:, c, :], xb[c].ap) for c in range(NCH)], reads=[w_dkv] + xb)
        act(C, rf[0], pk, AF.Copy)
        act(C, rb[0], pk, AF.Square)
        pq_ = nf()
        mm1(C, pq_, K["ones128"], rb[0])
        ts(C, "dve", r1[0], pq_, float(RMS_EPS), None, ALU.add)
        rsqrt_act(C, r1[0])
        stt(C, tv(ckvT, ckvT_t[:, c0:c0 + TT]), rf[0], kvg.ap[:, 0:1], r1[0], ALU.mult, ALU.mult, extra=[kvg])
        pt4 = nf()
        pt4v = T(pt4.ap.rearrange("p (g f) -> p g f", g=4), pt4.b)

        def fn_tr(e, c0=c0, pt4v=pt4v):
            inst = None
            for g in range(4):
                inst = e.matmul(pt4v.ap[:, g, :], lhsT=ckvT_t[:, c0 + g * 128:c0 + (g + 1) * 128], rhs=K["identb"].ap, start=True, stop=True)
            return inst
        op(C, "pe", fn_tr, reads=[ckvT, K["identb"]], writes=[pt4])
        act(C, tv(ckvk, ckvk_t[:, 4 * t:4 * t + 4, :]), pt4v, AF.Copy)
        pkr = nf()
        mm_group(C, pkr, [(w_kr.ap[:, c, :], xb[c].ap) for c in range(NCH)], reads=[w_kr] + xb)
        rope(pkr, kdst)
        op(C, "pool", lambda e, c0=c0: e.tensor_copy(out=krA_t[0:16, c0:c0 + TT], in_=kdst.ap[0:16, :]), reads=[kdst], writes=[krA])
        op(C, "pool", lambda e, c0=c0: e.tensor_copy(out=krB_t[64:80, c0:c0 + TT], in_=kdst.ap[64:80, :]), reads=[kdst], writes=[krB])
        pik = nf()
        mm_group(C, pik, [(w_ik.ap[:, c, :], xb[c].ap) for c in range(NCH)], reads=[w_ik] + xb)
        act(C, rf[0], pik, AF.Copy)
        act(C, rb[0], pik, AF.Copy)
        act(C, rb[1], pik, AF.Square)
        pm_ = nf()
        mm1(C, pm_, K["ones128"], rb[0])
        pq2 = nf()
        mm1(C, pq2, K["ones128"], rb[1])
        act(C, r1[0], pm_, AF.Square)
        act(C, ang, pm_, AF.Copy)
        stt(C, r1[0], pq2, float(LN_EPS), r1[0], ALU.add, ALU.subtract)
        rsqrt_act(C, r1[0])
        tt(C, "dve", rf[0], rf[0], ang, ALU.subtract)
        tt(C, "dve", rf[0], rf[0], r1[0], ALU.mult)
        act(C, ang, rf[0], AF.Identity, extra=[ikg], scale=ikg.ap[:, 0:1], bias=ikg.ap[:, 1:2])
        rope_sb(ang, kdst)
        op(C, "pool", lambda e, c0=c0: e.tensor_copy(out=kiA_t[0:64, c0:c0 + TT], in_=kdst.ap[0:64, :]), reads=[kdst], writes=[kiA])
        op(C, "pool", lambda e, c0=c0: e.tensor_copy(out=kiB_t[64:128, c0:c0 + TT], in_=kdst.ap[64:128, :]), reads=[kdst], writes=[kiB])
        for j in range(2):
            pc_ = nf()
            mm_group(C, pc_, [(w_dq.ap[:, c, j * 128:(j + 1) * 128], xb[c].ap) for c in range(NCH)], reads=[w_dq] + xb)
            act(C, cqf[j], pc_, AF.Copy)
            act(C, cqs[j], pc_, AF.Square)
        pq3 = nf()
        mm_group(C, pq3, [(o256.ap, cqs[j].ap) for j in range(2)], reads=[o256] + cqs)
        ts(C, "dve", r1[0], pq3, float(RMS_EPS), None, ALU.add)
        rsqrt_act(C, r1[0])
        for j in range(2):
            stt(C, cq[j], cqf[j], qng.ap[:, j:j + 1], r1[0], ALU.mult, ALU.mult, extra=[qng])
        for j in range(NCH):
            pj = nf()
            mm_group(C, pj, [(w_uq.ap[:, c, j * 128:(j + 1) * 128], cq[c].ap) for c in range(2)], reads=[w_uq] + cq)
            rope(pj, qr[j])
        for j in range(4):
            pj = nf()
            mm_group(C, pj, [(w_iq.ap[:, c, j * 128:(j + 1) * 128], cq[c].ap) for c in range(2)], reads=[w_iq] + cq)
            rope(pj, qi[j])
        pw_ = nf()
        pwv = T(pw_.ap[:, 0:32].rearrange("p (g f) -> p g f", g=4), pw_.b)

        def fn_wi(e, pwv=pwv, xb=xb):
            inst = None
            for qs in range(4):
                for c in range(NCH):
                    inst = e.matmul(pwv.ap[:, qs, :], lhsT=xb[c].ap[:, qs * 128:(qs + 1) * 128], rhs=w_iw.ap[:, c, :], start=(c == 0), stop=(c == NCH - 1))
            return inst
        op(C, "pe", fn_wi, reads=[w_iw] + xb, writes=[pw_])
        ts(C, "dve", wi4, pwv, float(512.0 ** -0.5), None, ALU.mult)
        if STOP <= 1:
            continue
        for qs in range(4):
            qt = 4 * t + qs
            Lk = (qt + 1) * 128
            q0 = qs * 128
            nblk = (Lk + TT - 1) // TT
            for blk in range(nblk):
                b0 = blk * TT
                w = min(TT, Lk - b0)
                for hp in range(4):
                    i = pctr["o"] % 2
                    pctr["o"] += 1
                    bk = 2 + 2 * i
                    pA, pB = ps[bk], ps[bk + 1]
                    mm1(C, tv(pA, pA.ap[:, 0:w]), tv(qi[hp], qi[hp].ap[:, q0:q0 + 128]), tv(kiA, kiA_t[:, b0:b0 + w]))
                    mm1(C, tv(pB, pB.ap[:, 0:w]), tv(qi[hp], qi[hp].ap[:, q0:q0 + 128]), tv(kiB, kiB_t[:, b0:b0 + w]))
                    rl = relu_t[i]
                    act(C, tv(rl, rl.ap[:, 0, 0:w]), tv(pA, pA.ap[:, 0:w]), AF.Relu)
                    act(C, tv(rl, rl.ap[:, 1, 0:w]), tv(pB, pB.ap[:, 0:w]), AF.Relu)
                    accv = tv(acc, acc.ap[:, b0:b0 + w])
                    for ab in range(2):
                        h = 2 * hp + ab
                        wcol = wi4.ap[:, qs, h:h + 1]
                        if h == 0:
                            ts(C, "dve", accv, tv(rl, rl.ap[:, ab, 0:w]), wcol, None, ALU.mult, extra=[wi4])
                        else:
                            stt(C, accv, tv(rl, rl.ap[:, ab, 0:w]), wcol, accv, ALU.mult, ALU.add, extra=[wi4])
            accL = tv(acc, acc.ap[:, 0:Lk])
            selL = tv(sel, sel.ap[:, 0:Lk])
            dg = tv(acc, acc.ap[:, qt * 128:(qt + 1) * 128])
            if qt >= 2:
                op(C, "dve", lambda e, accL=accL: e.tensor_reduce(out=sm["lo"].ap, in_=accL.ap, axis=AX.X, op=ALU.min), reads=[accL], writes=[sm["lo"]])
            tt(C, "dve", dg, dg, cmask, ALU.add)
            if qt >= 2:
                op(C, "dve", lambda e, accL=accL: e.tensor_reduce(out=sm["hi"].ap, in_=accL.ap, axis=AX.X, op=ALU.max), reads=[accL], writes=[sm["hi"]])
                tt(C, "dve", sm["rng"], sm["hi"], sm["lo"], ALU.subtract)
                ts(C, "dve", hwt, pw2, sm["rng"].ap[:, 0:1], None, ALU.mult, extra=[sm["rng"]])
                tt(C, "dve", sm["mid"], sm["lo"], tv(hwt, hwt.ap[:, 0:1]), ALU.add)
                Lh = ((Lk // 2 + 127) // 128) * 128
                nB = Lk - Lh
                thr = float(TOPK) - 0.5 * nB
                accA, accB = T(acc.ap[:, 0:Lh], acc.b), T(acc.ap[:, Lh:Lk], acc.b)
                jA, jB = T(sel.ap[:, 0:Lh], selbufA), T(sel.ap[:, Lh:Lk], selbufB)
                stt(C, sm["nmid"], sm["lo"], -1.0, tv(hwt, hwt.ap[:, 0:1]), ALU.mult, ALU.subtract, extra=[hwt])
                for it in range(NIT):
                    if it == 0:
                        op(C, "dve", lambda e: e.tensor_copy(out=sm["ge"].ap, in_=sm["ge"].ap), reads=[sm["ge"]], writes=[sm["ge"], sel, jA, jB])
                    ts(C, "dve", jA, accA, sm["mid"].ap[:, 0:1], None, ALU.is_ge, ALU.add, extra=[sm["mid"]], accum=sm["cnt"])
                    op(C, "act", lambda e, jB=jB, accB=accB: e.activation(out=jB.ap, in_=accB.ap, func=AF.Sign, bias=sm["nmid"].ap[:, 0:1], scale=1.0,
                                                                      accum_out=sm["sg"].ap), reads=[accB, sm["nmid"]], writes=[jB, sm["sg"]])
                    stt(C, sm["cnt"], sm["sg"], 0.5, sm["cnt"], ALU.mult, ALU.add)
                    ts(C, "dve", sm["ge"], sm["cnt"], thr, None, ALU.is_ge)
                    stt(C, sm["lo"], sm["ge"], hwt.ap[:, it:it + 1], sm["lo"], ALU.mult, ALU.add, extra=[hwt])
                    tt(C, "dve", sm["mid"], sm["lo"], tv(hwt, hwt.ap[:, it + 1:it + 2]), ALU.add)
                    stt(C, sm["nmid"], sm["lo"], -1.0, tv(hwt, hwt.ap[:, it + 1:it + 2]), ALU.mult, ALU.subtract, extra=[hwt])
                op(C, "dve", lambda e, selL=selL, accL=accL: e.tensor_scalar(out=selL.ap, in0=accL.ap, scalar1=sm["lo"].ap[:, 0:1], scalar2=None, op0=ALU.is_ge),
                   reads=[accL, sm["lo"], jA, jB], writes=[selL, jA, jB])
            else:
                ts(C, "dve", selL, accL, -1.0e29, None, ALU.is_ge)
            if DBG == qt:
                op(C, "sp", lambda e: e.dma_start(out=dbg0, in_=acc.ap), reads=[acc], dma=True)
                op(C, "sp", lambda e: e.dma_start(out=dbg1, in_=sel.ap), reads=[sel], dma=True)
                op(C, "sp", lambda e: e.dma_start(allow_slow_non_contiguous=True, out=dbg2[:, 0:1], in_=sm["lo"].ap), reads=[sm["lo"]], dma=True)
                op(C, "sp", lambda e: e.dma_start(allow_slow_non_contiguous=True, out=dbg2[:, 1:2], in_=sm["hi"].ap), reads=[sm["hi"]], dma=True)
                op(C, "sp", lambda e: e.dma_start(allow_slow_non_contiguous=True, out=dbg2[:, 2:3], in_=sm["cnt"].ap), reads=[sm["cnt"]], dma=True)
            half = qs // 2
            qc0 = (qs % 2) * 128
            nkt_tile = 4 * t + 2 * half + 2
            for kb0 in range(0, nkt_tile, 4):
                kbs = list(range(kb0, min(kb0 + 4, nkt_tile)))
                valid = [kb for kb in kbs if kb <= qt]
                pz = nf()
                pzv = T(pz.ap.rearrange("p (g f) -> p g f", g=4), pz.b)
                dstv = T(selT_t[:, kb0:kb0 + len(kbs), qc0:qc0 + 128], selT.b)
                if valid:
                    def fn_st(e, valid=valid, kb0=kb0, pzv=pzv):
                        inst = None
                        for kb in valid:
                            inst = e.matmul(pzv.ap[:, kb - kb0, :], lhsT=sel.ap[:, kb * 128:(kb + 1) * 128], rhs=K["identb"].ap, start=True, stop=True)
                        return inst
                    op(C, "pe", fn_st, reads=[sel, K["identb"], T(sel.ap, selbufA), T(sel.ap, selbufB)], writes=[pz])
                    act(C, T(selT_t[:, kb0:kb0 + len(valid), qc0:qc0 + 128], selT.b), T(pzv.ap[:, 0:len(valid), :], pz.b), AF.Copy)
                if len(valid) < len(kbs):
                    nz = len(kbs) - len(valid)
                    op(C, "pool", lambda e, kb0=kb0, nv=len(valid), nz=nz, qc0=qc0: e.memset(selT_t[:, kb0 + nv:kb0 + nv + nz, qc0:qc0 + 128], 0.0), writes=[selT])
            if STOP <= 2 or qs % 2 == 0:
                continue
            hq0 = half * 256
            nkt = nkt_tile
            items = []
            for h in range(16):
                ngrp = (nkt + 3) // 4
                for gi in range(ngrp):
                    items.append((h, gi, list(range(gi * 4, min(gi * 4 + 4, nkt))), gi == ngrp - 1))

            def prologue(h):
                hp = h // 2
                ql = qlat[h % 2]
                pl = ps[0]
                mm1(C, tv(pl, pl.ap[:, 0:256]), tv(w_uk, w_uk.ap[:, h, :]), tv(qr[hp], qr[hp].ap[:, hq0:hq0 + 256]))
                act(C, tv(ql, ql.ap[:, 0:256]), tv(pl, pl.ap[:, 0:256]), AF.Copy)

            def emitS(item, i):
                h, gi, kts, last = item
                hp, ab = h // 2, h % 2
                ql = qlat[h % 2]
                kr_t = krA_t if ab == 0 else krB_t
                kr_T = krA if ab == 0 else krB
                bk = 2 + 2 * i
                pst = T(C.psum[bk][:, :], C.pbuf[bk])
                pst2 = T(C.psum[bk + 1][:, :], C.pbuf[bk + 1])

                def fn_s(e, kts=kts, bk=bk, ql=ql, kr_t=kr_t, hp=hp, hq0=hq0):
                    inst = None
                    for n_, kt in enumerate(kts):
                        dst = C.psum[bk + n_ // 2][:, (n_ % 2) * 256:(n_ % 2) * 256 + 256]
                        e.matmul(dst, lhsT=ckvT_t[:, kt * 128:(kt + 1) * 128], rhs=ql.ap[:, 0:256], start=True, stop=False)
                        inst = e.matmul(dst, lhsT=kr_t[:, kt * 128:(kt + 1) * 128], rhs=qr_t[:, hp, hq0:hq0 + 256], start=False, stop=True)
                    return inst
                op(C, "pe", fn_s, reads=[ckvT, ql, kr_T, qr[hp]], writes=[pst, pst2])

            def emitEM(item, i):
                h, gi, kts, last = item
                bk = 2 + 2 * i
                pt_ = PT[i]
                n2 = len(kts)
                act(C, T(pt_.ap[:, 0:min(n2, 2), :], pt_.b), T(C.psum[bk][:, 0:min(n2, 2) * 256].rearrange("p (g f) -> p g f", f=256), C.pbuf[bk]), AF.Exp, scale=0.125)
                if n2 > 2:
                    act(C, T(pt_.ap[:, 2:n2, :], pt_.b), T(C.psum[bk + 1][:, 0:(n2 - 2) * 256].rearrange("p (g f) -> p g f", f=256), C.pbuf[bk + 1]), AF.Exp, scale=0.125)
                ptv = T(pt_.ap[:, 0:n2, :], pt_.b)
                tt(C, "dve", ptv, ptv, T(selT_t[:, kts[0]:kts[0] + n2, :], selT.b), ALU.mult)

            def emitPV(item, i):
                h, gi, kts, last = item
                hp, ab = h // 2, h % 2
                pt_ = PT[i]
                hc = (h % 2) * 256
                pov = T(C.psum[6][:, hc:hc + 256], pobuf[h % 2])
                pdv = T(C.psum[7][:, hc:hc + 256], pobuf[2 + h % 2])

                def fn_pv(e, kts=kts, pt_=pt_, pov=pov, pdv=pdv, nkt=nkt):
                    inst = None
                    for n_, kt in enumerate(kts):
                        e.matmul(pov.ap, lhsT=ckvk_t[:, kt, :], rhs=pt_.ap[:, n_, :], start=(kt == 0), stop=(kt == nkt - 1))
                    for n_, kt in enumerate(kts):
                        inst = e.matmul(pdv.ap, lhsT=K["ones1"].ap, rhs=pt_.ap[:, n_, :], start=(kt == 0), stop=(kt == nkt - 1))
                    return inst
                op(C, "pe", fn_pv, reads=[ckvk, pt_, K["ones1"]], writes=[pov, pdv])
                if last:
                    op(C, "dve", lambda e, pdv=pdv: e.reciprocal(out=rden.ap, in_=pdv.ap), reads=[pdv], writes=[rden])
                    ol = olat2[h % 2]
                    tt(C, "dve", ol, pov, rden, ALU.mult)
                    pv_ = ps[1]
                    mm1(C, T(pv_.ap[ab * 64:(ab + 1) * 64, 0:256], pv_.b), tv(w_uv, w_uv.ap[:, h, :]), ol)
                    if ab == 1:
                        act(C, tv(oT[hp], oT[hp].ap[:, hq0:hq0 + 256]), tv(pv_, pv_.ap[:, 0:256]), AF.Copy)

            prologue(items[0][0])
            par0 = pctr["o"] % 2
            emitS(items[0], par0)
            for n_it, item in enumerate(items):
                i = (par0 + n_it) % 2
                if n_it + 1 < len(items):
                    nxt = items[n_it + 1]
                    if nxt[0] != item[0]:
                        prologue(nxt[0])
                    emitS(nxt, 1 - i)
                emitEM(item, i)
                emitPV(item, i)
            pctr["o"] += len(items)
        if DBG == 100 + t:
            op(C, "sp", lambda e: e.dma_start(out=dbg1.rearrange("p (c t) -> p c t", c=NCH), in_=oT_t[:]), reads=oT, dma=True)
        if STOP <= 3:
            continue
        for j in range(NCH):
            py = ps[j % 2]
            wo = wo_s[pctr["w"] % 4]
            pctr["w"] += 1
            op(C, "pool", (lambda wo, j: lambda e: e.dma_start(out=wo.ap, in_=w_o_d[:, :, j * 128:(j + 1) * 128]))(wo, j), writes=[wo], dma=True)
            mm_group(C, py, [(wo.ap[:, hh, :], oT[hh].ap) for hh in range(NCH)], reads=[wo] + oT)
            M.resid(t, j, py)
        M.finish(t)


def dsa_host(inputs, ph):
    d = {}
    g = lambda k: np.ascontiguousarray(inputs[k][0])
    d["dsa_w_dq"] = g("dsa_w_dq")
    d["dsa_w_uq"] = g("dsa_w_uq")
    d["dsa_w_iq"] = g("dsa_w_iq")
    d["dsa_w_dkv"] = g("dsa_w_dkv")
    kr = np.zeros((D_MODEL, 128), np.float32)
    kr[:, 0:16] = inputs["dsa_w_kr"][0]
    kr[:, 64:80] = inputs["dsa_w_kr"][0]
    d["dsa_w_kr_pad"] = kr
    ik = np.zeros((D_MODEL, 128), np.float32)
    ik[:, 0:64] = inputs["dsa_w_ik"][0]
    ik[:, 64:128] = inputs["dsa_w_ik"][0]
    d["dsa_w_ik_pad"] = ik
    d["dsa_w_iw"] = g("dsa_w_iw")
    uk = np.zeros((16, 128, 128), np.float32)
    for h in range(16):
        base = (h % 2) * 64
        uk[h, base + 16:base + 64, :] = inputs["dsa_w_uk"][0][h]
    d["dsa_w_uk_pad"] = uk.reshape(16 * 128, 128)
    d["dsa_w_uv"] = np.ascontiguousarray(inputs["dsa_w_uv"][0].reshape(16 * 128, 64))
    d["dsa_w_o"] = g("dsa_w_o")
    d["dsa_qng"] = np.ascontiguousarray(inputs["dsa_q_norm"][0].reshape(2, 128).T)
    d["dsa_kvg"] = np.ascontiguousarray(inputs["dsa_kv_norm"][0].reshape(128, 1))
    gb = np.zeros((128, 2), np.float32)
    gb[:, 0] = np.tile(inputs["dsa_ik_ln_g"][0], 2)
    gb[:, 1] = np.tile(inputs["dsa_ik_ln_b"][0], 2)
    d["dsa_ikgb"] = gb
    inv, rot, pw = dsa_consts()
    d["dsa_inv"] = inv
    d["dsa_rot"] = rot
    d["dsa_pw2"] = pw[:, :NIT + 1]
    return d


MIXERS["dsa"] = dsa_phase
MIXER_HOST["dsa"] = dsa_host


MIX_OF_LAYER = ("conv", "sconv", "dsa", "gdn")
FUSED = True


def full_plan():
    plan = []
    for i in range(DEPTH):
        plan += [("ffn", i, 0), (MIX_OF_LAYER[i % 4],), ("ffn", i, 1)]
    return plan


def _run(plan, inputs, xT):
    nc = build_program(plan)
    hw = host_inputs_for(plan, inputs)
    has_dsa = any(p[0] == "dsa" for p in plan)
    pos = np.ascontiguousarray(inputs["positions"]).astype(np.int32)
    in_maps = []
    for b in range(BATCH):
        m = dict(hw)
        m["xin"] = xT[b]
        if has_dsa:
            m["dsa_pos"] = np.ascontiguousarray(pos[b:b + 1])
        in_maps.append(m)
    res = run_bass_kernel_spmd(nc, in_maps, core_ids=list(range(BATCH)))
    return [np.asarray(res.results[b]["xout"]) for b in range(BATCH)]


def kernel(**inputs):
    inputs = {k: np.asarray(v) for k, v in inputs.items()}
    x = inputs["x"].astype(np.float32)
    xT = [np.ascontiguousarray(x[b].T) for b in range(BATCH)]
    plan = full_plan()
    if FUSED:
        xT = _run(plan, inputs, xT)
    else:
        for ph in plan:
            xT = _run([ph], inputs, xT)
    return np.ascontiguousarray(np.stack([xT[b].T for b in range(BATCH)]).astype(np.float32))
```
